# Optimizing a Trainium2 kernel written in Bass

```python
import math
import jax, jax.numpy as jnp
from jax import lax
import numpy as np


D_MODEL = 1024
BATCH = 16
SEQ = 2048
DEPTH = 1
DEC_BATCH = 32
DEC_SEQ = 32
PAST_LEN = 2048

CHUNK = 64
D_MIX = 2 * D_MODEL
W_POOL = D_MIX // 2
W_MLSTM = D_MIX - W_POOL
POOL_WINDOWS = (2, 4, 8, 16)
N_POOL_GROUPS = 4
POOL_GW = W_POOL // N_POOL_GROUPS
POOL_BUF = 15
N_HEADS = 4
HEAD_DIM = W_MLSTM // N_HEADS
EPS = 1e-6
IN_SECTIONS = (W_POOL, W_POOL, W_MLSTM, W_MLSTM, W_MLSTM, W_MLSTM, W_MLSTM, N_HEADS, N_HEADS)
D_IN = 2 * W_POOL + 5 * W_MLSTM + 2 * N_HEADS

kernel_name = 'hybrid_pool_mlstm_streaming_step'


def _split_points():
    pts, acc = [], 0
    for s in IN_SECTIONS[:-1]:
        acc += s
        pts.append(acc)
    return pts


def rmsnorm(x, g):
    xf = x.astype(jnp.float32)
    r = lax.rsqrt(jnp.mean(xf * xf, axis=-1, keepdims=True) + EPS)
    return (xf * r).astype(x.dtype) * g


def head_layernorm(h):
    mu = jnp.mean(h, axis=-1, keepdims=True)
    hc = h - mu
    out = hc * lax.rsqrt(jnp.mean(hc * hc, axis=-1, keepdims=True) + EPS)
    B, H, T, Dh = h.shape
    return out.transpose(0, 2, 1, 3).reshape(B, T, H * Dh)


def pool_mixer(xp, buf, start, w_pool, pool_scale):
    B, T, W = xp.shape
    ext = jnp.concatenate([buf, xp], axis=1).astype(jnp.float32)
    cs = jnp.cumsum(ext, axis=1)
    cs = jnp.concatenate([jnp.zeros((B, 1, W), jnp.float32), cs], axis=1)
    top = cs[:, POOL_BUF + 1:POOL_BUF + 1 + T]
    pos = jnp.arange(T) + start
    means = []
    for g, w in enumerate(POOL_WINDOWS):
        sl = slice(g * POOL_GW, (g + 1) * POOL_GW)
        win = top[..., sl] - cs[:, POOL_BUF + 1 - w:POOL_BUF + 1 - w + T, sl]
        cnt = jnp.minimum(pos + 1, w).astype(jnp.float32)
        means.append(win / cnt[None, :, None])
    pooled = jnp.concatenate(means, axis=-1) - xp.astype(jnp.float32)
    pooled = pooled.astype(xp.dtype).reshape(B, T, N_POOL_GROUPS, POOL_GW)
    mixed = jnp.einsum('btgc,gcd->btgd', pooled, w_pool).reshape(B, T, W)
    return mixed * pool_scale


def mlstm_block(carry, blk):
    C0, n0, m0 = carry
    q, k, v, ig, lf = blk
    L = q.shape[2]
    F = jnp.cumsum(lf, axis=-1)
    a = ig - F
    m = F + jnp.maximum(m0[..., None], lax.cummax(a, axis=a.ndim - 1))
    causal = jnp.tril(jnp.ones((L, L), dtype=bool))
    logD = F[..., :, None] + a[..., None, :] - m[..., :, None]
    D = jnp.exp(jnp.where(causal, logD, -jnp.inf))
    decay0 = jnp.exp(m0[..., None] + F - m)
    S = jnp.einsum('bhtd,bhsd->bhts', q, k) * D
    num = jnp.einsum('bhts,bhsd->bhtd', S, v) + decay0[..., None] * jnp.einsum('bhtk,bhkv->bhtv', q, C0)
    nq = jnp.sum(S, axis=-1) + decay0 * jnp.einsum('bhtk,bhk->bht', q, n0)
    h = num / jnp.maximum(jnp.abs(nq), jnp.exp(-m))[..., None]
    mL = m[..., -1]
    wL = jnp.exp(a + F[..., -1:] - mL[..., None])
    dL = jnp.exp(m0 + F[..., -1] - mL)
    C = dL[..., None, None] * C0 + jnp.einsum('bhs,bhsk,bhsv->bhkv', wL, k, v)
    n = dL[..., None] * n0 + jnp.einsum('bhs,bhsk->bhk', wL, k)
    return (C, n, mL), h


def mlstm_sequence(q, k, v, ig, lf, C0, n0, m0):
    B, H, T, Dh = q.shape
    L = min(T, CHUNK)
    NB = T // L

    def blocks(t):
        return jnp.moveaxis(t.reshape(t.shape[:2] + (NB, L) + t.shape[3:]), 2, 0)

    carry0 = (C0.astype(jnp.float32), n0.astype(jnp.float32), m0.astype(jnp.float32))
    carry, hs = lax.scan(mlstm_block, carry0, (blocks(q), blocks(k), blocks(v), blocks(ig), blocks(lf)))
    h = jnp.moveaxis(hs, 0, 2).reshape(B, H, T, Dh)
    return h, carry


def mixer_layer(x, c, pool_buf, C0, n0, m0, start, w_ada, b_ada, g_norm, w_in, b_i, b_f,
                w_pool, pool_scale, g_head, w_out):
    B, T, _ = x.shape
    mod = jnp.einsum('bd,de->be', jax.nn.silu(c), w_ada) + b_ada
    shift, scale, gate = jnp.split(mod[:, None, :], 3, axis=-1)
    h = rmsnorm(x, g_norm) * (1 + scale) + shift
    u = jnp.einsum('btd,de->bte', h, w_in)
    xp, zp, q, k, v, o, zm, ig, fg = jnp.split(u, _split_points(), axis=-1)
    y_pool = pool_mixer(xp, pool_buf, start, w_pool, pool_scale) * jax.nn.silu(zp)
    new_buf = jnp.concatenate([pool_buf, xp], axis=1)[:, -POOL_BUF:]
    def heads(t):
        return t.reshape(B, T, N_HEADS, HEAD_DIM).transpose(0, 2, 1, 3).astype(jnp.float32)
    qh = heads(q)
    kh = heads(k) * (HEAD_DIM ** -0.5)
    vh = heads(v)
    igh = (ig + b_i).astype(jnp.float32).transpose(0, 2, 1)
    lfh = jax.nn.log_sigmoid((fg + b_f).astype(jnp.float32)).transpose(0, 2, 1)
    hm, (C1, n1, m1) = mlstm_sequence(qh, kh, vh, igh, lfh, C0, n0, m0)
    hm = head_layernorm(hm).astype(x.dtype) * g_head
    y_m = hm * jax.nn.sigmoid(o) * jax.nn.silu(zm)
    y = jnp.einsum('bte,ed->btd', jnp.concatenate([y_pool, y_m], axis=-1), w_out)
    x = x + gate * y
    return x, new_buf, C1.astype(x.dtype), n1.astype(x.dtype), m1.astype(x.dtype)


def setup_inputs(seed: int = 0) -> dict:
    key = jax.random.key(seed)
    ks = jax.random.split(key, 24)
    nrm = jax.random.normal
    f32 = jnp.float32
    return {
        'x_prompt': nrm(ks[0], (BATCH, SEQ, D_MODEL), f32),
        'x_sample': nrm(ks[1], (DEC_BATCH, DEC_SEQ, D_MODEL), f32),
        'c_prompt': nrm(ks[2], (BATCH, D_MODEL), f32),
        'c_sample': nrm(ks[3], (DEC_BATCH, D_MODEL), f32),
        'state_pool': nrm(ks[4], (DEPTH, DEC_BATCH, POOL_BUF, W_POOL), f32),
        'state_C': 0.1 * nrm(ks[5], (DEPTH, DEC_BATCH, N_HEADS, HEAD_DIM, HEAD_DIM), f32),
        'state_n': 0.1 * nrm(ks[6], (DEPTH, DEC_BATCH, N_HEADS, HEAD_DIM), f32),
        'state_m': nrm(ks[7], (DEPTH, DEC_BATCH, N_HEADS), f32),
        'w_ada': 0.5 * nrm(ks[8], (DEPTH, D_MODEL, 3 * D_MODEL), f32) * D_MODEL ** -0.5,
        'b_ada': 0.01 * nrm(ks[9], (DEPTH, 3 * D_MODEL), f32),
        'g_norm': 1.0 + 0.02 * nrm(ks[10], (DEPTH, D_MODEL), f32),
        'w_in': nrm(ks[11], (DEPTH, D_MODEL, D_IN), f32) * D_MODEL ** -0.5,
        'b_i': 0.1 * nrm(ks[12], (DEPTH, N_HEADS), f32),
        'b_f': jnp.linspace(3.0, 6.0, N_HEADS, dtype=f32)[None, :] + 0.1 * nrm(ks[13], (DEPTH, N_HEADS), f32),
        'w_pool': nrm(ks[14], (DEPTH, N_POOL_GROUPS, POOL_GW, POOL_GW), f32) * POOL_GW ** -0.5,
        'pool_scale': 1.0 + 0.02 * nrm(ks[15], (DEPTH, W_POOL), f32),
        'g_head': 1.0 + 0.02 * nrm(ks[16], (DEPTH, W_MLSTM), f32),
        'w_out': nrm(ks[17], (DEPTH, D_MIX, D_MODEL), f32) * D_MIX ** -0.5,
        'g_final': 1.0 + 0.02 * nrm(ks[18], (D_MODEL,), f32),
    }


def reference(x_prompt, x_sample, c_prompt, c_sample, state_pool, state_C, state_n, state_m,
              w_ada, b_ada, g_norm, w_in, b_i, b_f, w_pool, pool_scale, g_head, w_out, g_final):
    xpr, xsm = x_prompt, x_sample
    dt = x_prompt.dtype
    pp, pc, pn, pm, sp, sc, sn, sm = [], [], [], [], [], [], [], []
    for l in range(DEPTH):
        lw = (w_ada[l], b_ada[l], g_norm[l], w_in[l], b_i[l], b_f[l], w_pool[l], pool_scale[l], g_head[l], w_out[l])
        zb = jnp.zeros((BATCH, POOL_BUF, W_POOL), dt)
        zC = jnp.zeros((BATCH, N_HEADS, HEAD_DIM, HEAD_DIM), dt)
        zn = jnp.zeros((BATCH, N_HEADS, HEAD_DIM), dt)
        zm = jnp.zeros((BATCH, N_HEADS), dt)
        xpr, b1, C1, n1, m1 = mixer_layer(xpr, c_prompt, zb, zC, zn, zm, 0, *lw)
        xsm, b2, C2, n2, m2 = mixer_layer(xsm, c_sample, state_pool[l], state_C[l], state_n[l], state_m[l],
                                          PAST_LEN, *lw)
        pp.append(b1); pc.append(C1); pn.append(n1); pm.append(m1)
        sp.append(b2); sc.append(C2); sn.append(n2); sm.append(m2)
    y_prompt = rmsnorm(xpr, g_final)
    y_sample = rmsnorm(xsm, g_final)
    return (y_prompt, y_sample, jnp.stack(pp), jnp.stack(pc), jnp.stack(pn), jnp.stack(pm),
            jnp.stack(sp), jnp.stack(sc), jnp.stack(sn), jnp.stack(sm))
```

```python
import contextlib
import os
import numpy as np
SUB = int(os.environ.get('SUB', '99'))
import concourse.bass as bass
import concourse.mybir as mybir
from concourse.bass_utils import run_bass_kernel_spmd

F32 = mybir.dt.float32
BF16 = mybir.dt.bfloat16
AF = mybir.ActivationFunctionType
ALU = mybir.AluOpType

D = 1024
DIN = 7176
TP = 2048
TS = 32
NTOK = 2 * TP + 4 * TS
TMB = 1024
EPS = 1e-6
NCORES = 8


class Res:
    __slots__ = ("name", "w", "rc", "rd", "excl")

    def __init__(self, name, excl=False):
        self.name = name
        self.excl = excl
        self.w = None
        self.rc = {}
        self.rd = []


class Op:
    __slots__ = ("eng", "fn", "deps", "signal", "count", "dma_sem", "idx")


class FW:
    ENGS = ("pe", "act", "dve", "pool", "sp")

    def __init__(self, nc):
        self.nc = nc
        self.ops = {e: [] for e in self.ENGS}
        self.dma_keys = {}
        self.n = 0

    def op(self, eng, fn, reads=(), writes=(), dma_key=None):
        o = Op()
        o.eng, o.fn, o.signal, o.count, o.dma_sem, o.idx = eng, fn, False, None, None, self.n
        self.n += 1
        deps = []
        writes = list(writes) + [r for r in reads if r.excl]
        reads = [r for r in reads if not r.excl]
        for r in reads:
            if r.w is not None:
                deps.append(r.w)
        for r in writes:
            if r.w is not None:
                pw = r.w
                if not (dma_key is not None and pw.dma_sem is not None and pw.dma_sem[0] == dma_key and pw.eng == eng):
                    deps.append(pw)
            deps.extend(r.rc.values())
            deps.extend(r.rd)
        best = {}
        ded = []
        for d in deps:
            if d.dma_sem is not None:
                ded.append(d)
                continue
            if d.eng == eng and eng in ("pe", "sp"):
                continue
            b = best.get(d.eng)
            if b is None or d.idx > b.idx:
                best[d.eng] = d
        ded.extend(best.values())
        o.deps = ded
        for d in ded:
            d.signal = True
        if dma_key is not None:
            ent = self.dma_keys.setdefault(dma_key, [len(self.dma_keys), 0])
            ent[1] += 16
            o.dma_sem = (dma_key, ent[1])
        for r in reads:
            if dma_key is not None:
                r.rd.append(o)
            else:
                r.rc[eng] = o
        for r in writes:
            r.w = o
            r.rc = {}
            r.rd = []
        self.ops[eng].append(o)
        return o

    def emit(self, final_wait_ops=()):
        nc = self.nc
        for e in self.ENGS:
            c = 0
            for o in self.ops[e]:
                if o.dma_sem is None and o.signal:
                    c += 1
                    o.count = c
        with contextlib.ExitStack() as st:
            esem = {e: st.enter_context(nc.semaphore("s_" + e)) for e in self.ENGS}
            dsem = {k: st.enter_context(nc.semaphore("d_%d" % v[0])) for k, v in self.dma_keys.items()}
            block = st.enter_context(nc.Block())

            def tok(o):
                if o.dma_sem is not None:
                    return dsem[o.dma_sem[0]], o.dma_sem[1]
                return esem[o.eng], o.count

            def run(e, handle):
                seen = {}

                def wait(o):
                    s, v = tok(o)
                    if seen.get(id(s), 0) >= v:
                        return
                    seen[id(s)] = v
                    handle.wait_ge(s, v)

                for o in self.ops[e]:
                    mx = {}
                    for d in o.deps:
                        s_, v_ = tok(d)
                        if v_ > mx.get(id(s_), (None, 0))[1]:
                            mx[id(s_)] = (s_, v_)
                    for s_, v_ in mx.values():
                        if seen.get(id(s_), 0) >= v_:
                            continue
                        seen[id(s_)] = v_
                        handle.wait_ge(s_, v_)
                    ins = o.fn(handle)
                    if o.dma_sem is not None:
                        ins.then_inc(dsem[o.dma_sem[0]], 16)
                    elif o.signal:
                        ins.then_inc(esem[e], 1)
                if e == "sp":
                    mx = {}
                    for o in final_wait_ops:
                        s_, v_ = tok(o)
                        if v_ > mx.get(id(s_), (None, 0))[1]:
                            mx[id(s_)] = (s_, v_)
                    for s_, v_ in mx.values():
                        if seen.get(id(s_), 0) < v_:
                            seen[id(s_)] = v_
                            handle.wait_ge(s_, v_)

            @block.tensor
            def _(h):
                run("pe", h)

            @block.scalar
            def _(h):
                run("act", h)

            @block.vector
            def _(h):
                run("dve", h)

            @block.gpsimd
            def _(h):
                run("pool", h)

            @block.sync
            def _(h):
                run("sp", h)


def MM(out, lhsT, rhs, start=True, stop=True):
    return lambda e: e.matmul(out, lhsT=lhsT, rhs=rhs, start=start, stop=stop)


def TR(out, in_, ident):
    return lambda e: e.transpose(out=out, in_=in_, identity=ident)


def ACT(out, in_, func, **kw):
    return lambda e: e.activation(out=out, in_=in_, func=func, **kw)


def TT(out, in0, in1, op):
    return lambda e: e.tensor_tensor(out=out, in0=in0, in1=in1, op=op)


def TSC(out, in0, s1, s2, op0, op1=None):
    if op1 is None:
        return lambda e: e.tensor_scalar(out=out, in0=in0, scalar1=s1, scalar2=None, op0=op0)
    return lambda e: e.tensor_scalar(out=out, in0=in0, scalar1=s1, scalar2=s2, op0=op0, op1=op1)


def STT(out, in0, scalar, in1, op0, op1):
    return lambda e: e.scalar_tensor_tensor(out=out, in0=in0, scalar=scalar, in1=in1, op0=op0, op1=op1)


def CP(out, in_):
    return lambda e: e.tensor_copy(out=out, in_=in_)


def MSET(ap, v):
    return lambda e: e.memset(ap, v)


def DMA(out, in_, **kw):
    return lambda e: e.dma_start(out=out, in_=in_, **kw)


def SCAN(out, d0, d1, init, op0, op1):
    return lambda e: e.tensor_tensor_scan(out=out, data0=d0, data1=d1, initial=init, op0=op0, op1=op1)


def build(mb_limit=None, dbg=False, stop=99):
    nc = bass.Bass("TRN2", target_bir_lowering=False)

    def din(name, shape):
        return nc.dram_tensor(name, list(shape), F32, kind="ExternalInput").ap()

    def dout(name, shape):
        return nc.dram_tensor(name, list(shape), F32, kind="ExternalOutput").ap()

    x_d = din("x", [NTOK, D])
    c_d = din("c", [6, D])
    sp_d = din("st_pool", [4, 15, D])
    sC_d = din("st_C", [4, 4, 256, 256])
    sn_d = din("st_n", [4, 4, 256])
    sm_d = din("st_mT", [4, 4])
    wada_d = din("w_ada", [D, 3 * D])
    vec_d = din("vecs", [6, D])
    win_d = din("w_in", [D, DIN])
    bi_d = din("b_i", [4, 1])
    bf_d = din("b_f", [4, 1])
    wpool_d = din("w_pool", [4, 256, 256])
    wout_d = din("w_out", [2 * D, D])
    gfin_d = din("g_final", [1, D])
    ident_d = din("ident", [128, 128])
    mask_d = din("maskT", [128, 128])
    invc_d = din("invcnt", [128, 16])
    y_o = dout("y", [NTOK, D])
    pool_o = dout("pool_o", [6, 15, D])
    C_o = dout("C_o", [6, 4, 256, 256])
    n_o = dout("n_o", [6, 4, 256])
    m_o = dout("m_o", [6, 4])
    dbg_o = {}
    scrP = nc.dram_tensor("scrP", [6, 128, 8 * 512], BF16).ap()
    scrH = nc.dram_tensor("scrH", [5, 128, 8 * 1280], BF16).ap()

    st = contextlib.ExitStack()
    with st:
        st.enter_context(nc.allow_low_precision("bf16 matmul operands, fp32 accumulation"))
        fw = FW(nc)

        def sb(name, shape, dt=F32):
            return st.enter_context(nc.sbuf_tensor("sb_" + name, list(shape), dt))

        def R(name):
            return Res(name)

        identf = sb("identf", [128, 128]); r_identf = R("identf")
        identb = sb("identb", [128, 128], BF16); r_identb = R("identb")
        maskT = sb("maskT", [128, 128]); r_mask = R("mask")
        onesf = sb("onesf", [128, 128]); r_ones = R("ones")
        smask = sb("smask", [128, 4, 128]); r_smask = R("smask")
        mhalf = sb("mhalf", [128, 1]); r_mhalf = R("mhalf")
        vecT = sb("vecT", [128, 8, 6]); r_vecT = R("vecT")
        gfin = sb("gfin", [128, D]); r_gfin = R("gfin")
        invc = sb("invc", [128, 16]); r_invc = R("invc")
        sm4 = sb("sm4", [128, 64]); r_sm4 = R("sm4")
        r_bias = R("bias"); r_carry = R("carry"); r_m0T = R("m0T"); r_Rp = R("Rp"); r_dLr = R("dLr"); r_mend = R("mend")
        Xd = sb("Xd", [128, 4, 8]); r_Xd = R("Xd")
        modT = sb("modT", [128, 24, 6]); r_modT = R("modT")
        Amod = sb("Amod", [128, 8, 6]); r_A = R("A")
        sTbf = sb("sTbf", [128, 8, 6], BF16); r_sT = R("sT")
        NT = 512
        hT = sb("hT", [128, 8, TMB], BF16); r_hT = [R("hT%d" % i) for i in range(8)]
        ycatT = sb("ycatT", [128, 16, TMB], BF16); r_yc = [R("yc%d" % i) for i in range(8)]
        hslot = [sb("hslot%d" % s, [128, 8, 1280], BF16) for s in range(2)]
        r_hs = [[R("hs%d_%d" % (s, b)) for b in range(5)] for s in range(2)]
        pslot = [sb("pslot%d" % s, [128, 8, 512], BF16) for s in range(2)]
        r_ps = [[R("psl%d_%d" % (s, b)) for b in range(2)] for s in range(2)]
        wpool = sb("wpool", [128, 4, 2, 256], BF16); r_wpool = R("wpool")
        wg = sb("wg", [128, 8, 8], BF16); r_wg = R("wg")
        Call = sb("Call", [128, 4, 2, 257]); r_C = [R("C%d" % i) for i in range(4)]
        Cbf = sb("Cbf", [128, 2, 257], BF16); r_Cbf = R("Cbf")
        wLT = sb("wLT", [128, 32]); wLT16 = sb("wLT16", [128, 32]); fl2T = sb("fl2T", [128, 32]); r_tms = R("tms")
        dLbc = sb("dLbc", [128, 4, 8]); r_dLbc = R("dLbc")
        qT = [sb("qT%d" % p, [128, 2, NT], BF16) for p in range(2)]; r_qT = [R("qT%d" % p) for p in range(2)]
        kT = [sb("kT%d" % p, [128, 2, NT], BF16) for p in range(2)]; r_kT = [R("kT%d" % p) for p in range(2)]
        so = [sb("so%d" % p, [128, 2, NT], BF16) for p in range(2)]; r_so = [R("so%d" % p) for p in range(2)]
        gm = [sb("gm%d" % p, [128, 2, NT], BF16) for p in range(2)]; r_gm = [R("gm%d" % p) for p in range(2)]
        xpT = [sb("xpT%d" % p, [128, 2, 15 + NT]) for p in range(2)]; r_xpT = [R("xpT%d" % p) for p in range(2)]
        szp = [sb("szp%d" % p, [128, 2, NT], BF16) for p in range(2)]; r_szp = [R("szp%d" % p) for p in range(2)]
        pooledT = [sb("pooledT%d" % p, [128, 2, NT], BF16) for p in range(2)]; r_pooled = [R("pooled%d" % p) for p in range(2)]
        yv = [sb("yv%d" % p, [128, D]) for p in range(2)]; r_yv = [R("yv%d" % p) for p in range(2)]
        yo = [sb("yo%d" % p, [128, D]) for p in range(2)]; r_yo = [R("yo%d" % p) for p in range(2)]
        tA = yv[0]; r_tA = r_yv[0]
        tB = yv[1]; r_tB = r_yv[1]
        halo = sb("halo", [128, 8, 15]); r_halo = R("halo")
        halos = sb("halos", [128, 8, 4, 15]); r_halos = R("halos")
        pstage = [sb("pstage%d" % p, [128, 256]) for p in range(2)]; r_pstage = [R("pstage%d" % p) for p in range(2)]
        kw = [sb("kw%d" % s, [128, 256], BF16) for s in range(2)]; r_kw = [R("kw%d" % s) for s in range(2)]
        vaug = [sb("vaug%d" % s, [128, 257], BF16) for s in range(2)]; r_va = [R("va%d" % s) for s in range(2)]
        PT = [sb("PT%d" % s, [128, 128], BF16) for s in range(2)]; r_PT = [R("PT%d" % s) for s in range(2)]
        hn = [sb("hn%d" % s, [128, 256], BF16) for s in range(2)]; r_hn = [R("hn%d" % s) for s in range(2)]
        stt = [sb("stt%d" % s, [128, 16]) for s in range(2)]; r_stt = [R("stt%d" % s) for s in range(2)]
        xs = [xpT[p][:].rearrange("p a b -> p (a b)") for p in range(2)]; r_xs = r_xpT
        ss = [sb("ss%d" % s, [128, 4]) for s in range(2)]; r_ss = [R("ss%d" % s) for s in range(2)]
        xsn = [sb("xsn%d" % s, [128, D]) for s in range(2)]; r_xsn = [R("xsn%d" % s) for s in range(2)]
        xhatn = [sb("xhatn%d" % p, [128, D], BF16) for p in range(2)]; r_xhatn = [R("xhatn%d" % p) for p in range(2)]
        ssn = [sb("ssn%d" % s, [128, 4]) for s in range(2)]; r_ssn = [R("ssn%d" % s) for s in range(2)]
        gate_bc = sb("gate_bc", [128, D]); r_gbc = R("gbc")
        dg = [sb("dg%d" % s, [128, 128]) for s in range(2)]; r_dg = [R("dg%d" % s) for s in range(2)]
        banks = [st.enter_context(nc.psum_tensor("ps%d" % i, [128, 512], F32)) for i in range(8)]
        r_bk = [Res("bank%d" % i, excl=True) for i in range(8)]
        big_i = [0]

        def nb():
            i = big_i[0] % 4
            big_i[0] += 1
            return banks[i], r_bk[i]

        outs = []

        fw.op("sp", DMA(identf[:], ident_d), writes=[r_identf], dma_key="c_ident")
        fw.op("sp", DMA(maskT[:], mask_d), writes=[r_mask], dma_key="c_mask")
        fw.op("sp", DMA(invc[:], invc_d), writes=[r_invc], dma_key="c_invc")
        fw.op("sp", DMA(gfin[:], gfin_d.to_broadcast([128, D])), writes=[r_gfin], dma_key="c_gfin")
        fw.op("sp", DMA(sm4[0:4, 0:1], bi_d), writes=[r_bias], dma_key="c_bias")
        fw.op("sp", DMA(sm4[0:4, 1:2], bf_d), writes=[r_bias], dma_key="c_bias")
        fw.op("sp", DMA(sm4[0:4, 8:12], sm_d), writes=[r_m0T], dma_key="c_m0")
        fw.op("sp", DMA(xs[0][0:6, 0:D], c_d), writes=[r_xs[0]], dma_key="xs0")
        fw.op("sp", DMA(xs[1][0:6, 0:D], vec_d), writes=[r_xs[1]], dma_key="xs1")
        fw.op("pool", DMA(wg[:], win_d[:, 7168:7176].rearrange("(kc p) n -> p kc n", p=128)), writes=[r_wg], dma_key="c_wg")
        fw.op("pool", DMA(wpool[:].rearrange("p g c d -> p (g c) d"),
                          wpool_d.rearrange("g (c p) d -> p (g c) d", p=128)), writes=[r_wpool], dma_key="c_wpool")
        fw.op("dve", CP(identb[:], identf[:]), reads=[r_identf], writes=[r_identb])
        fw.op("dve", MSET(onesf[:], 1.0), writes=[r_ones])
        fw.op("dve", MSET(smask[:], 0.0), writes=[r_smask])
        for j in range(4):
            fw.op("dve", MSET(smask[:, j, j * 32:(j + 1) * 32], 1.0), writes=[r_smask])
        fw.op("dve", MSET(mhalf[:], -0.5), writes=[r_mhalf])
        fw.op("dve", TSC(sm4[0:4, 2:3], sm4[0:4, 1:2], -1.0, None, ALU.mult), reads=[r_bias], writes=[r_bias])
        for s in range(2):
            fw.op("dve", MSET(vaug[s][:, 256:257], 1.0), writes=[r_va[s]])
        fw.op("act", ACT(xs[0][0:6, 0:D], xs[0][0:6, 0:D], AF.Silu), reads=[r_xs[0]], writes=[r_xs[0]])
        for kc in range(8):
            fw.op("pe", MM(banks[4][:, kc * 6:kc * 6 + 6], xs[0][0:6, kc * 128:(kc + 1) * 128], identf[0:6, 0:6]),
                  reads=[r_xs[0], r_identf], writes=[r_bk[4]])
        fw.op("dve", CP(sTbf[:].rearrange("p a b -> p (a b)"), banks[4][:, 0:48]), reads=[r_bk[4]], writes=[r_sT])
        for kc in range(8):
            fw.op("pe", MM(banks[5][:, kc * 6:kc * 6 + 6], xs[1][0:6, kc * 128:(kc + 1) * 128], identf[0:6, 0:6]),
                  reads=[r_xs[1], r_identf], writes=[r_bk[5]])
        fw.op("dve", CP(vecT[:].rearrange("p a b -> p (a b)"), banks[5][:, 0:48]), reads=[r_bk[5]], writes=[r_vecT])
        for j in range(3):
            s = j % 2
            for b in range(4):
                fw.op("pool", DMA(hslot[s][:, :, b * 256:(b + 1) * 256],
                                  wada_d[:, j * 1024 + b * 256:j * 1024 + (b + 1) * 256].rearrange("(kc p) n -> p kc n", p=128)),
                      writes=[r_hs[s][b]], dma_key="hs%d_%d" % (s, b))
            bk, rb = nb()
            for fc in range(8):
                for kc in range(8):
                    fw.op("pe", MM(bk[:, fc * 6:fc * 6 + 6], hslot[s][:, kc, fc * 128:(fc + 1) * 128], sTbf[:, kc, :],
                                   start=(kc == 0), stop=(kc == 7)),
                          reads=[r_hs[s][fc // 2], r_sT], writes=[rb])
            fw.op("dve", TT(modT[:, j * 8:(j + 1) * 8, :], bk[:, 0:48].rearrange("p (a b) -> p a b", b=6),
                            vecT[:, :, 3 + j:4 + j].to_broadcast([128, 8, 6]), ALU.add),
                  reads=[rb, r_vecT], writes=[r_modT])
        fw.op("dve", TSC(Amod[:], modT[:, 8:16, :], 1.0, None, ALU.add), reads=[r_modT], writes=[r_A])
        fw.op("dve", TT(Amod[:], Amod[:], vecT[:, :, 0:1].to_broadcast([128, 8, 6]), ALU.mult), reads=[r_A, r_vecT], writes=[r_A])
        def halos_phase():
            for j in range(4):
                s = j % 2
                fw.op("sp", DMA(xsn[s][0:15, :], sp_d[j]), writes=[r_xsn[s]], dma_key="xsn%d" % s)
                for fc in range(8):
                    fw.op("pe", MM(banks[6][:, (fc * 4 + j) * 15:(fc * 4 + j) * 15 + 15], xsn[s][0:15, fc * 128:(fc + 1) * 128],
                                   identf[0:15, 0:15]), reads=[r_xsn[s], r_identf], writes=[r_bk[6]])
            fw.op("dve", CP(halos[:].rearrange("p a b c -> p (a b c)"), banks[6][:, 0:480]), reads=[r_bk[6]], writes=[r_halos])

        mbs = []
        for p in range(2):
            for half in range(2):
                mbs.append(dict(tok0=p * TP + half * TMB, ntok=TMB, prompt=True, first=(half == 0), last=(half == 1),
                                segs=[(p, 0, TMB)], L=128))
        mbs.append(dict(tok0=2 * TP, ntok=128, prompt=False, first=True, last=True,
                        segs=[(2 + j, j * 32, 32) for j in range(4)], L=32))
        if mb_limit is not None:
            mbs = [mbs[i] for i in mb_limit]

        PCOLS = [[(0, g * 256), (1, 1024 + g * 256)] for g in range(4)]
        HCOLS = [[(0, 2048 + 256 * h), (1, 5120 + 256 * h), (2, 6144 + 256 * h), (3, 3072 + 256 * h), (4, 4096 + 256 * h)]
                 for h in range(4)]

        r_scrP = [Res("scrP%d" % i) for i in range(6)]
        r_scrH = [Res("scrH%d" % i) for i in range(5)]
        scr_ok = set()
        pflat = [pslot[s_][:].rearrange("p a b -> p (a b)") for s_ in range(2)]
        hflat = [hslot[s_][:].rearrange("p a b -> p (a b)") for s_ in range(2)]

        def load_pool_w(g):
            s = g % 2
            if ("P", g) in scr_ok:
                fw.op("sp", DMA(pflat[s], scrP[g]), reads=[r_scrP[g]], writes=r_ps[s], dma_key="lp%d" % s)
                return
            for (b, c0) in PCOLS[g]:
                fw.op("pool", DMA(pslot[s][:, :, b * 256:(b + 1) * 256],
                                  win_d[:, c0:c0 + 256].rearrange("(kc p) n -> p kc n", p=128)),
                      writes=[r_ps[s][b]], dma_key="psl%d_%d" % (s, b))
            fw.op("sp", DMA(scrP[g], pflat[s]), reads=r_ps[s], writes=[r_scrP[g]], dma_key="sp%d" % g)
            scr_ok.add(("P", g))

        def load_head_w(h):
            s = h % 2
            if ("H", h) in scr_ok:
                fw.op("sp", DMA(hflat[s], scrH[h]), reads=[r_scrH[h]], writes=r_hs[s], dma_key="lh%d" % s)
                return
            for (b, c0) in HCOLS[h]:
                fw.op("pool", DMA(hslot[s][:, :, b * 256:(b + 1) * 256],
                                  win_d[:, c0:c0 + 256].rearrange("(kc p) n -> p kc n", p=128)),
                      writes=[r_hs[s][b]], dma_key="hs%d_%d" % (s, b))
            fw.op("sp", DMA(scrH[h], hflat[s]), reads=r_hs[s], writes=[r_scrH[h]], dma_key="sh%d" % h)
            scr_ok.add(("H", h))

        def load_wout_a():
            for s in range(2):
                if ("WA", s) in scr_ok:
                    fw.op("sp", DMA(pflat[s], scrP[4 + s]), reads=[r_scrP[4 + s]], writes=r_ps[s], dma_key="lp%d" % s)
                    continue
                for b in range(2):
                    fw.op("pool", DMA(pslot[s][:, :, b * 256:(b + 1) * 256],
                                      wout_d[s * 1024:(s + 1) * 1024, b * 256:(b + 1) * 256].rearrange("(kc p) n -> p kc n", p=128)),
                          writes=[r_ps[s][b]], dma_key="psl%d_%d" % (s, b))
                fw.op("sp", DMA(scrP[4 + s], pflat[s]), reads=r_ps[s], writes=[r_scrP[4 + s]], dma_key="sp%d" % (4 + s))
                scr_ok.add(("WA", s))

        def load_wout_b():
            if "WB" in scr_ok:
                fw.op("sp", DMA(hflat[0][:, 0:16 * 512], scrH[4][:, 0:16 * 512]), reads=[r_scrH[4]], writes=r_hs[0], dma_key="lh0")
                return
            wo1 = hflat[0][:, 0:16 * 512].rearrange("p (e c) -> p e c", c=512)
            fw.op("pool", DMA(wo1, wout_d[:, 512:1024].rearrange("(kc p) n -> p kc n", p=128)),
                  writes=r_hs[0], dma_key="hs0_0")
            fw.op("sp", DMA(scrH[4][:, 0:16 * 512], hflat[0][:, 0:16 * 512]), reads=r_hs[0], writes=[r_scrH[4]], dma_key="sh4")
            scr_ok.add("WB")

        def rstd_ops(ssl, r_ssl):
            fw.op("dve", TSC(ssl[:, 1:2], ssl[:, 0:1], 1.0 / D, EPS, ALU.mult, ALU.add), reads=[r_ssl], writes=[r_ssl])
            fw.op("pool", TT(ssl[:, 2:3], ssl[:, 1:2], mhalf[:, 0:1], ALU.pow), reads=[r_ssl, r_mhalf], writes=[r_ssl])

        def htiles(c0, n):
            return [r_hT[i] for i in range(c0 // 128, (c0 + n + 127) // 128)]

        def norm_phase(mb, solo=False):
            nt = mb["ntok"] // 128
            fw.op("sp", DMA(xsn[0][:], x_d[mb["tok0"]:mb["tok0"] + 128, :]), writes=[r_xsn[0]], dma_key="xsn0")
            for i in range(nt):
                s = i % 2
                if i + 1 < nt:
                    s2 = (i + 1) % 2
                    fw.op("sp", DMA(xsn[s2][:], x_d[mb["tok0"] + (i + 1) * 128:mb["tok0"] + (i + 2) * 128, :]),
                          writes=[r_xsn[s2]], dma_key="xsn%d" % s2)
                if solo:
                    fw.op("dve", lambda e, s=s: e.scalar_tensor_tensor(out=xhatn[s][:], in0=xsn[s][:], scalar=1.0, in1=xsn[s][:],
                                                                       op0=ALU.mult, op1=ALU.mult, accum_out=ssn[s][:, 0:1]),
                          reads=[r_xsn[s]], writes=[r_xhatn[s], r_ssn[s]])
                else:
                    fw.op("act", ACT(xhatn[s][:], xsn[s][:], AF.Square, accum_out=ssn[s][:, 0:1]), reads=[r_xsn[s]], writes=[r_xhatn[s], r_ssn[s]])
                fw.op("pool", TSC(ssn[s][:, 1:2], ssn[s][:, 0:1], 1.0 / D, EPS, ALU.mult, ALU.add), reads=[r_ssn[s]], writes=[r_ssn[s]])
                fw.op("pool", TT(ssn[s][:, 2:3], ssn[s][:, 1:2], mhalf[:, 0:1], ALU.pow), reads=[r_ssn[s], r_mhalf], writes=[r_ssn[s]])
                if solo:
                    fw.op("dve", TSC(xhatn[s][:], xsn[s][:], ssn[s][:, 2:3], None, ALU.mult), reads=[r_xsn[s], r_ssn[s]], writes=[r_xhatn[s]])
                else:
                    fw.op("act", ACT(xhatn[s][:], xsn[s][:], AF.Identity, scale=ssn[s][:, 2:3]), reads=[r_xsn[s], r_ssn[s]], writes=[r_xhatn[s]])
                bk, rb = nb()
                psb = bk[:].bitcast(BF16)
                for fc in range(8):
                    fw.op("pe", TR(psb[:, fc * 128:(fc + 1) * 128], xhatn[s][:, fc * 128:(fc + 1) * 128], identb[:]),
                          reads=[r_xhatn[s], r_identb], writes=[rb])
                for fc in range(8):
                    for (seq, c0, T) in mb["segs"]:
                        lo = max(c0, i * 128)
                        hi = min(c0 + T, (i + 1) * 128)
                        if hi <= lo:
                            continue
                        src = psb[:, fc * 128 + lo - i * 128:fc * 128 + hi - i * 128]
                        dst = hT[:, fc, lo:hi]
                        fw.op("act", ACT(dst, src, AF.Identity, scale=Amod[:, fc, seq:seq + 1], bias=modT[:, fc, seq:seq + 1]),
                              reads=[rb, r_A, r_modT], writes=[r_hT[i]])
                yield

        def gate_phase(mb):
            n = mb["ntok"]
            L = mb["L"]
            nch = n // L
            rb_ = yv[0][0:4, 0:n]; r_rb = r_yv[0]
            rsp = yo[0][0:4, 0:n]; r_rsp = r_yo[0]
            rGn = gate_bc[0:4, 0:n]; r_rGn = r_gbc
            rN = xs[0][0:4, 0:n]; r_rN = r_xs[0]
            N = min(512, n)
            for tg in range(n // N):
                cs = slice(tg * N, (tg + 1) * N)
                hr = htiles(tg * N, N)
                bk, rb = nb()
                for kc in range(8):
                    fw.op("pe", MM(bk[0:4, 0:N], wg[:, kc, 0:4], hT[:, kc, cs], start=(kc == 0), stop=(kc == 7)),
                          reads=[r_wg] + hr, writes=[rb])
                fw.op("act", ACT(rb_[:, cs], bk[0:4, 0:N], AF.Identity, bias=sm4[0:4, 0:1]), reads=[rb, r_bias], writes=[r_rb])
                bk, rb = nb()
                for kc in range(8):
                    fw.op("pe", MM(bk[0:4, 0:N], wg[:, kc, 4:8], hT[:, kc, cs], start=(kc == 0), stop=(kc == 7)),
                          reads=[r_wg] + hr, writes=[rb])
                fw.op("act", ACT(rsp[:, cs], bk[0:4, 0:N], AF.Exp, scale=-1.0, bias=sm4[0:4, 2:3]), reads=[rb, r_bias], writes=[r_rsp])
            fw.op("act", ACT(rsp, rsp, AF.Ln, bias=1.0), reads=[r_rsp], writes=[r_rsp])
            for (seq, c0, T) in mb["segs"]:
                init = 0.0 if mb["first"] else sm4[0:4, 3:4]
                fw.op("dve", SCAN(rGn[:, c0:c0 + T], rsp[:, c0:c0 + T], rsp[:, c0:c0 + T], init, ALU.add, ALU.max),
                      reads=[r_rsp, r_carry], writes=[r_rGn])
            fw.op("dve", TT(rb_, rb_, rGn, ALU.add), reads=[r_rb, r_rGn], writes=[r_rb])
            for si, (seq, c0, T) in enumerate(mb["segs"]):
                if mb["prompt"]:
                    init = 0.0 if mb["first"] else sm4[0:4, 4:5]
                else:
                    init = sm4[0:4, 8 + si:9 + si]
                fw.op("dve", SCAN(rN[:, c0:c0 + T], rb_[:, c0:c0 + T], rb_[:, c0:c0 + T], init, ALU.max, ALU.max),
                      reads=[r_rb, r_carry, r_m0T], writes=[r_rN])
            if mb["last"]:
                for (seq, c0, T) in mb["segs"]:
                    fw.op("dve", TT(sm4[0:4, 5:6], rN[:, c0 + T - 1:c0 + T], rGn[:, c0 + T - 1:c0 + T], ALU.subtract),
                          reads=[r_rGn, r_rN], writes=[r_mend])
                    outs.append(fw.op("sp", DMA(m_o[seq:seq + 1, :].rearrange("o h -> h o"), sm4[0:4, 5:6], allow_slow_non_contiguous=True),
                                      reads=[r_mend], dma_key="o_m"))
            v = lambda a: a.rearrange("p (c l) -> p c l", l=L)
            Rv = v(rN)[:, :, L - 1]
            Rp = sm4[0:4, 16:16 + nch]
            dLr = sm4[0:4, 24:24 + nch]
            if mb["prompt"]:
                if mb["first"]:
                    fw.op("dve", MSET(Rp[:, 0:1], 0.0), writes=[r_Rp])
                else:
                    fw.op("dve", CP(Rp[:, 0:1], sm4[0:4, 4:5]), reads=[r_carry], writes=[r_Rp])
                fw.op("dve", CP(Rp[:, 1:nch], Rv[:, 0:nch - 1]), reads=[r_rN], writes=[r_Rp])
            else:
                fw.op("dve", CP(Rp, sm4[0:4, 8:12]), reads=[r_m0T], writes=[r_Rp])
            if not mb["last"]:
                fw.op("dve", CP(sm4[0:4, 3:4], rGn[:, n - 1:n]), reads=[r_rGn], writes=[r_carry])
                fw.op("dve", CP(sm4[0:4, 4:5], rN[:, n - 1:n]), reads=[r_rN, r_Rp], writes=[r_carry])
            Rbc = v(rN)[:, :, L - 1:L].to_broadcast([4, nch, L])
            rwL = rsp
            rfl = rGn
            fw.op("dve", TT(v(rwL), v(rb_), Rbc, ALU.subtract), reads=[r_rb, r_rN], writes=[r_rsp])
            fw.op("dve", TT(v(rfl), v(rGn), Rbc, ALU.subtract), reads=[r_rGn, r_rN, r_carry, r_mend], writes=[r_rGn])
            fw.op("act", ACT(rwL, rwL, AF.Exp), reads=[r_rsp], writes=[r_rsp])
            fw.op("act", ACT(rfl, rfl, AF.Exp, scale=2.0), reads=[r_rGn], writes=[r_rGn])
            fw.op("dve", TT(dLr, Rp, Rv, ALU.subtract), reads=[r_Rp, r_rN], writes=[r_dLr])
            fw.op("act", ACT(dLr, dLr, AF.Exp), reads=[r_dLr], writes=[r_dLr])
            fw.op("dve", TT(Xd[0:4, :, 0:nch], dLr.unsqueeze(1).to_broadcast([4, 4, nch]),
                            identf[0:4, 0:4].unsqueeze(2).to_broadcast([4, 4, nch]), ALU.mult),
                  reads=[r_dLr, r_identf], writes=[r_Xd])
            for hh in range(4):
                fw.op("pe", MM(banks[4][:, 128 + hh * 8:128 + hh * 8 + nch], onesf[0:4, :], Xd[0:4, hh, 0:nch]),
                      reads=[r_ones, r_Xd], writes=[r_bk[4]])
            ntile = n // L
            for ti in range(ntile):
                fw.op("pe", MM(banks[4][0:L, ti * 4:ti * 4 + 4], rwL[:, ti * L:(ti + 1) * L], identf[0:4, 0:4]),
                      reads=[r_rsp, r_identf], writes=[r_bk[4]])
                fw.op("pe", MM(banks[4][0:L, 64 + ti * 4:64 + ti * 4 + 4], rfl[:, ti * L:(ti + 1) * L], identf[0:4, 0:4]),
                      reads=[r_rGn, r_identf], writes=[r_bk[4]])
            fw.op("dve", CP(dLbc[:].rearrange("p a b -> p (a b)"), banks[4][:, 128:160]), reads=[r_bk[4]], writes=[r_dLbc])
            fw.op("dve", CP(wLT[0:L, 0:ntile * 4], banks[4][0:L, 0:ntile * 4]), reads=[r_bk[4]], writes=[r_tms])
            fw.op("dve", TSC(wLT16[0:L, 0:ntile * 4], banks[4][0:L, 0:ntile * 4], 0.0625, None, ALU.mult), reads=[r_bk[4]], writes=[r_tms])
            fw.op("dve", CP(fl2T[0:L, 0:ntile * 4], banks[4][0:L, 64:64 + ntile * 4]), reads=[r_bk[4]], writes=[r_tms])

        def pgeom(mb):
            n = mb["ntok"]
            N = min(NT, n)
            if mb["prompt"]:
                return N, 1, N
            return N, 4, 32

        def pool_inproj(mb, g, tg, ntg):
            par = tg % 2
            s = g % 2
            W = pslot[s]
            N, nseg, T = pgeom(mb)
            E = [xpT[par][:, f, 0:nseg * (15 + T)].rearrange("p (s t) -> p s t", s=nseg) for f in range(2)]
            Eo = [xpT[1 - par][:, f, 0:nseg * (15 + T)].rearrange("p (s t) -> p s t", s=nseg) for f in range(2)]
            cs = slice(tg * N, (tg + 1) * N)
            hr = htiles(tg * N, N)
            for f in range(2):
                if mb["prompt"]:
                    if tg == 0:
                        if mb["first"]:
                            fw.op("dve", MSET(E[f][:, :, 0:15], 0.0), writes=[r_xpT[par]])
                        else:
                            fw.op("dve", CP(E[f][:, 0, 0:15], halo[:, 2 * g + f, :]), reads=[r_halo], writes=[r_xpT[par]])
                    else:
                        fw.op("dve", CP(E[f][:, 0, 0:15], Eo[f][:, 0, T:T + 15]), reads=[r_xpT[1 - par]], writes=[r_xpT[par]])
                else:
                    fw.op("dve", CP(E[f][:, :, 0:15], halos[:, 2 * g + f, :, :]), reads=[r_halos], writes=[r_xpT[par]])
            for f in range(2):
                bk, rb = nb()
                for kc in range(8):
                    fw.op("pe", MM(bk[:, 0:N], W[:, kc, f * 128:(f + 1) * 128], hT[:, kc, cs], start=(kc == 0), stop=(kc == 7)),
                          reads=[r_ps[s][0]] + hr, writes=[rb])
                fw.op("dve", CP(E[f][:, :, 15:15 + T], bk[:, 0:N].rearrange("p (s t) -> p s t", s=nseg)), reads=[rb], writes=[r_xpT[par]])
                yield
            for f in range(2):
                bk, rb = nb()
                for kc in range(8):
                    fw.op("pe", MM(bk[:, 0:N], W[:, kc, 256 + f * 128:256 + (f + 1) * 128], hT[:, kc, cs], start=(kc == 0), stop=(kc == 7)),
                          reads=[r_ps[s][1]] + hr, writes=[rb])
                fw.op("act", ACT(szp[par][:, f, 0:N], bk[:, 0:N], AF.Silu), reads=[rb], writes=[r_szp[par]])
                yield

        pcount = [0]

        def pool_dep(mb, g, tg, ntg):
            par = tg % 2
            s = g % 2
            W = pslot[s]
            N, nseg, T = pgeom(mb)
            w = 2 ** (g + 1)
            E = [xpT[par][:, f, 0:nseg * (15 + T)].rearrange("p (s t) -> p s t", s=nseg) for f in range(2)]
            tAv = tA[:, 0:nseg * (15 + T)].rearrange("p (s t) -> p s t", s=nseg)
            tBv = tB[:, 0:nseg * (15 + T)].rearrange("p (s t) -> p s t", s=nseg)
            cs = slice(tg * N, (tg + 1) * N)
            Ltot = 15 + T
            flush(pend_hn)
            tCv = yo[0][:, 0:nseg * (15 + T)].rearrange("p (s t) -> p s t", s=nseg)
            tDv = yo[1][:, 0:nseg * (15 + T)].rearrange("p (s t) -> p s t", s=nseg)
            for f in range(2):
                cur, rcur = E[f], r_xpT[par]
                tmps = [(tAv, r_tA), (tBv, r_tB)] if f == 0 else [(tCv, r_yo[0]), (tDv, r_yo[1])]
                weng = "pool" if f == 0 else "dve"
                step = 1
                k = 0
                while step < w:
                    lo = 2 * step - 1
                    nxt, rn = tmps[k % 2]
                    fw.op(weng, TT(nxt[:, :, lo:Ltot], cur[:, :, lo:Ltot], cur[:, :, lo - step:Ltot - step], ALU.add),
                          reads=[rcur], writes=[rn])
                    cur, rcur = nxt, rn
                    step *= 2
                    k += 1
                oth, roth = tmps[k % 2]
                pv = pooledT[par][:, f, 0:N].rearrange("p (s t) -> p s t", s=nseg)
                fw.op("dve", STT(pv, cur[:, :, 15:Ltot], 1.0 / w, E[f][:, :, 15:Ltot], ALU.mult, ALU.subtract),
                      reads=[rcur, r_xpT[par]], writes=[r_pooled[par]])
                if mb["prompt"] and mb["first"] and tg == 0:
                    fw.op("dve", TT(oth[:, 0, 0:w - 1], cur[:, 0, 15:15 + w - 1], invc[:, 0:w - 1], ALU.mult),
                          reads=[rcur, r_invc], writes=[roth])
                    fw.op("dve", TT(pooledT[par][:, f, 0:w - 1], oth[:, 0, 0:w - 1], E[f][:, 0, 15:15 + w - 1], ALU.subtract),
                          reads=[roth, r_xpT[par]], writes=[r_pooled[par]])
                if mb["prompt"] and not mb["last"] and tg == ntg - 1:
                    fw.op("dve", CP(halo[:, 2 * g + f, :], E[f][:, 0, T:T + 15]), reads=[r_xpT[par]], writes=[r_halo])
                yield
            for dcl in range(2):
                if dcl == 0:
                    flush(pend_tr)
                bk, rb = nb()
                for ccl in range(2):
                    fw.op("pe", MM(bk[:, 0:N], wpool[:, g, ccl, dcl * 128:(dcl + 1) * 128], pooledT[par][:, ccl, 0:N],
                                   start=(ccl == 0), stop=(ccl == 1)), reads=[r_wpool, r_pooled[par]], writes=[rb])
                fw.op("dve", STT(ycatT[:, 2 * g + dcl, cs], bk[:, 0:N], vecT[:, 2 * g + dcl, 1:2], szp[par][:, dcl, 0:N], ALU.mult, ALU.mult),
                      reads=[rb, r_vecT, r_szp[par]], writes=[r_yc[i] for i in range(tg * N // 128, ((tg + 1) * N + 127) // 128)])
                yield
            if mb["last"] and tg == ntg - 1:
                for si, (seq, c0, Ts) in enumerate(mb["segs"]):
                    bk, rb = nb()
                    lc = slice(c0 + Ts - 15, c0 + Ts)
                    pp = pcount[0] % 2
                    pcount[0] += 1
                    for kc in range(8):
                        fw.op("pe", MM(bk[0:15, 0:256], hT[:, kc, lc], W[:, kc, 0:256], start=(kc == 0), stop=(kc == 7)),
                              reads=[r_ps[s][0]] + htiles(c0 + Ts - 15, 15), writes=[rb])
                    fw.op("act", ACT(pstage[pp][0:15, :], bk[0:15, 0:256], AF.Identity), reads=[rb], writes=[r_pstage[pp]])
                    outs.append(fw.op("sp", DMA(pool_o[seq, :, g * 256:(g + 1) * 256], pstage[pp][0:15, :]),
                                      reads=[r_pstage[pp]], dma_key="o_pool%d" % pp))
                    yield

        def head_inproj(mb, h, tg, ntg):
            par = tg % 2
            s = h % 2
            W = hslot[s]
            n = mb["ntok"]
            N = min(NT, n)
            cs = slice(tg * N, (tg + 1) * N)
            hr = htiles(tg * N, N)
            if tg == 0:
                if mb["prompt"]:
                    if mb["first"]:
                        fw.op("dve", MSET(Call[:, h, :, :], 0.0), writes=[r_C[h]])
                else:
                    for hl in ([0, 1] if h == 0 else ([h + 1] if h + 1 < 4 else [])):
                        for j in range(4):
                            rc = Cres(mb, hl, j)
                            extra = list(r_yc) if hl == 1 else []
                            for dc in range(2):
                                fw.op("sp", DMA(Cst(mb, hl, j, dc)[:, 0:256], sC_d[j, hl, dc * 128:(dc + 1) * 128, :]),
                                      writes=[rc] + extra, dma_key="ldC%d_%d" % (hl % 2, j))
                                fw.op("sp", DMA(Cst(mb, hl, j, dc)[:, 256:257], sn_d[j, hl, dc * 128:(dc + 1) * 128].rearrange("(p o) -> p o", o=1),
                                                allow_slow_non_contiguous=True),
                                      writes=[rc] + extra, dma_key="ldC%d_%d" % (hl % 2, j))
            specs = [(0, None, qT, r_qT), (3, "k", kT, r_kT), (1, "sig", so, r_so), (2, "silu", gm, r_gm)]
            for (blk, kind, dstT, rdst) in specs:
                for f in range(2):
                    bk, rb = nb()
                    for kc in range(8):
                        fw.op("pe", MM(bk[:, 0:N], W[:, kc, blk * 256 + f * 128:blk * 256 + (f + 1) * 128], hT[:, kc, cs],
                                       start=(kc == 0), stop=(kc == 7)), reads=[r_hs[s][blk]] + hr, writes=[rb])
                    dst = dstT[par][:, f, 0:N]
                    if kind is None:
                        fw.op("dve", CP(dst, bk[:, 0:N]), reads=[rb], writes=[rdst[par]])
                    elif kind == "k":
                        fw.op("act", ACT(dst, bk[:, 0:N], AF.Identity, scale=0.0625), reads=[rb], writes=[rdst[par]])
                    elif kind == "sig":
                        fw.op("act", ACT(dst, bk[:, 0:N], AF.Sigmoid), reads=[rb], writes=[rdst[par]])
                    else:
                        fw.op("act", ACT(dst, bk[:, 0:N], AF.Silu), reads=[rb], writes=[rdst[par]])
                    yield
            fw.op("pool", TT(gm[par][:, :, 0:N], gm[par][:, :, 0:N], so[par][:, :, 0:N], ALU.mult), reads=[r_gm[par], r_so[par]], writes=[r_gm[par]])

        ccount = [0]
        r_C2 = [R("C2_%d" % i) for i in range(4)]

        def Cst(mb, h, cidx, dc):
            if mb["prompt"] or h % 2 == 0:
                return Call[:, cidx, dc, :]
            return ycatT[:, cidx * 2 + dc, 128:128 + 514].bitcast(F32)

        def Cres(mb, h, cidx):
            if mb["prompt"] or h % 2 == 0:
                return r_C[cidx]
            return r_C2[cidx]

        pend_tr = []
        pend_hn = []

        def flush(lst):
            while lst:
                lst.pop(0)()


        def head_dep(mb, h, tg, ntg):
            par = tg % 2
            s = h % 2
            W = hslot[s]
            n = mb["ntok"]
            N = min(NT, n)
            L = mb["L"]
            npt = N // L
            ntile = n // L
            cbase = ccount[0]
            ccount[0] += npt
            M = L

            def kv_ops(j):
                ti = tg * npt + j
                c0 = ti * L
                cc = (cbase + j) % 2
                cols = slice(c0, c0 + M)
                sc16 = wLT16[0:M, ti * 4 + h:ti * 4 + h + 1]
                bk, rb = nb()
                psb = bk[:].bitcast(BF16)
                lcj = slice(j * L, (j + 1) * L)
                sc1 = wLT[0:M, ti * 4 + h:ti * 4 + h + 1]
                for kc in range(8):
                    fw.op("pe", MM(bk[0:M, 0:256], hT[:, kc, cols], W[:, kc, 1024:1280], start=(kc == 0), stop=(kc == 7)),
                          reads=[r_hs[s][4]] + htiles(c0, M), writes=[rb])
                for dc in range(2):
                    fw.op("pe", TR(psb[0:M, 512 + dc * 128:512 + (dc + 1) * 128], kT[par][:, dc, lcj], identb[:]),
                          reads=[r_kT[par], r_identb], writes=[rb])
                fw.op("dve", TSC(kw[cc][0:M, :], psb[0:M, 512:768], sc1, None, ALU.mult), reads=[rb, r_tms], writes=[r_kw[cc]])
                fw.op("act", ACT(vaug[cc][0:M, 0:256], bk[0:M, 0:256], AF.Identity), reads=[rb], writes=[r_va[cc]])

            kv_ops(0)
            for j in range(npt):
                ti = tg * npt + j
                c0 = ti * L
                cidx = h if mb["prompt"] else ti
                cc = (cbase + j) % 2
                cols = slice(c0, c0 + M)
                lc = slice(j * L, (j + 1) * L)
                sc = wLT[0:M, ti * 4 + h:ti * 4 + h + 1]
                fl2 = fl2T[0:M, ti * 4 + h:ti * 4 + h + 1]
                dl = dLbc[:, h, ti:ti + 1]
                for dc in range(2):
                    fw.op("pe", MM(banks[4][0:M, 0:M], kT[par][:, dc, lc], qT[par][:, dc, lc], start=(dc == 0), stop=(dc == 1)),
                          reads=[r_kT[par], r_qT[par]], writes=[r_bk[4]])
                fw.op("dve", STT(PT[cc][0:M, 0:M], banks[4][0:M, 0:M], sc, maskT[0:M, 0:M], ALU.mult, ALU.mult),
                      reads=[r_bk[4], r_tms, r_mask], writes=[r_PT[cc]])
                flush(pend_hn)
                rc = Cres(mb, h, cidx)
                for dc in range(2):
                    fw.op("act", ACT(Cbf[:, dc, :], Cst(mb, h, cidx, dc), AF.Identity, scale=dl), reads=[rc, r_dLbc], writes=[r_Cbf])
                if j + 1 < npt:
                    kv_ops(j + 1)
                fw.op("pe", MM(banks[5][0:M, 0:257], PT[cc][0:M, 0:M], vaug[cc][0:M, :], start=True, stop=False),
                      reads=[r_PT[cc], r_va[cc]], writes=[r_bk[5]])
                for dc in range(2):
                    fw.op("pe", MM(banks[5][0:M, 0:257], qT[par][:, dc, lc], Cbf[:, dc, :], start=False, stop=(dc == 1)),
                          reads=[r_qT[par], r_Cbf], writes=[r_bk[5]])
                for dc in range(2):
                    fw.op("pe", MM(banks[6 + dc][:, 0:257], kw[cc][0:M, dc * 128:(dc + 1) * 128], vaug[cc][0:M, :]),
                          reads=[r_kw[cc], r_va[cc]], writes=[r_bk[6 + dc]])
                flush(pend_tr)
                for dc in range(2):
                    fw.op("dve", STT(Cst(mb, h, cidx, dc), Cst(mb, h, cidx, dc), dl, banks[6 + dc][:, 0:257], ALU.mult, ALU.add),
                          reads=[rc, r_dLbc, r_bk[6 + dc]], writes=[rc])
                t = stt[cc]
                rt = r_stt[cc]
                fw.op("dve", lambda e, t=t, M=M: e.bn_stats(out=t[0:M, 0:6], in_=banks[5][0:M, 0:256]), reads=[r_bk[5]], writes=[rt])
                fw.op("dve", lambda e, t=t, M=M: e.bn_aggr(out=t[0:M, 6:8], in_=t[0:M, 0:6]), reads=[rt], writes=[rt])
                fw.op("dve", CP(t[0:M, 8:9], banks[5][0:M, 256:257]), reads=[r_bk[5]], writes=[rt])
                fw.op("dve", TT(t[0:M, 9:10], t[0:M, 8:9], t[0:M, 8:9], ALU.mult), reads=[rt], writes=[rt])
                fw.op("dve", TT(t[0:M, 10:11], t[0:M, 9:10], fl2, ALU.max), reads=[rt, r_tms], writes=[rt])
                fw.op("dve", STT(t[0:M, 11:12], t[0:M, 10:11], EPS, t[0:M, 7:8], ALU.mult, ALU.add), reads=[rt], writes=[rt])
                fw.op("pool", TT(t[0:M, 12:13], t[0:M, 11:12], mhalf[0:M, 0:1], ALU.pow), reads=[rt, r_mhalf], writes=[rt])
                def emit_hn(cc=cc, M=M, t=t, rt=rt):
                    fw.op("dve", TSC(hn[cc][0:M, :], banks[5][0:M, 0:256], t[0:M, 6:7], t[0:M, 12:13], ALU.subtract, ALU.mult),
                          reads=[r_bk[5], rt], writes=[r_hn[cc]])
                pend_hn.append(emit_hn)
                def emit_tr(cc=cc, M=M, cols=cols, lc=lc, c0=c0, h=h, par=par):
                    bk, rb = nb()
                    psb = bk[:].bitcast(BF16)
                    for dcl in range(2):
                        fw.op("pe", TR(psb[:, dcl * 128:dcl * 128 + M], hn[cc][0:M, dcl * 128:(dcl + 1) * 128], identb[0:M, 0:M]),
                              reads=[r_hn[cc], r_identb], writes=[rb])
                    for dcl in range(2):
                        fw.op("dve", STT(ycatT[:, 8 + 2 * h + dcl, cols], psb[:, dcl * 128:dcl * 128 + M], vecT[:, 2 * h + dcl, 2:3],
                                         gm[par][:, dcl, lc], ALU.mult, ALU.mult),
                              reads=[rb, r_vecT, r_gm[par]], writes=[r_yc[i] for i in range(c0 // 128, (c0 + M + 127) // 128)])
                pend_tr.append(emit_tr)
                if mb["last"] and (not mb["prompt"] or ti == ntile - 1):
                    seq = mb["segs"][0][0] if mb["prompt"] else mb["segs"][ti][0]
                    for dc in range(2):
                        outs.append(fw.op("sp", DMA(C_o[seq, h, dc * 128:(dc + 1) * 128, :], Cst(mb, h, cidx, dc)[:, 0:256]),
                                          reads=[rc], dma_key="o_C%d_%d" % (h % 2 if not mb["prompt"] else 0, cidx)))
                        outs.append(fw.op("sp", DMA(n_o[seq, h, dc * 128:(dc + 1) * 128].rearrange("(p o) -> p o", o=1), Cst(mb, h, cidx, dc)[:, 256:257],
                                                    allow_slow_non_contiguous=True),
                                          reads=[rc], dma_key="o_C%d_%d" % (h % 2 if not mb["prompt"] else 0, cidx)))
                yield

        def gatebc_phase(mb):
            k = 0
            for half in range(2):
                bk, rb = nb()
                for fcl in range(4):
                    fc = half * 4 + fcl
                    for si, (seq, c0, T) in enumerate(mb["segs"]):
                        d = k % 2
                        k += 1
                        fw.op("dve", TSC(dg[d][:], identf[:], modT[:, 16 + fc, seq:seq + 1], None, ALU.mult),
                              reads=[r_identf, r_modT], writes=[r_dg[d]])
                        lhs = onesf[:, :] if mb["prompt"] else smask[:, si, :]
                        fw.op("pe", MM(bk[:, fcl * 128:(fcl + 1) * 128], lhs, dg[d][:], start=(si == 0), stop=(si == len(mb["segs"]) - 1)),
                              reads=[r_ones, r_smask, r_dg[d]], writes=[rb])
                fw.op("dve", CP(gate_bc[:, half * 512:(half + 1) * 512], bk[:, 0:512]), reads=[rb], writes=[r_gbc])

        def final_phase(mb):
            n = mb["ntok"]
            nt = n // 128
            wo1 = hslot[0][:].rearrange("p a b -> p (a b)")[:, 0:16 * 512].rearrange("p (e c) -> p e c", c=512)
            prev_tail = [None]
            fw.op("sp", DMA(xs[0][:, 0:D], x_d[mb["tok0"]:mb["tok0"] + 128, :]), writes=[r_xs[0]], dma_key="xs0")
            for i in range(nt):
                s = i % 2
                if i + 1 < nt:
                    s2 = (i + 1) % 2
                    fw.op("sp", DMA(xs[s2][:, 0:D], x_d[mb["tok0"] + (i + 1) * 128:mb["tok0"] + (i + 2) * 128, :]),
                          writes=[r_xs[s2]], dma_key="xs%d" % s2)
                if prev_tail[0] is not None:
                    prev_tail[0]()
                    prev_tail[0] = None
                for nh in range(2):
                    bk, rb = nb()
                    for ec in range(16):
                        if nh == 0:
                            rhs = pslot[ec // 8][:, ec % 8, :]
                            rr = r_ps[ec // 8]
                        else:
                            rhs = wo1[:, ec, :]
                            rr = r_hs[0]
                        fw.op("pe", MM(bk[:, 0:512], ycatT[:, ec, i * 128:(i + 1) * 128], rhs, start=(ec == 0), stop=(ec == 15)),
                              reads=[r_yc[i]] + rr, writes=[rb])
                    fw.op("dve", TT(yv[s][:, nh * 512:(nh + 1) * 512], bk[:, 0:512], gate_bc[:, nh * 512:(nh + 1) * 512], ALU.mult),
                          reads=[rb, r_gbc], writes=[r_yv[s]])
                fw.op("dve", TT(yv[s][:], yv[s][:], xs[s][:, 0:D], ALU.add), reads=[r_yv[s], r_xs[s]], writes=[r_yv[s]])
                fw.op("dve", lambda e, s=s: e.scalar_tensor_tensor(out=yo[s][:], in0=yv[s][:], scalar=1.0, in1=yv[s][:], op0=ALU.mult, op1=ALU.mult,
                                                                   accum_out=ss[s][:, 0:1]),
                      reads=[r_yv[s]], writes=[r_yo[s], r_ss[s]])

                def tail(s=s, i=i):
                    rstd_ops(ss[s], r_ss[s])
                    fw.op("dve", STT(yo[s][:], yv[s][:], ss[s][:, 2:3], gfin[:], ALU.mult, ALU.mult), reads=[r_yv[s], r_ss[s], r_gfin], writes=[r_yo[s]])
                    outs.append(fw.op("sp", DMA(y_o[mb["tok0"] + i * 128:mb["tok0"] + (i + 1) * 128, :], yo[s][:]), reads=[r_yo[s]], dma_key="o_y%d" % s))
                prev_tail[0] = tail
                yield
            if prev_tail[0] is not None:
                prev_tail[0]()
                prev_tail[0] = None

        def run_all(g):
            if g is not None:
                for _ in g:
                    pass

        def interleave(a, b, ra=1, rb_=1):
            gens = [a, b]
            rates = [ra, rb_]
            alive = [a is not None, b is not None]
            while any(alive):
                for gi in range(2):
                    if not alive[gi]:
                        continue
                    for _ in range(rates[gi]):
                        try:
                            next(gens[gi])
                        except StopIteration:
                            alive[gi] = False
                            break

        INP = {"P": pool_inproj, "H": head_inproj}
        DEP = {"P": pool_dep, "H": head_dep}
        prev_final = None
        for mi, mb in enumerate(mbs):
            ntg = max(1, mb["ntok"] // NT)
            load_head_w(1)
            interleave(prev_final, norm_phase(mb, solo=(prev_final is None)))
            prev_final = None
            load_pool_w(0)
            load_pool_w(1)
            load_head_w(0)
            if stop < 1:
                continue
            gate_phase(mb)
            if stop < 2:
                continue
            steps = []
            for i in range(4):
                for tg in range(ntg):
                    steps.append(("P", i, tg))
                for tg in range(ntg):
                    steps.append(("H", i, tg))
            steps = [st_ for st_ in steps if stop >= 3 + (st_[1] * 2 + (1 if st_[0] == "H" else 0))]
            if steps:
                k0, i0, t0_ = steps[0]
                run_all(INP[k0](mb, i0, t0_, ntg))
            for si_, (kind, i, tg) in enumerate(steps):
                nxt = steps[si_ + 1] if si_ + 1 < len(steps) else None
                gen_dep = DEP[kind](mb, i, tg, ntg)
                gen_in = INP[nxt[0]](mb, nxt[1], nxt[2], ntg) if nxt is not None else None
                interleave(gen_dep, gen_in, 1, 2)
                if kind == "H" and i == 0 and tg == ntg - 1:
                    gatebc_phase(mb)
                if kind == "H" and i == 1 and tg == ntg - 1 and mi == 0:
                    halos_phase()
                if tg == ntg - 1:
                    if kind == "P":
                        if i + 2 < 4:
                            load_pool_w(i + 2)
                        elif i == 3 and stop >= 11:
                            load_wout_a()
                    else:
                        if i + 2 < 4:
                            load_head_w(i + 2)
                        elif i == 2 and stop >= 11:
                            load_wout_b()
            flush(pend_hn)
            flush(pend_tr)
            if stop < 11:
                continue
            prev_final = final_phase(mb)
        run_all(prev_final)

        if dbg:
            def dump(name, ap, shape, res):
                d = dout(name, shape)
                dbg_o[name] = shape
                outs.append(fw.op("pool", DMA(d, ap), reads=res, dma_key="dbg_" + name))
            dump("d_hT", hT[:], [128, 8, TMB], r_hT)
            dump("d_ycatT", ycatT[:], [128, 16, TMB], r_yc)
            dump("d_modT", modT[:], [128, 24, 6], [r_modT])
            dump("d_wLT", wLT[:], [128, 32], [r_tms])
            dump("d_fl2T", fl2T[:], [128, 32], [r_tms])
            dump("d_dLbc", dLbc[:], [128, 4, 8], [r_dLbc])
            dump("d_gbc", gate_bc[:], [128, D], [r_gbc])
            dump("d_pooledT", pooledT[0][:], [128, 2, NT], [r_pooled[0]])
            dump("d_gm", gm[0][:], [128, 2, NT], [r_gm[0]])

        fw.emit(final_wait_ops=outs)
    return nc, dbg_o


_CACHE = {}


def _consts():
    ident = np.eye(128, dtype=np.float32)
    s = np.arange(128)
    maskT = (s[:, None] <= s[None, :]).astype(np.float32)
    invc = np.tile((1.0 / np.arange(1, 17, dtype=np.float32))[None, :], (128, 1)).astype(np.float32)
    return ident, maskT, invc


def make_in_maps(x_prompt, x_sample, c_prompt, c_sample, state_pool, state_C, state_n, state_m,
                 w_ada, b_ada, g_norm, w_in, b_i, b_f, w_pool, pool_scale, g_head, w_out, g_final):
    f = lambda a: np.ascontiguousarray(np.asarray(a, dtype=np.float32))
    ident, maskT, invc = _consts()
    vecs = f(np.stack([np.asarray(g_norm)[0], np.asarray(pool_scale)[0], np.asarray(g_head)[0],
                       np.asarray(b_ada)[0, 0:D], np.asarray(b_ada)[0, D:2 * D], np.asarray(b_ada)[0, 2 * D:3 * D]], axis=0))
    shared = {
        "w_ada": f(np.asarray(w_ada)[0]), "vecs": vecs, "w_in": f(np.asarray(w_in)[0]),
        "b_i": f(np.asarray(b_i)[0].reshape(4, 1)), "b_f": f(np.asarray(b_f)[0].reshape(4, 1)),
        "w_pool": f(np.asarray(w_pool)[0]), "w_out": f(np.asarray(w_out)[0]), "g_final": f(np.asarray(g_final).reshape(1, D)),
        "ident": ident, "maskT": maskT, "invcnt": invc,
    }
    xp = np.asarray(x_prompt); xsm = np.asarray(x_sample)
    in_maps = []
    for c in range(NCORES):
        m = dict(shared)
        m["x"] = f(np.concatenate([xp[2 * c].reshape(TP, D), xp[2 * c + 1].reshape(TP, D), xsm[4 * c:4 * c + 4].reshape(4 * TS, D)], axis=0))
        m["c"] = f(np.concatenate([np.asarray(c_prompt)[2 * c:2 * c + 2], np.asarray(c_sample)[4 * c:4 * c + 4]], axis=0))
        m["st_pool"] = f(np.asarray(state_pool)[0, 4 * c:4 * c + 4])
        m["st_C"] = f(np.asarray(state_C)[0, 4 * c:4 * c + 4])
        m["st_n"] = f(np.asarray(state_n)[0, 4 * c:4 * c + 4])
        m["st_mT"] = f(np.asarray(state_m)[0, 4 * c:4 * c + 4].T)
        in_maps.append(m)
    return in_maps


def kernel(**inputs):
    if "nc" not in _CACHE:
        _CACHE["nc"] = build()[0]
    nc = _CACHE["nc"]
    in_maps = make_in_maps(**inputs)
    res = run_bass_kernel_spmd(nc, in_maps, core_ids=list(range(NCORES)))
    R_ = res.results
    y_prompt = np.zeros((16, TP, D), np.float32)
    y_sample = np.zeros((32, TS, D), np.float32)
    pp = np.zeros((1, 16, 15, D), np.float32); pc = np.zeros((1, 16, 4, 256, 256), np.float32)
    pn = np.zeros((1, 16, 4, 256), np.float32); pm = np.zeros((1, 16, 4), np.float32)
    sp_ = np.zeros((1, 32, 15, D), np.float32); sc = np.zeros((1, 32, 4, 256, 256), np.float32)
    sn = np.zeros((1, 32, 4, 256), np.float32); sm = np.zeros((1, 32, 4), np.float32)
    for c in range(NCORES):
        r = R_[c]
        y = r["y"]
        y_prompt[2 * c] = y[0:TP]
        y_prompt[2 * c + 1] = y[TP:2 * TP]
        y_sample[4 * c:4 * c + 4] = y[2 * TP:].reshape(4, TS, D)
        pp[0, 2 * c:2 * c + 2] = r["pool_o"][0:2]; sp_[0, 4 * c:4 * c + 4] = r["pool_o"][2:6]
        pc[0, 2 * c:2 * c + 2] = r["C_o"][0:2]; sc[0, 4 * c:4 * c + 4] = r["C_o"][2:6]
        pn[0, 2 * c:2 * c + 2] = r["n_o"][0:2]; sn[0, 4 * c:4 * c + 4] = r["n_o"][2:6]
        pm[0, 2 * c:2 * c + 2] = r["m_o"][0:2]; sm[0, 4 * c:4 * c + 4] = r["m_o"][2:6]
    return (y_prompt, y_sample, pp, pc, pn, pm, sp_, sc, sn, sm)
```

```python
import contextlib
import os
import numpy as np
SUB = int(os.environ.get('SUB', '99'))
import concourse.bass as bass
import concourse.mybir as mybir
from concourse.bass_utils import run_bass_kernel_spmd

F32 = mybir.dt.float32
BF16 = mybir.dt.bfloat16
AF = mybir.ActivationFunctionType
ALU = mybir.AluOpType

D = 1024
DIN = 7176
TP = 2048
TS = 32
NTOK = 2 * TP + 4 * TS
TMB = 1024
EPS = 1e-6
NCORES = 8


class Res:
    __slots__ = ("name", "w", "rc", "rd", "excl")

    def __init__(self, name, excl=False):
        self.name = name
        self.excl = excl
        self.w = None
        self.rc = {}
        self.rd = []


class Op:
    __slots__ = ("eng", "fn", "deps", "signal", "count", "dma_sem", "idx")


class FW:
    ENGS = ("pe", "act", "dve", "pool", "sp")

    def __init__(self, nc):
        self.nc = nc
        self.ops = {e: [] for e in self.ENGS}
        self.dma_keys = {}
        self.n = 0

    def op(self, eng, fn, reads=(), writes=(), dma_key=None):
        o = Op()
        o.eng, o.fn, o.signal, o.count, o.dma_sem, o.idx = eng, fn, False, None, None, self.n
        self.n += 1
        deps = []
        writes = list(writes) + [r for r in reads if r.excl]
        reads = [r for r in reads if not r.excl]
        for r in reads:
            if r.w is not None:
                deps.append(r.w)
        for r in writes:
            if r.w is not None:
                pw = r.w
                if not (dma_key is not None and pw.dma_sem is not None and pw.dma_sem[0] == dma_key and pw.eng == eng):
                    deps.append(pw)
            deps.extend(r.rc.values())
            deps.extend(r.rd)
        best = {}
        ded = []
        for d in deps:
            if d.dma_sem is not None:
                ded.append(d)
                continue
            if d.eng == eng and eng in ("pe", "sp"):
                continue
            b = best.get(d.eng)
            if b is None or d.idx > b.idx:
                best[d.eng] = d
        ded.extend(best.values())
        o.deps = ded
        for d in ded:
            d.signal = True
        if dma_key is not None:
            ent = self.dma_keys.setdefault(dma_key, [len(self.dma_keys), 0])
            ent[1] += 16
            o.dma_sem = (dma_key, ent[1])
        for r in reads:
            if dma_key is not None:
                r.rd.append(o)
            else:
                r.rc[eng] = o
        for r in writes:
            r.w = o
            r.rc = {}
            r.rd = []
        self.ops[eng].append(o)
        return o

    def emit(self, final_wait_ops=()):
        nc = self.nc
        for e in self.ENGS:
            c = 0
            for o in self.ops[e]:
                if o.dma_sem is None and o.signal:
                    c += 1
                    o.count = c
        with contextlib.ExitStack() as st:
            esem = {e: st.enter_context(nc.semaphore("s_" + e)) for e in self.ENGS}
            dsem = {k: st.enter_context(nc.semaphore("d_%d" % v[0])) for k, v in self.dma_keys.items()}
            block = st.enter_context(nc.Block())

            def tok(o):
                if o.dma_sem is not None:
                    return dsem[o.dma_sem[0]], o.dma_sem[1]
                return esem[o.eng], o.count

            def run(e, handle):
                seen = {}

                def wait(o):
                    s, v = tok(o)
                    if seen.get(id(s), 0) >= v:
                        return
                    seen[id(s)] = v
                    handle.wait_ge(s, v)

                for o in self.ops[e]:
                    mx = {}
                    for d in o.deps:
                        s_, v_ = tok(d)
                        if v_ > mx.get(id(s_), (None, 0))[1]:
                            mx[id(s_)] = (s_, v_)
                    for s_, v_ in mx.values():
                        if seen.get(id(s_), 0) >= v_:
                            continue
                        seen[id(s_)] = v_
                        handle.wait_ge(s_, v_)
                    ins = o.fn(handle)
                    if o.dma_sem is not None:
                        ins.then_inc(dsem[o.dma_sem[0]], 16)
                    elif o.signal:
                        ins.then_inc(esem[e], 1)
                if e == "sp":
                    mx = {}
                    for o in final_wait_ops:
                        s_, v_ = tok(o)
                        if v_ > mx.get(id(s_), (None, 0))[1]:
                            mx[id(s_)] = (s_, v_)
                    for s_, v_ in mx.values():
                        if seen.get(id(s_), 0) < v_:
                            seen[id(s_)] = v_
                            handle.wait_ge(s_, v_)

            @block.tensor
            def _(h):
                run("pe", h)

            @block.scalar
            def _(h):
                run("act", h)

            @block.vector
            def _(h):
                run("dve", h)

            @block.gpsimd
            def _(h):
                run("pool", h)

            @block.sync
            def _(h):
                run("sp", h)


def MM(out, lhsT, rhs, start=True, stop=True):
    return lambda e: e.matmul(out, lhsT=lhsT, rhs=rhs, start=start, stop=stop)


def TR(out, in_, ident):
    return lambda e: e.transpose(out=out, in_=in_, identity=ident)


def ACT(out, in_, func, **kw):
    return lambda e: e.activation(out=out, in_=in_, func=func, **kw)


def TT(out, in0, in1, op):
    return lambda e: e.tensor_tensor(out=out, in0=in0, in1=in1, op=op)


def TSC(out, in0, s1, s2, op0, op1=None):
    if op1 is None:
        return lambda e: e.tensor_scalar(out=out, in0=in0, scalar1=s1, scalar2=None, op0=op0)
    return lambda e: e.tensor_scalar(out=out, in0=in0, scalar1=s1, scalar2=s2, op0=op0, op1=op1)


def STT(out, in0, scalar, in1, op0, op1):
    return lambda e: e.scalar_tensor_tensor(out=out, in0=in0, scalar=scalar, in1=in1, op0=op0, op1=op1)


def CP(out, in_):
    return lambda e: e.tensor_copy(out=out, in_=in_)


def MSET(ap, v):
    return lambda e: e.memset(ap, v)


def DMA(out, in_, **kw):
    return lambda e: e.dma_start(out=out, in_=in_, **kw)


def SCAN(out, d0, d1, init, op0, op1):
    return lambda e: e.tensor_tensor_scan(out=out, data0=d0, data1=d1, initial=init, op0=op0, op1=op1)


def build(mb_limit=None, dbg=False, stop=99):
    nc = bass.Bass("TRN2", target_bir_lowering=False)

    def din(name, shape):
        return nc.dram_tensor(name, list(shape), F32, kind="ExternalInput").ap()

    def dout(name, shape):
        return nc.dram_tensor(name, list(shape), F32, kind="ExternalOutput").ap()

    x_d = din("x", [NTOK, D])
    c_d = din("c", [6, D])
    sp_d = din("st_pool", [4, 15, D])
    sC_d = din("st_C", [4, 4, 256, 256])
    sn_d = din("st_n", [4, 4, 256])
    sm_d = din("st_mT", [4, 4])
    wada_d = din("w_ada", [D, 3 * D])
    vec_d = din("vecs", [6, D])
    win_d = din("w_in", [D, DIN])
    bi_d = din("b_i", [4, 1])
    bf_d = din("b_f", [4, 1])
    wpool_d = din("w_pool", [4, 256, 256])
    wout_d = din("w_out", [2 * D, D])
    gfin_d = din("g_final", [1, D])
    ident_d = din("ident", [128, 128])
    mask_d = din("maskT", [128, 128])
    invc_d = din("invcnt", [128, 16])
    y_o = dout("y", [NTOK, D])
    pool_o = dout("pool_o", [6, 15, D])
    C_o = dout("C_o", [6, 4, 256, 256])
    n_o = dout("n_o", [6, 4, 256])
    m_o = dout("m_o", [6, 4])
    dbg_o = {}
    scrP = nc.dram_tensor("scrP", [6, 128, 8 * 512], BF16).ap()
    scrH = nc.dram_tensor("scrH", [5, 128, 8 * 1280], BF16).ap()

    st = contextlib.ExitStack()
    with st:
        st.enter_context(nc.allow_low_precision("bf16 matmul operands, fp32 accumulation"))
        fw = FW(nc)

        def sb(name, shape, dt=F32):
            return st.enter_context(nc.sbuf_tensor("sb_" + name, list(shape), dt))

        def R(name):
            return Res(name)

        identf = sb("identf", [128, 128]); r_identf = R("identf")
        identb = sb("identb", [128, 128], BF16); r_identb = R("identb")
        maskT = sb("maskT", [128, 128]); r_mask = R("mask")
        onesf = sb("onesf", [128, 128]); r_ones = R("ones")
        smask = sb("smask", [128, 4, 128]); r_smask = R("smask")
        mhalf = sb("mhalf", [128, 1]); r_mhalf = R("mhalf")
        vecT = sb("vecT", [128, 8, 6]); r_vecT = R("vecT")
        gfin = sb("gfin", [128, D]); r_gfin = R("gfin")
        invc = sb("invc", [128, 16]); r_invc = R("invc")
        sm4 = sb("sm4", [128, 64]); r_sm4 = R("sm4")
        r_bias = R("bias"); r_carry = R("carry"); r_m0T = R("m0T"); r_Rp = R("Rp"); r_dLr = R("dLr"); r_mend = R("mend")
        Xd = sb("Xd", [128, 4, 8]); r_Xd = R("Xd")
        modT = sb("modT", [128, 24, 6]); r_modT = R("modT")
        Amod = sb("Amod", [128, 8, 6]); r_A = R("A")
        sTbf = sb("sTbf", [128, 8, 6], BF16); r_sT = R("sT")
        NT = 512
        hT = sb("hT", [128, 8, TMB], BF16); r_hT = [R("hT%d" % i) for i in range(8)]
        ycatT = sb("ycatT", [128, 16, TMB], BF16); r_yc = [R("yc%d" % i) for i in range(8)]
        hslot = [sb("hslot%d" % s, [128, 8, 1280], BF16) for s in range(2)]
        r_hs = [[R("hs%d_%d" % (s, b)) for b in range(5)] for s in range(2)]
        pslot = [sb("pslot%d" % s, [128, 8, 512], BF16) for s in range(2)]
        r_ps = [[R("psl%d_%d" % (s, b)) for b in range(2)] for s in range(2)]
        wpool = sb("wpool", [128, 4, 2, 256], BF16); r_wpool = R("wpool")
        wg = sb("wg", [128, 8, 8], BF16); r_wg = R("wg")
        Call = sb("Call", [128, 4, 2, 257]); r_C = [R("C%d" % i) for i in range(4)]
        Cbf = sb("Cbf", [128, 2, 257], BF16); r_Cbf = R("Cbf")
        wLT = sb("wLT", [128, 32]); wLT16 = sb("wLT16", [128, 32]); fl2T = sb("fl2T", [128, 32]); r_tms = R("tms")
        dLbc = sb("dLbc", [128, 4, 8]); r_dLbc = R("dLbc")
        qT = [sb("qT%d" % p, [128, 2, NT], BF16) for p in range(2)]; r_qT = [R("qT%d" % p) for p in range(2)]
        kT = [sb("kT%d" % p, [128, 2, NT], BF16) for p in range(2)]; r_kT = [R("kT%d" % p) for p in range(2)]
        so = [sb("so%d" % p, [128, 2, NT], BF16) for p in range(2)]; r_so = [R("so%d" % p) for p in range(2)]
        gm = [sb("gm%d" % p, [128, 2, NT], BF16) for p in range(2)]; r_gm = [R("gm%d" % p) for p in range(2)]
        xpT = [sb("xpT%d" % p, [128, 2, 15 + NT]) for p in range(2)]; r_xpT = [R("xpT%d" % p) for p in range(2)]
        szp = [sb("szp%d" % p, [128, 2, NT], BF16) for p in range(2)]; r_szp = [R("szp%d" % p) for p in range(2)]
        pooledT = [sb("pooledT%d" % p, [128, 2, NT], BF16) for p in range(2)]; r_pooled = [R("pooled%d" % p) for p in range(2)]
        yv = [sb("yv%d" % p, [128, D]) for p in range(2)]; r_yv = [R("yv%d" % p) for p in range(2)]
        yo = [sb("yo%d" % p, [128, D]) for p in range(2)]; r_yo = [R("yo%d" % p) for p in range(2)]
        tA = yv[0]; r_tA = r_yv[0]
        tB = yv[1]; r_tB = r_yv[1]
        halo = sb("halo", [128, 8, 15]); r_halo = R("halo")
        halos = sb("halos", [128, 8, 4, 15]); r_halos = R("halos")
        pstage = [sb("pstage%d" % p, [128, 256]) for p in range(2)]; r_pstage = [R("pstage%d" % p) for p in range(2)]
        kw = [sb("kw%d" % s, [128, 256], BF16) for s in range(2)]; r_kw = [R("kw%d" % s) for s in range(2)]
        vaug = [sb("vaug%d" % s, [128, 257], BF16) for s in range(2)]; r_va = [R("va%d" % s) for s in range(2)]
        PT = [sb("PT%d" % s, [128, 128], BF16) for s in range(2)]; r_PT = [R("PT%d" % s) for s in range(2)]
        hn = [sb("hn%d" % s, [128, 256], BF16) for s in range(2)]; r_hn = [R("hn%d" % s) for s in range(2)]
        stt = [sb("stt%d" % s, [128, 16]) for s in range(2)]; r_stt = [R("stt%d" % s) for s in range(2)]
        xs = [xpT[p][:].rearrange("p a b -> p (a b)") for p in range(2)]; r_xs = r_xpT
        ss = [sb("ss%d" % s, [128, 4]) for s in range(2)]; r_ss = [R("ss%d" % s) for s in range(2)]
        xsn = [sb("xsn%d" % s, [128, D]) for s in range(2)]; r_xsn = [R("xsn%d" % s) for s in range(2)]
        xhatn = [sb("xhatn%d" % p, [128, D], BF16) for p in range(2)]; r_xhatn = [R("xhatn%d" % p) for p in range(2)]
        ssn = [sb("ssn%d" % s, [128, 4]) for s in range(2)]; r_ssn = [R("ssn%d" % s) for s in range(2)]
        gate_bc = sb("gate_bc", [128, D]); r_gbc = R("gbc")
        dg = [sb("dg%d" % s, [128, 128]) for s in range(2)]; r_dg = [R("dg%d" % s) for s in range(2)]
        banks = [st.enter_context(nc.psum_tensor("ps%d" % i, [128, 512], F32)) for i in range(8)]
        r_bk = [Res("bank%d" % i, excl=True) for i in range(8)]
        big_i = [0]

        def nb():
            i = big_i[0] % 4
            big_i[0] += 1
            return banks[i], r_bk[i]

        outs = []

        fw.op("sp", DMA(identf[:], ident_d), writes=[r_identf], dma_key="c_ident")
        fw.op("sp", DMA(maskT[:], mask_d), writes=[r_mask], dma_key="c_mask")
        fw.op("sp", DMA(invc[:], invc_d), writes=[r_invc], dma_key="c_invc")
        fw.op("sp", DMA(gfin[:], gfin_d.to_broadcast([128, D])), writes=[r_gfin], dma_key="c_gfin")
        fw.op("sp", DMA(sm4[0:4, 0:1], bi_d), writes=[r_bias], dma_key="c_bias")
        fw.op("sp", DMA(sm4[0:4, 1:2], bf_d), writes=[r_bias], dma_key="c_bias")
        fw.op("sp", DMA(sm4[0:4, 8:12], sm_d), writes=[r_m0T], dma_key="c_m0")
        fw.op("sp", DMA(xs[0][0:6, 0:D], c_d), writes=[r_xs[0]], dma_key="xs0")
        fw.op("sp", DMA(xs[1][0:6, 0:D], vec_d), writes=[r_xs[1]], dma_key="xs1")
        fw.op("pool", DMA(wg[:], win_d[:, 7168:7176].rearrange("(kc p) n -> p kc n", p=128)), writes=[r_wg], dma_key="c_wg")
        fw.op("pool", DMA(wpool[:].rearrange("p g c d -> p (g c) d"),
                          wpool_d.rearrange("g (c p) d -> p (g c) d", p=128)), writes=[r_wpool], dma_key="c_wpool")
        fw.op("dve", CP(identb[:], identf[:]), reads=[r_identf], writes=[r_identb])
        fw.op("dve", MSET(onesf[:], 1.0), writes=[r_ones])
        fw.op("dve", MSET(smask[:], 0.0), writes=[r_smask])
        for j in range(4):
            fw.op("dve", MSET(smask[:, j, j * 32:(j + 1) * 32], 1.0), writes=[r_smask])
        fw.op("dve", MSET(mhalf[:], -0.5), writes=[r_mhalf])
        fw.op("dve", TSC(sm4[0:4, 2:3], sm4[0:4, 1:2], -1.0, None, ALU.mult), reads=[r_bias], writes=[r_bias])
        for s in range(2):
            fw.op("dve", MSET(vaug[s][:, 256:257], 1.0), writes=[r_va[s]])
        fw.op("act", ACT(xs[0][0:6, 0:D], xs[0][0:6, 0:D], AF.Silu), reads=[r_xs[0]], writes=[r_xs[0]])
        for kc in range(8):
            fw.op("pe", MM(banks[4][:, kc * 6:kc * 6 + 6], xs[0][0:6, kc * 128:(kc + 1) * 128], identf[0:6, 0:6]),
                  reads=[r_xs[0], r_identf], writes=[r_bk[4]])
        fw.op("dve", CP(sTbf[:].rearrange("p a b -> p (a b)"), banks[4][:, 0:48]), reads=[r_bk[4]], writes=[r_sT])
        for kc in range(8):
            fw.op("pe", MM(banks[5][:, kc * 6:kc * 6 + 6], xs[1][0:6, kc * 128:(kc + 1) * 128], identf[0:6, 0:6]),
                  reads=[r_xs[1], r_identf], writes=[r_bk[5]])
        fw.op("dve", CP(vecT[:].rearrange("p a b -> p (a b)"), banks[5][:, 0:48]), reads=[r_bk[5]], writes=[r_vecT])
        for j in range(3):
            s = j % 2
            for b in range(4):
                fw.op("pool", DMA(hslot[s][:, :, b * 256:(b + 1) * 256],
                                  wada_d[:, j * 1024 + b * 256:j * 1024 + (b + 1) * 256].rearrange("(kc p) n -> p kc n", p=128)),
                      writes=[r_hs[s][b]], dma_key="hs%d_%d" % (s, b))
            bk, rb = nb()
            for fc in range(8):
                for kc in range(8):
                    fw.op("pe", MM(bk[:, fc * 6:fc * 6 + 6], hslot[s][:, kc, fc * 128:(fc + 1) * 128], sTbf[:, kc, :],
                                   start=(kc == 0), stop=(kc == 7)),
                          reads=[r_hs[s][fc // 2], r_sT], writes=[rb])
            fw.op("dve", TT(modT[:, j * 8:(j + 1) * 8, :], bk[:, 0:48].rearrange("p (a b) -> p a b", b=6),
                            vecT[:, :, 3 + j:4 + j].to_broadcast([128, 8, 6]), ALU.add),
                  reads=[rb, r_vecT], writes=[r_modT])
        fw.op("dve", TSC(Amod[:], modT[:, 8:16, :], 1.0, None, ALU.add), reads=[r_modT], writes=[r_A])
        fw.op("dve", TT(Amod[:], Amod[:], vecT[:, :, 0:1].to_broadcast([128, 8, 6]), ALU.mult), reads=[r_A, r_vecT], writes=[r_A])
        for j in range(4):
            s = j % 2
            fw.op("sp", DMA(xs[s][0:15, 0:D], sp_d[j]), writes=[r_xs[s]], dma_key="xs%d" % s)
            for fc in range(8):
                fw.op("pe", MM(banks[6][:, (fc * 4 + j) * 15:(fc * 4 + j) * 15 + 15], xs[s][0:15, fc * 128:(fc + 1) * 128],
                               identf[0:15, 0:15]), reads=[r_xs[s], r_identf], writes=[r_bk[6]])
        fw.op("dve", CP(halos[:].rearrange("p a b c -> p (a b c)"), banks[6][:, 0:480]), reads=[r_bk[6]], writes=[r_halos])

        mbs = []
        for p in range(2):
            for half in range(2):
                mbs.append(dict(tok0=p * TP + half * TMB, ntok=TMB, prompt=True, first=(half == 0), last=(half == 1),
                                segs=[(p, 0, TMB)], L=128))
        mbs.append(dict(tok0=2 * TP, ntok=128, prompt=False, first=True, last=True,
                        segs=[(2 + j, j * 32, 32) for j in range(4)], L=32))
        if mb_limit is not None:
            mbs = [mbs[i] for i in mb_limit]

        PCOLS = [[(0, g * 256), (1, 1024 + g * 256)] for g in range(4)]
        HCOLS = [[(0, 2048 + 256 * h), (1, 5120 + 256 * h), (2, 6144 + 256 * h), (3, 3072 + 256 * h), (4, 4096 + 256 * h)]
                 for h in range(4)]

        r_scrP = [Res("scrP%d" % i) for i in range(6)]
        r_scrH = [Res("scrH%d" % i) for i in range(5)]
        scr_ok = set()
        pflat = [pslot[s_][:].rearrange("p a b -> p (a b)") for s_ in range(2)]
        hflat = [hslot[s_][:].rearrange("p a b -> p (a b)") for s_ in range(2)]

        def load_pool_w(g):
            s = g % 2
            if ("P", g) in scr_ok:
                fw.op("sp", DMA(pflat[s], scrP[g]), reads=[r_scrP[g]], writes=r_ps[s], dma_key="lp%d" % s)
                return
            for (b, c0) in PCOLS[g]:
                fw.op("pool", DMA(pslot[s][:, :, b * 256:(b + 1) * 256],
                                  win_d[:, c0:c0 + 256].rearrange("(kc p) n -> p kc n", p=128)),
                      writes=[r_ps[s][b]], dma_key="psl%d_%d" % (s, b))
            fw.op("sp", DMA(scrP[g], pflat[s]), reads=r_ps[s], writes=[r_scrP[g]], dma_key="sp%d" % g)
            scr_ok.add(("P", g))

        def load_head_w(h):
            s = h % 2
            if ("H", h) in scr_ok:
                fw.op("sp", DMA(hflat[s], scrH[h]), reads=[r_scrH[h]], writes=r_hs[s], dma_key="lh%d" % s)
                return
            for (b, c0) in HCOLS[h]:
                fw.op("pool", DMA(hslot[s][:, :, b * 256:(b + 1) * 256],
                                  win_d[:, c0:c0 + 256].rearrange("(kc p) n -> p kc n", p=128)),
                      writes=[r_hs[s][b]], dma_key="hs%d_%d" % (s, b))
            fw.op("sp", DMA(scrH[h], hflat[s]), reads=r_hs[s], writes=[r_scrH[h]], dma_key="sh%d" % h)
            scr_ok.add(("H", h))

        def load_wout_a():
            for s in range(2):
                if ("WA", s) in scr_ok:
                    fw.op("sp", DMA(pflat[s], scrP[4 + s]), reads=[r_scrP[4 + s]], writes=r_ps[s], dma_key="lp%d" % s)
                    continue
                for b in range(2):
                    fw.op("pool", DMA(pslot[s][:, :, b * 256:(b + 1) * 256],
                                      wout_d[s * 1024:(s + 1) * 1024, b * 256:(b + 1) * 256].rearrange("(kc p) n -> p kc n", p=128)),
                          writes=[r_ps[s][b]], dma_key="psl%d_%d" % (s, b))
                fw.op("sp", DMA(scrP[4 + s], pflat[s]), reads=r_ps[s], writes=[r_scrP[4 + s]], dma_key="sp%d" % (4 + s))
                scr_ok.add(("WA", s))

        def load_wout_b():
            if "WB" in scr_ok:
                fw.op("sp", DMA(hflat[0][:, 0:16 * 512], scrH[4][:, 0:16 * 512]), reads=[r_scrH[4]], writes=r_hs[0], dma_key="lh0")
                return
            wo1 = hflat[0][:, 0:16 * 512].rearrange("p (e c) -> p e c", c=512)
            fw.op("pool", DMA(wo1, wout_d[:, 512:1024].rearrange("(kc p) n -> p kc n", p=128)),
                  writes=r_hs[0], dma_key="hs0_0")
            fw.op("sp", DMA(scrH[4][:, 0:16 * 512], hflat[0][:, 0:16 * 512]), reads=r_hs[0], writes=[r_scrH[4]], dma_key="sh4")
            scr_ok.add("WB")

        def rstd_ops(ssl, r_ssl):
            fw.op("dve", TSC(ssl[:, 1:2], ssl[:, 0:1], 1.0 / D, EPS, ALU.mult, ALU.add), reads=[r_ssl], writes=[r_ssl])
            fw.op("pool", TT(ssl[:, 2:3], ssl[:, 1:2], mhalf[:, 0:1], ALU.pow), reads=[r_ssl, r_mhalf], writes=[r_ssl])

        def htiles(c0, n):
            return [r_hT[i] for i in range(c0 // 128, (c0 + n + 127) // 128)]

        def norm_phase(mb, solo=False):
            nt = mb["ntok"] // 128
            fw.op("sp", DMA(xsn[0][:], x_d[mb["tok0"]:mb["tok0"] + 128, :]), writes=[r_xsn[0]], dma_key="xsn0")
            for i in range(nt):
                s = i % 2
                if i + 1 < nt:
                    s2 = (i + 1) % 2
                    fw.op("sp", DMA(xsn[s2][:], x_d[mb["tok0"] + (i + 1) * 128:mb["tok0"] + (i + 2) * 128, :]),
                          writes=[r_xsn[s2]], dma_key="xsn%d" % s2)
                if solo:
                    fw.op("dve", lambda e, s=s: e.scalar_tensor_tensor(out=xhatn[s][:], in0=xsn[s][:], scalar=1.0, in1=xsn[s][:],
                                                                       op0=ALU.mult, op1=ALU.mult, accum_out=ssn[s][:, 0:1]),
                          reads=[r_xsn[s]], writes=[r_xhatn[s], r_ssn[s]])
                else:
                    fw.op("act", ACT(xhatn[s][:], xsn[s][:], AF.Square, accum_out=ssn[s][:, 0:1]), reads=[r_xsn[s]], writes=[r_xhatn[s], r_ssn[s]])
                fw.op("pool", TSC(ssn[s][:, 1:2], ssn[s][:, 0:1], 1.0 / D, EPS, ALU.mult, ALU.add), reads=[r_ssn[s]], writes=[r_ssn[s]])
                fw.op("pool", TT(ssn[s][:, 2:3], ssn[s][:, 1:2], mhalf[:, 0:1], ALU.pow), reads=[r_ssn[s], r_mhalf], writes=[r_ssn[s]])
                if solo:
                    fw.op("dve", TSC(xhatn[s][:], xsn[s][:], ssn[s][:, 2:3], None, ALU.mult), reads=[r_xsn[s], r_ssn[s]], writes=[r_xhatn[s]])
                else:
                    fw.op("act", ACT(xhatn[s][:], xsn[s][:], AF.Identity, scale=ssn[s][:, 2:3]), reads=[r_xsn[s], r_ssn[s]], writes=[r_xhatn[s]])
                bk, rb = nb()
                psb = bk[:].bitcast(BF16)
                for fc in range(8):
                    fw.op("pe", TR(psb[:, fc * 128:(fc + 1) * 128], xhatn[s][:, fc * 128:(fc + 1) * 128], identb[:]),
                          reads=[r_xhatn[s], r_identb], writes=[rb])
                for fc in range(8):
                    for (seq, c0, T) in mb["segs"]:
                        lo = max(c0, i * 128)
                        hi = min(c0 + T, (i + 1) * 128)
                        if hi <= lo:
                            continue
                        src = psb[:, fc * 128 + lo - i * 128:fc * 128 + hi - i * 128]
                        dst = hT[:, fc, lo:hi]
                        fw.op("act", ACT(dst, src, AF.Identity, scale=Amod[:, fc, seq:seq + 1], bias=modT[:, fc, seq:seq + 1]),
                              reads=[rb, r_A, r_modT], writes=[r_hT[i]])
                yield

        def gate_phase(mb, mid_hook=None):
            n = mb["ntok"]
            L = mb["L"]
            nch = n // L
            rb_ = xsn[0][0:4, 0:n]; r_rb = r_xsn[0]
            rsp = xsn[1][0:4, 0:n]; r_rsp = r_xsn[1]
            rGn = gate_bc[0:4, 0:n]; r_rGn = r_gbc
            rN = ycatT[:, 14:16, :].rearrange("p a b -> p (a b)").bitcast(F32)[0:4, 0:n]; r_rN = r_rowN
            N = min(512, n)
            for tg in range(n // N):
                cs = slice(tg * N, (tg + 1) * N)
                hr = htiles(tg * N, N)
                bk, rb = nb()
                for kc in range(8):
                    fw.op("pe", MM(bk[0:4, 0:N], wg[:, kc, 0:4], hT[:, kc, cs], start=(kc == 0), stop=(kc == 7)),
                          reads=[r_wg] + hr, writes=[rb])
                fw.op("act", ACT(rb_[:, cs], bk[0:4, 0:N], AF.Identity, bias=sm4[0:4, 0:1]), reads=[rb, r_bias], writes=[r_rb])
                bk, rb = nb()
                for kc in range(8):
                    fw.op("pe", MM(bk[0:4, 0:N], wg[:, kc, 4:8], hT[:, kc, cs], start=(kc == 0), stop=(kc == 7)),
                          reads=[r_wg] + hr, writes=[rb])
                fw.op("act", ACT(rsp[:, cs], bk[0:4, 0:N], AF.Exp, scale=-1.0, bias=sm4[0:4, 2:3]), reads=[rb, r_bias], writes=[r_rsp])
            if mid_hook is not None:
                mid_hook()
            fw.op("act", ACT(rsp, rsp, AF.Ln, bias=1.0), reads=[r_rsp], writes=[r_rsp])
            for (seq, c0, T) in mb["segs"]:
                init = 0.0 if mb["first"] else sm4[0:4, 3:4]
                fw.op("dve", SCAN(rGn[:, c0:c0 + T], rsp[:, c0:c0 + T], rsp[:, c0:c0 + T], init, ALU.add, ALU.max),
                      reads=[r_rsp, r_carry], writes=[r_rGn])
            fw.op("dve", TT(rb_, rb_, rGn, ALU.add), reads=[r_rb, r_rGn], writes=[r_rb])
            for si, (seq, c0, T) in enumerate(mb["segs"]):
                if mb["prompt"]:
                    init = 0.0 if mb["first"] else sm4[0:4, 4:5]
                else:
                    init = sm4[0:4, 8 + si:9 + si]
                fw.op("dve", SCAN(rN[:, c0:c0 + T], rb_[:, c0:c0 + T], rb_[:, c0:c0 + T], init, ALU.max, ALU.max),
                      reads=[r_rb, r_carry, r_m0T], writes=[r_rN] + (list(r_yc) if si == 0 else []))
            if mb["last"]:
                for (seq, c0, T) in mb["segs"]:
                    fw.op("dve", TT(sm4[0:4, 5:6], rN[:, c0 + T - 1:c0 + T], rGn[:, c0 + T - 1:c0 + T], ALU.subtract),
                          reads=[r_rGn, r_rN], writes=[r_mend])
                    outs.append(fw.op("sp", DMA(m_o[seq:seq + 1, :].rearrange("o h -> h o"), sm4[0:4, 5:6], allow_slow_non_contiguous=True),
                                      reads=[r_mend], dma_key="o_m"))
            v = lambda a: a.rearrange("p (c l) -> p c l", l=L)
            Rv = v(rN)[:, :, L - 1]
            Rp = sm4[0:4, 16:16 + nch]
            dLr = sm4[0:4, 24:24 + nch]
            if mb["prompt"]:
                if mb["first"]:
                    fw.op("dve", MSET(Rp[:, 0:1], 0.0), writes=[r_Rp])
                else:
                    fw.op("dve", CP(Rp[:, 0:1], sm4[0:4, 4:5]), reads=[r_carry], writes=[r_Rp])
                fw.op("dve", CP(Rp[:, 1:nch], Rv[:, 0:nch - 1]), reads=[r_rN], writes=[r_Rp])
            else:
                fw.op("dve", CP(Rp, sm4[0:4, 8:12]), reads=[r_m0T], writes=[r_Rp])
            if not mb["last"]:
                fw.op("dve", CP(sm4[0:4, 3:4], rGn[:, n - 1:n]), reads=[r_rGn], writes=[r_carry])
                fw.op("dve", CP(sm4[0:4, 4:5], rN[:, n - 1:n]), reads=[r_rN, r_Rp], writes=[r_carry])
            Rbc = v(rN)[:, :, L - 1:L].to_broadcast([4, nch, L])
            rwL = rsp
            rfl = rGn
            fw.op("dve", TT(v(rwL), v(rb_), Rbc, ALU.subtract), reads=[r_rb, r_rN], writes=[r_rsp])
            fw.op("dve", TT(v(rfl), v(rGn), Rbc, ALU.subtract), reads=[r_rGn, r_rN, r_carry, r_mend], writes=[r_rGn])
            fw.op("act", ACT(rwL, rwL, AF.Exp), reads=[r_rsp], writes=[r_rsp])
            fw.op("act", ACT(rfl, rfl, AF.Exp, scale=2.0), reads=[r_rGn], writes=[r_rGn])
            fw.op("dve", TT(dLr, Rp, Rv, ALU.subtract), reads=[r_Rp, r_rN], writes=[r_dLr])
            fw.op("act", ACT(dLr, dLr, AF.Exp), reads=[r_dLr], writes=[r_dLr])
            fw.op("dve", TT(Xd[0:4, :, 0:nch], dLr.unsqueeze(1).to_broadcast([4, 4, nch]),
                            identf[0:4, 0:4].unsqueeze(2).to_broadcast([4, 4, nch]), ALU.mult),
                  reads=[r_dLr, r_identf], writes=[r_Xd])
            for hh in range(4):
                fw.op("pe", MM(banks[4][:, 128 + hh * 8:128 + hh * 8 + nch], onesf[0:4, :], Xd[0:4, hh, 0:nch]),
                      reads=[r_ones, r_Xd], writes=[r_bk[4]])
            ntile = n // L
            for ti in range(ntile):
                fw.op("pe", MM(banks[4][0:L, ti * 4:ti * 4 + 4], rwL[:, ti * L:(ti + 1) * L], identf[0:4, 0:4]),
                      reads=[r_rsp, r_identf], writes=[r_bk[4]])
                fw.op("pe", MM(banks[4][0:L, 64 + ti * 4:64 + ti * 4 + 4], rfl[:, ti * L:(ti + 1) * L], identf[0:4, 0:4]),
                      reads=[r_rGn, r_identf], writes=[r_bk[4]])
            fw.op("dve", CP(dLbc[:].rearrange("p a b -> p (a b)"), banks[4][:, 128:160]), reads=[r_bk[4]], writes=[r_dLbc])
            fw.op("dve", CP(wLT[0:L, 0:ntile * 4], banks[4][0:L, 0:ntile * 4]), reads=[r_bk[4]], writes=[r_tms])
            fw.op("dve", TSC(wLT16[0:L, 0:ntile * 4], banks[4][0:L, 0:ntile * 4], 0.0625, None, ALU.mult), reads=[r_bk[4]], writes=[r_tms])
            fw.op("dve", CP(fl2T[0:L, 0:ntile * 4], banks[4][0:L, 64:64 + ntile * 4]), reads=[r_bk[4]], writes=[r_tms])

        def pgeom(mb):
            n = mb["ntok"]
            N = min(NT, n)
            if mb["prompt"]:
                return N, 1, N
            return N, 4, 32

        def pool_inproj(mb, g, tg, ntg):
            par = tg % 2
            s = g % 2
            W = pslot[s]
            N, nseg, T = pgeom(mb)
            E = [xpT[par][:, f, 0:nseg * (15 + T)].rearrange("p (s t) -> p s t", s=nseg) for f in range(2)]
            Eo = [xpT[1 - par][:, f, 0:nseg * (15 + T)].rearrange("p (s t) -> p s t", s=nseg) for f in range(2)]
            cs = slice(tg * N, (tg + 1) * N)
            hr = htiles(tg * N, N)
            for f in range(2):
                if mb["prompt"]:
                    if tg == 0:
                        if mb["first"]:
                            fw.op("dve", MSET(E[f][:, :, 0:15], 0.0), writes=[r_xpT[par]])
                        else:
                            fw.op("dve", CP(E[f][:, 0, 0:15], halo[:, 2 * g + f, :]), reads=[r_halo], writes=[r_xpT[par]])
                    else:
                        fw.op("dve", CP(E[f][:, 0, 0:15], Eo[f][:, 0, T:T + 15]), reads=[r_xpT[1 - par]], writes=[r_xpT[par]])
                else:
                    fw.op("dve", CP(E[f][:, :, 0:15], halos[:, 2 * g + f, :, :]), reads=[r_halos], writes=[r_xpT[par]])
            for f in range(2):
                bk, rb = nb()
                for kc in range(8):
                    fw.op("pe", MM(bk[:, 0:N], W[:, kc, f * 128:(f + 1) * 128], hT[:, kc, cs], start=(kc == 0), stop=(kc == 7)),
                          reads=[r_ps[s][0]] + hr, writes=[rb])
                fw.op("dve", CP(E[f][:, :, 15:15 + T], bk[:, 0:N].rearrange("p (s t) -> p s t", s=nseg)), reads=[rb], writes=[r_xpT[par]])
                yield
            for f in range(2):
                bk, rb = nb()
                for kc in range(8):
                    fw.op("pe", MM(bk[:, 0:N], W[:, kc, 256 + f * 128:256 + (f + 1) * 128], hT[:, kc, cs], start=(kc == 0), stop=(kc == 7)),
                          reads=[r_ps[s][1]] + hr, writes=[rb])
                fw.op("act", ACT(szp[par][:, f, 0:N], bk[:, 0:N], AF.Silu), reads=[rb], writes=[r_szp[par]])
                yield

        pcount = [0]

        def pool_dep(mb, g, tg, ntg):
            par = tg % 2
            s = g % 2
            W = pslot[s]
            N, nseg, T = pgeom(mb)
            w = 2 ** (g + 1)
            E = [xpT[par][:, f, 0:nseg * (15 + T)].rearrange("p (s t) -> p s t", s=nseg) for f in range(2)]
            tAv = tA[:, 0:nseg * (15 + T)].rearrange("p (s t) -> p s t", s=nseg)
            tBv = tB[:, 0:nseg * (15 + T)].rearrange("p (s t) -> p s t", s=nseg)
            cs = slice(tg * N, (tg + 1) * N)
            Ltot = 15 + T
            flush(pend_hn)
            tCv = yo[0][:, 0:nseg * (15 + T)].rearrange("p (s t) -> p s t", s=nseg)
            tDv = yo[1][:, 0:nseg * (15 + T)].rearrange("p (s t) -> p s t", s=nseg)
            for f in range(2):
                cur, rcur = E[f], r_xpT[par]
                tmps = [(tAv, r_tA), (tBv, r_tB)] if f == 0 else [(tCv, r_yo[0]), (tDv, r_yo[1])]
                weng = "pool" if f == 0 else "dve"
                step = 1
                k = 0
                while step < w:
                    lo = 2 * step - 1
                    nxt, rn = tmps[k % 2]
                    fw.op(weng, TT(nxt[:, :, lo:Ltot], cur[:, :, lo:Ltot], cur[:, :, lo - step:Ltot - step], ALU.add),
                          reads=[rcur], writes=[rn])
                    cur, rcur = nxt, rn
                    step *= 2
                    k += 1
                oth, roth = tmps[k % 2]
                pv = pooledT[par][:, f, 0:N].rearrange("p (s t) -> p s t", s=nseg)
                fw.op("dve", STT(pv, cur[:, :, 15:Ltot], 1.0 / w, E[f][:, :, 15:Ltot], ALU.mult, ALU.subtract),
                      reads=[rcur, r_xpT[par]], writes=[r_pooled[par]])
                if mb["prompt"] and mb["first"] and tg == 0:
                    fw.op("dve", TT(oth[:, 0, 0:w - 1], cur[:, 0, 15:15 + w - 1], invc[:, 0:w - 1], ALU.mult),
                          reads=[rcur, r_invc], writes=[roth])
                    fw.op("dve", TT(pooledT[par][:, f, 0:w - 1], oth[:, 0, 0:w - 1], E[f][:, 0, 15:15 + w - 1], ALU.subtract),
                          reads=[roth, r_xpT[par]], writes=[r_pooled[par]])
                if mb["prompt"] and not mb["last"] and tg == ntg - 1:
                    fw.op("dve", CP(halo[:, 2 * g + f, :], E[f][:, 0, T:T + 15]), reads=[r_xpT[par]], writes=[r_halo])
                yield
            for dcl in range(2):
                if dcl == 0:
                    flush(pend_tr)
                bk, rb = nb()
                for ccl in range(2):
                    fw.op("pe", MM(bk[:, 0:N], wpool[:, g, ccl, dcl * 128:(dcl + 1) * 128], pooledT[par][:, ccl, 0:N],
                                   start=(ccl == 0), stop=(ccl == 1)), reads=[r_wpool, r_pooled[par]], writes=[rb])
                fw.op("dve", STT(ycatT[:, 2 * g + dcl, cs], bk[:, 0:N], vecT[:, 2 * g + dcl, 1:2], szp[par][:, dcl, 0:N], ALU.mult, ALU.mult),
                      reads=[rb, r_vecT, r_szp[par]], writes=[r_yc[i] for i in range(tg * N // 128, ((tg + 1) * N + 127) // 128)])
                yield
            if mb["last"] and tg == ntg - 1:
                for si, (seq, c0, Ts) in enumerate(mb["segs"]):
                    bk, rb = nb()
                    lc = slice(c0 + Ts - 15, c0 + Ts)
                    pp = pcount[0] % 2
                    pcount[0] += 1
                    for kc in range(8):
                        fw.op("pe", MM(bk[0:15, 0:256], hT[:, kc, lc], W[:, kc, 0:256], start=(kc == 0), stop=(kc == 7)),
                              reads=[r_ps[s][0]] + htiles(c0 + Ts - 15, 15), writes=[rb])
                    fw.op("act", ACT(pstage[pp][0:15, :], bk[0:15, 0:256], AF.Identity), reads=[rb], writes=[r_pstage[pp]])
                    outs.append(fw.op("sp", DMA(pool_o[seq, :, g * 256:(g + 1) * 256], pstage[pp][0:15, :]),
                                      reads=[r_pstage[pp]], dma_key="o_pool%d" % pp))
                    yield

        def head_inproj(mb, h, tg, ntg):
            par = tg % 2
            s = h % 2
            W = hslot[s]
            n = mb["ntok"]
            N = min(NT, n)
            cs = slice(tg * N, (tg + 1) * N)
            hr = htiles(tg * N, N)
            if tg == 0:
                if mb["prompt"]:
                    if mb["first"]:
                        fw.op("dve", MSET(Call[:, h, :, :], 0.0), writes=[r_C[h]])
                else:
                    for hl in ([0, 1] if h == 0 else ([h + 1] if h + 1 < 4 else [])):
                        for j in range(4):
                            rc = Cres(mb, hl, j)
                            extra = list(r_yc) if hl == 1 else []
                            for dc in range(2):
                                fw.op("sp", DMA(Cst(mb, hl, j, dc)[:, 0:256], sC_d[j, hl, dc * 128:(dc + 1) * 128, :]),
                                      writes=[rc] + extra, dma_key="ldC%d_%d" % (hl % 2, j))
                                fw.op("sp", DMA(Cst(mb, hl, j, dc)[:, 256:257], sn_d[j, hl, dc * 128:(dc + 1) * 128].rearrange("(p o) -> p o", o=1),
                                                allow_slow_non_contiguous=True),
                                      writes=[rc] + extra, dma_key="ldC%d_%d" % (hl % 2, j))
            specs = [(0, None, qT, r_qT), (3, "k", kT, r_kT), (1, "sig", so, r_so), (2, "silu", gm, r_gm)]
            for (blk, kind, dstT, rdst) in specs:
                for f in range(2):
                    bk, rb = nb()
                    for kc in range(8):
                        fw.op("pe", MM(bk[:, 0:N], W[:, kc, blk * 256 + f * 128:blk * 256 + (f + 1) * 128], hT[:, kc, cs],
                                       start=(kc == 0), stop=(kc == 7)), reads=[r_hs[s][blk]] + hr, writes=[rb])
                    dst = dstT[par][:, f, 0:N]
                    if kind is None:
                        fw.op("dve", CP(dst, bk[:, 0:N]), reads=[rb], writes=[rdst[par]])
                    elif kind == "k":
                        fw.op("act", ACT(dst, bk[:, 0:N], AF.Identity, scale=0.0625), reads=[rb], writes=[rdst[par]])
                    elif kind == "sig":
                        fw.op("act", ACT(dst, bk[:, 0:N], AF.Sigmoid), reads=[rb], writes=[rdst[par]])
                    else:
                        fw.op("act", ACT(dst, bk[:, 0:N], AF.Silu), reads=[rb], writes=[rdst[par]])
                    yield
            fw.op("pool", TT(gm[par][:, :, 0:N], gm[par][:, :, 0:N], so[par][:, :, 0:N], ALU.mult), reads=[r_gm[par], r_so[par]], writes=[r_gm[par]])

        ccount = [0]
        r_rowN = R("rowN")
        r_C2 = [R("C2_%d" % i) for i in range(4)]

        def Cst(mb, h, cidx, dc):
            if mb["prompt"] or h % 2 == 0:
                return Call[:, cidx, dc, :]
            return ycatT[:, cidx * 2 + dc, 128:128 + 514].bitcast(F32)

        def Cres(mb, h, cidx):
            if mb["prompt"] or h % 2 == 0:
                return r_C[cidx]
            return r_C2[cidx]

        pend_tr = []
        pend_hn = []

        def flush(lst):
            while lst:
                lst.pop(0)()


        def head_dep(mb, h, tg, ntg):
            par = tg % 2
            s = h % 2
            W = hslot[s]
            n = mb["ntok"]
            N = min(NT, n)
            L = mb["L"]
            npt = N // L
            ntile = n // L
            cbase = ccount[0]
            ccount[0] += npt
            M = L

            def kv_ops(j):
                ti = tg * npt + j
                c0 = ti * L
                cc = (cbase + j) % 2
                cols = slice(c0, c0 + M)
                sc16 = wLT16[0:M, ti * 4 + h:ti * 4 + h + 1]
                bk, rb = nb()
                psb = bk[:].bitcast(BF16)
                lcj = slice(j * L, (j + 1) * L)
                sc1 = wLT[0:M, ti * 4 + h:ti * 4 + h + 1]
                for kc in range(8):
                    fw.op("pe", MM(bk[0:M, 0:256], hT[:, kc, cols], W[:, kc, 1024:1280], start=(kc == 0), stop=(kc == 7)),
                          reads=[r_hs[s][4]] + htiles(c0, M), writes=[rb])
                for dc in range(2):
                    fw.op("pe", TR(psb[0:M, 512 + dc * 128:512 + (dc + 1) * 128], kT[par][:, dc, lcj], identb[:]),
                          reads=[r_kT[par], r_identb], writes=[rb])
                fw.op("dve", TSC(kw[cc][0:M, :], psb[0:M, 512:768], sc1, None, ALU.mult), reads=[rb, r_tms], writes=[r_kw[cc]])
                fw.op("act", ACT(vaug[cc][0:M, 0:256], bk[0:M, 0:256], AF.Identity), reads=[rb], writes=[r_va[cc]])

            kv_ops(0)
            for j in range(npt):
                ti = tg * npt + j
                c0 = ti * L
                cidx = h if mb["prompt"] else ti
                cc = (cbase + j) % 2
                cols = slice(c0, c0 + M)
                lc = slice(j * L, (j + 1) * L)
                sc = wLT[0:M, ti * 4 + h:ti * 4 + h + 1]
                fl2 = fl2T[0:M, ti * 4 + h:ti * 4 + h + 1]
                dl = dLbc[:, h, ti:ti + 1]
                for dc in range(2):
                    fw.op("pe", MM(banks[4][0:M, 0:M], kT[par][:, dc, lc], qT[par][:, dc, lc], start=(dc == 0), stop=(dc == 1)),
                          reads=[r_kT[par], r_qT[par]], writes=[r_bk[4]])
                fw.op("dve", STT(PT[cc][0:M, 0:M], banks[4][0:M, 0:M], sc, maskT[0:M, 0:M], ALU.mult, ALU.mult),
                      reads=[r_bk[4], r_tms, r_mask], writes=[r_PT[cc]])
                flush(pend_hn)
                rc = Cres(mb, h, cidx)
                for dc in range(2):
                    fw.op("act", ACT(Cbf[:, dc, :], Cst(mb, h, cidx, dc), AF.Identity, scale=dl), reads=[rc, r_dLbc], writes=[r_Cbf])
                if j + 1 < npt:
                    kv_ops(j + 1)
                fw.op("pe", MM(banks[5][0:M, 0:257], PT[cc][0:M, 0:M], vaug[cc][0:M, :], start=True, stop=False),
                      reads=[r_PT[cc], r_va[cc]], writes=[r_bk[5]])
                for dc in range(2):
                    fw.op("pe", MM(banks[5][0:M, 0:257], qT[par][:, dc, lc], Cbf[:, dc, :], start=False, stop=(dc == 1)),
                          reads=[r_qT[par], r_Cbf], writes=[r_bk[5]])
                for dc in range(2):
                    fw.op("pe", MM(banks[6 + dc][:, 0:257], kw[cc][0:M, dc * 128:(dc + 1) * 128], vaug[cc][0:M, :]),
                          reads=[r_kw[cc], r_va[cc]], writes=[r_bk[6 + dc]])
                flush(pend_tr)
                for dc in range(2):
                    fw.op("dve", STT(Cst(mb, h, cidx, dc), Cst(mb, h, cidx, dc), dl, banks[6 + dc][:, 0:257], ALU.mult, ALU.add),
                          reads=[rc, r_dLbc, r_bk[6 + dc]], writes=[rc])
                t = stt[cc]
                rt = r_stt[cc]
                fw.op("dve", lambda e, t=t, M=M: e.bn_stats(out=t[0:M, 0:6], in_=banks[5][0:M, 0:256]), reads=[r_bk[5]], writes=[rt])
                fw.op("dve", lambda e, t=t, M=M: e.bn_aggr(out=t[0:M, 6:8], in_=t[0:M, 0:6]), reads=[rt], writes=[rt])
                fw.op("dve", CP(t[0:M, 8:9], banks[5][0:M, 256:257]), reads=[r_bk[5]], writes=[rt])
                fw.op("dve", TT(t[0:M, 9:10], t[0:M, 8:9], t[0:M, 8:9], ALU.mult), reads=[rt], writes=[rt])
                fw.op("dve", TT(t[0:M, 10:11], t[0:M, 9:10], fl2, ALU.max), reads=[rt, r_tms], writes=[rt])
                fw.op("dve", STT(t[0:M, 11:12], t[0:M, 10:11], EPS, t[0:M, 7:8], ALU.mult, ALU.add), reads=[rt], writes=[rt])
                fw.op("pool", TT(t[0:M, 12:13], t[0:M, 11:12], mhalf[0:M, 0:1], ALU.pow), reads=[rt, r_mhalf], writes=[rt])
                def emit_hn(cc=cc, M=M, t=t, rt=rt):
                    fw.op("dve", TSC(hn[cc][0:M, :], banks[5][0:M, 0:256], t[0:M, 6:7], t[0:M, 12:13], ALU.subtract, ALU.mult),
                          reads=[r_bk[5], rt], writes=[r_hn[cc]])
                pend_hn.append(emit_hn)
                def emit_tr(cc=cc, M=M, cols=cols, lc=lc, c0=c0, h=h, par=par):
                    bk, rb = nb()
                    psb = bk[:].bitcast(BF16)
                    for dcl in range(2):
                        fw.op("pe", TR(psb[:, dcl * 128:dcl * 128 + M], hn[cc][0:M, dcl * 128:(dcl + 1) * 128], identb[0:M, 0:M]),
                              reads=[r_hn[cc], r_identb], writes=[rb])
                    for dcl in range(2):
                        fw.op("dve", STT(ycatT[:, 8 + 2 * h + dcl, cols], psb[:, dcl * 128:dcl * 128 + M], vecT[:, 2 * h + dcl, 2:3],
                                         gm[par][:, dcl, lc], ALU.mult, ALU.mult),
                              reads=[rb, r_vecT, r_gm[par]], writes=[r_yc[i] for i in range(c0 // 128, (c0 + M + 127) // 128)])
                pend_tr.append(emit_tr)
                if mb["last"] and (not mb["prompt"] or ti == ntile - 1):
                    seq = mb["segs"][0][0] if mb["prompt"] else mb["segs"][ti][0]
                    for dc in range(2):
                        outs.append(fw.op("sp", DMA(C_o[seq, h, dc * 128:(dc + 1) * 128, :], Cst(mb, h, cidx, dc)[:, 0:256]),
                                          reads=[rc], dma_key="o_C%d_%d" % (h % 2 if not mb["prompt"] else 0, cidx)))
                        outs.append(fw.op("sp", DMA(n_o[seq, h, dc * 128:(dc + 1) * 128].rearrange("(p o) -> p o", o=1), Cst(mb, h, cidx, dc)[:, 256:257],
                                                    allow_slow_non_contiguous=True),
                                          reads=[rc], dma_key="o_C%d_%d" % (h % 2 if not mb["prompt"] else 0, cidx)))
                yield

        def gatebc_phase(mb):
            k = 0
            for half in range(2):
                bk, rb = nb()
                for fcl in range(4):
                    fc = half * 4 + fcl
                    for si, (seq, c0, T) in enumerate(mb["segs"]):
                        d = k % 2
                        k += 1
                        fw.op("dve", TSC(dg[d][:], identf[:], modT[:, 16 + fc, seq:seq + 1], None, ALU.mult),
                              reads=[r_identf, r_modT], writes=[r_dg[d]])
                        lhs = onesf[:, :] if mb["prompt"] else smask[:, si, :]
                        fw.op("pe", MM(bk[:, fcl * 128:(fcl + 1) * 128], lhs, dg[d][:], start=(si == 0), stop=(si == len(mb["segs"]) - 1)),
                              reads=[r_ones, r_smask, r_dg[d]], writes=[rb])
                fw.op("dve", CP(gate_bc[:, half * 512:(half + 1) * 512], bk[:, 0:512]), reads=[rb], writes=[r_gbc])

        def final_phase(mb):
            n = mb["ntok"]
            nt = n // 128
            wo1 = hslot[0][:].rearrange("p a b -> p (a b)")[:, 0:16 * 512].rearrange("p (e c) -> p e c", c=512)
            prev_tail = [None]
            fw.op("sp", DMA(xs[0][:, 0:D], x_d[mb["tok0"]:mb["tok0"] + 128, :]), writes=[r_xs[0]], dma_key="xs0")
            for i in range(nt):
                s = i % 2
                if i + 1 < nt:
                    s2 = (i + 1) % 2
                    fw.op("sp", DMA(xs[s2][:, 0:D], x_d[mb["tok0"] + (i + 1) * 128:mb["tok0"] + (i + 2) * 128, :]),
                          writes=[r_xs[s2]], dma_key="xs%d" % s2)
                if prev_tail[0] is not None:
                    prev_tail[0]()
                    prev_tail[0] = None
                for nh in range(2):
                    bk, rb = nb()
                    for ec in range(16):
                        if nh == 0:
                            rhs = pslot[ec // 8][:, ec % 8, :]
                            rr = r_ps[ec // 8]
                        else:
                            rhs = wo1[:, ec, :]
                            rr = r_hs[0]
                        fw.op("pe", MM(bk[:, 0:512], ycatT[:, ec, i * 128:(i + 1) * 128], rhs, start=(ec == 0), stop=(ec == 15)),
                              reads=[r_yc[i]] + rr, writes=[rb])
                    fw.op("dve", TT(yv[s][:, nh * 512:(nh + 1) * 512], bk[:, 0:512], gate_bc[:, nh * 512:(nh + 1) * 512], ALU.mult),
                          reads=[rb, r_gbc], writes=[r_yv[s]])
                fw.op("dve", TT(yv[s][:], yv[s][:], xs[s][:, 0:D], ALU.add), reads=[r_yv[s], r_xs[s]], writes=[r_yv[s]])
                fw.op("dve", lambda e, s=s: e.scalar_tensor_tensor(out=yo[s][:], in0=yv[s][:], scalar=1.0, in1=yv[s][:], op0=ALU.mult, op1=ALU.mult,
                                                                   accum_out=ss[s][:, 0:1]),
                      reads=[r_yv[s]], writes=[r_yo[s], r_ss[s]])

                def tail(s=s, i=i):
                    rstd_ops(ss[s], r_ss[s])
                    fw.op("dve", STT(yo[s][:], yv[s][:], ss[s][:, 2:3], gfin[:], ALU.mult, ALU.mult), reads=[r_yv[s], r_ss[s], r_gfin], writes=[r_yo[s]])
                    outs.append(fw.op("sp", DMA(y_o[mb["tok0"] + i * 128:mb["tok0"] + (i + 1) * 128, :], yo[s][:]), reads=[r_yo[s]], dma_key="o_y%d" % s))
                prev_tail[0] = tail
                yield
            if prev_tail[0] is not None:
                prev_tail[0]()
                prev_tail[0] = None

        def run_all(g):
            if g is not None:
                for _ in g:
                    pass

        def interleave(a, b, ra=1, rb_=1):
            gens = [a, b]
            rates = [ra, rb_]
            alive = [a is not None, b is not None]
            while any(alive):
                for gi in range(2):
                    if not alive[gi]:
                        continue
                    for _ in range(rates[gi]):
                        try:
                            next(gens[gi])
                        except StopIteration:
                            alive[gi] = False
                            break

        INP = {"P": pool_inproj, "H": head_inproj}
        DEP = {"P": pool_dep, "H": head_dep}
        prev_final = None
        for mi, mb in enumerate(mbs):
            ntg = max(1, mb["ntok"] // NT)
            load_head_w(1)
            interleave(prev_final, norm_phase(mb, solo=(prev_final is None)))
            prev_final = None
            load_pool_w(0)
            load_pool_w(1)
            load_head_w(0)
            if stop < 1:
                continue
            steps = []
            for i in range(4):
                for tg in range(ntg):
                    steps.append(("P", i, tg))
                for tg in range(ntg):
                    steps.append(("H", i, tg))
            k0, i0, t0_ = steps[0]
            gate_phase(mb, mid_hook=lambda: run_all(INP[k0](mb, i0, t0_, ntg)))
            for si_, (kind, i, tg) in enumerate(steps):
                nxt = steps[si_ + 1] if si_ + 1 < len(steps) else None
                gen_dep = DEP[kind](mb, i, tg, ntg)
                gen_in = INP[nxt[0]](mb, nxt[1], nxt[2], ntg) if nxt is not None else None
                interleave(gen_dep, gen_in, 1, 2)
                if kind == "H" and i == 0 and tg == ntg - 1:
                    gatebc_phase(mb)
                if tg == ntg - 1:
                    if kind == "P":
                        if i + 2 < 4:
                            load_pool_w(i + 2)
                        elif i == 3 and stop >= 11:
                            load_wout_a()
                    else:
                        if i + 2 < 4:
                            load_head_w(i + 2)
                        elif i == 2 and stop >= 11:
                            load_wout_b()
            flush(pend_hn)
            flush(pend_tr)
            if stop < 11:
                continue
            prev_final = final_phase(mb)
        run_all(prev_final)

        if dbg:
            def dump(name, ap, shape, res):
                d = dout(name, shape)
                dbg_o[name] = shape
                outs.append(fw.op("pool", DMA(d, ap), reads=res, dma_key="dbg_" + name))
            dump("d_hT", hT[:], [128, 8, TMB], r_hT)
            dump("d_ycatT", ycatT[:], [128, 16, TMB], r_yc)
            dump("d_modT", modT[:], [128, 24, 6], [r_modT])
            dump("d_wLT", wLT[:], [128, 32], [r_tms])
            dump("d_fl2T", fl2T[:], [128, 32], [r_tms])
            dump("d_dLbc", dLbc[:], [128, 4, 8], [r_dLbc])
            dump("d_gbc", gate_bc[:], [128, D], [r_gbc])
            dump("d_pooledT", pooledT[0][:], [128, 2, NT], [r_pooled[0]])
            dump("d_gm", gm[0][:], [128, 2, NT], [r_gm[0]])

        fw.emit(final_wait_ops=outs)
    return nc, dbg_o


_CACHE = {}


def _consts():
    ident = np.eye(128, dtype=np.float32)
    s = np.arange(128)
    maskT = (s[:, None] <= s[None, :]).astype(np.float32)
    invc = np.tile((1.0 / np.arange(1, 17, dtype=np.float32))[None, :], (128, 1)).astype(np.float32)
    return ident, maskT, invc


def make_in_maps(x_prompt, x_sample, c_prompt, c_sample, state_pool, state_C, state_n, state_m,
                 w_ada, b_ada, g_norm, w_in, b_i, b_f, w_pool, pool_scale, g_head, w_out, g_final):
    f = lambda a: np.ascontiguousarray(np.asarray(a, dtype=np.float32))
    ident, maskT, invc = _consts()
    vecs = f(np.stack([np.asarray(g_norm)[0], np.asarray(pool_scale)[0], np.asarray(g_head)[0],
                       np.asarray(b_ada)[0, 0:D], np.asarray(b_ada)[0, D:2 * D], np.asarray(b_ada)[0, 2 * D:3 * D]], axis=0))
    shared = {
        "w_ada": f(np.asarray(w_ada)[0]), "vecs": vecs, "w_in": f(np.asarray(w_in)[0]),
        "b_i": f(np.asarray(b_i)[0].reshape(4, 1)), "b_f": f(np.asarray(b_f)[0].reshape(4, 1)),
        "w_pool": f(np.asarray(w_pool)[0]), "w_out": f(np.asarray(w_out)[0]), "g_final": f(np.asarray(g_final).reshape(1, D)),
        "ident": ident, "maskT": maskT, "invcnt": invc,
    }
    xp = np.asarray(x_prompt); xsm = np.asarray(x_sample)
    in_maps = []
    for c in range(NCORES):
        m = dict(shared)
        m["x"] = f(np.concatenate([xp[2 * c].reshape(TP, D), xp[2 * c + 1].reshape(TP, D), xsm[4 * c:4 * c + 4].reshape(4 * TS, D)], axis=0))
        m["c"] = f(np.concatenate([np.asarray(c_prompt)[2 * c:2 * c + 2], np.asarray(c_sample)[4 * c:4 * c + 4]], axis=0))
        m["st_pool"] = f(np.asarray(state_pool)[0, 4 * c:4 * c + 4])
        m["st_C"] = f(np.asarray(state_C)[0, 4 * c:4 * c + 4])
        m["st_n"] = f(np.asarray(state_n)[0, 4 * c:4 * c + 4])
        m["st_mT"] = f(np.asarray(state_m)[0, 4 * c:4 * c + 4].T)
        in_maps.append(m)
    return in_maps


def kernel(**inputs):
    if "nc" not in _CACHE:
        _CACHE["nc"] = build()[0]
    nc = _CACHE["nc"]
    in_maps = make_in_maps(**inputs)
    res = run_bass_kernel_spmd(nc, in_maps, core_ids=list(range(NCORES)))
    R_ = res.results
    y_prompt = np.zeros((16, TP, D), np.float32)
    y_sample = np.zeros((32, TS, D), np.float32)
    pp = np.zeros((1, 16, 15, D), np.float32); pc = np.zeros((1, 16, 4, 256, 256), np.float32)
    pn = np.zeros((1, 16, 4, 256), np.float32); pm = np.zeros((1, 16, 4), np.float32)
    sp_ = np.zeros((1, 32, 15, D), np.float32); sc = np.zeros((1, 32, 4, 256, 256), np.float32)
    sn = np.zeros((1, 32, 4, 256), np.float32); sm = np.zeros((1, 32, 4), np.float32)
    for c in range(NCORES):
        r = R_[c]
        y = r["y"]
        y_prompt[2 * c] = y[0:TP]
        y_prompt[2 * c + 1] = y[TP:2 * TP]
        y_sample[4 * c:4 * c + 4] = y[2 * TP:].reshape(4, TS, D)
        pp[0, 2 * c:2 * c + 2] = r["pool_o"][0:2]; sp_[0, 4 * c:4 * c + 4] = r["pool_o"][2:6]
        pc[0, 2 * c:2 * c + 2] = r["C_o"][0:2]; sc[0, 4 * c:4 * c + 4] = r["C_o"][2:6]
        pn[0, 2 * c:2 * c + 2] = r["n_o"][0:2]; sn[0, 4 * c:4 * c + 4] = r["n_o"][2:6]
        pm[0, 2 * c:2 * c + 2] = r["m_o"][0:2]; sm[0, 4 * c:4 * c + 4] = r["m_o"][2:6]
    return (y_prompt, y_sample, pp, pc, pn, pm, sp_, sc, sn, sm)
```

```python
import contextlib
import os
import numpy as np
SUB = int(os.environ.get('SUB', '99'))
import concourse.bass as bass
import concourse.mybir as mybir
from concourse.bass_utils import run_bass_kernel_spmd

F32 = mybir.dt.float32
BF16 = mybir.dt.bfloat16
AF = mybir.ActivationFunctionType
ALU = mybir.AluOpType

D = 1024
DIN = 7176
TP = 2048
TS = 32
NTOK = 2 * TP + 4 * TS
TMB = 1024
EPS = 1e-6
NCORES = 8


class Res:
    __slots__ = ("name", "w", "rc", "rd", "excl")

    def __init__(self, name, excl=False):
        self.name = name
        self.excl = excl
        self.w = None
        self.rc = {}
        self.rd = []


class Op:
    __slots__ = ("eng", "fn", "deps", "signal", "count", "dma_sem", "idx")


class FW:
    ENGS = ("pe", "act", "dve", "pool", "sp")

    def __init__(self, nc):
        self.nc = nc
        self.ops = {e: [] for e in self.ENGS}
        self.dma_keys = {}
        self.n = 0

    def op(self, eng, fn, reads=(), writes=(), dma_key=None):
        o = Op()
        o.eng, o.fn, o.signal, o.count, o.dma_sem, o.idx = eng, fn, False, None, None, self.n
        self.n += 1
        deps = []
        writes = list(writes) + [r for r in reads if r.excl]
        reads = [r for r in reads if not r.excl]
        for r in reads:
            if r.w is not None:
                deps.append(r.w)
        for r in writes:
            if r.w is not None:
                pw = r.w
                if not (dma_key is not None and pw.dma_sem is not None and pw.dma_sem[0] == dma_key and pw.eng == eng):
                    deps.append(pw)
            deps.extend(r.rc.values())
            deps.extend(r.rd)
        best = {}
        ded = []
        for d in deps:
            if d.dma_sem is not None:
                ded.append(d)
                continue
            if d.eng == eng and eng in ("pe", "sp"):
                continue
            b = best.get(d.eng)
            if b is None or d.idx > b.idx:
                best[d.eng] = d
        ded.extend(best.values())
        o.deps = ded
        for d in ded:
            d.signal = True
        if dma_key is not None:
            ent = self.dma_keys.setdefault(dma_key, [len(self.dma_keys), 0])
            ent[1] += 16
            o.dma_sem = (dma_key, ent[1])
        for r in reads:
            if dma_key is not None:
                r.rd.append(o)
            else:
                r.rc[eng] = o
        for r in writes:
            r.w = o
            r.rc = {}
            r.rd = []
        self.ops[eng].append(o)
        return o

    def emit(self, final_wait_ops=()):
        nc = self.nc
        for e in self.ENGS:
            c = 0
            for o in self.ops[e]:
                if o.dma_sem is None and o.signal:
                    c += 1
                    o.count = c
        with contextlib.ExitStack() as st:
            esem = {e: st.enter_context(nc.semaphore("s_" + e)) for e in self.ENGS}
            dsem = {k: st.enter_context(nc.semaphore("d_%d" % v[0])) for k, v in self.dma_keys.items()}
            block = st.enter_context(nc.Block())

            def tok(o):
                if o.dma_sem is not None:
                    return dsem[o.dma_sem[0]], o.dma_sem[1]
                return esem[o.eng], o.count

            def run(e, handle):
                seen = {}

                def wait(o):
                    s, v = tok(o)
                    if seen.get(id(s), 0) >= v:
                        return
                    seen[id(s)] = v
                    handle.wait_ge(s, v)

                for o in self.ops[e]:
                    mx = {}
                    for d in o.deps:
                        s_, v_ = tok(d)
                        if v_ > mx.get(id(s_), (None, 0))[1]:
                            mx[id(s_)] = (s_, v_)
                    for s_, v_ in mx.values():
                        if seen.get(id(s_), 0) >= v_:
                            continue
                        seen[id(s_)] = v_
                        handle.wait_ge(s_, v_)
                    ins = o.fn(handle)
                    if o.dma_sem is not None:
                        ins.then_inc(dsem[o.dma_sem[0]], 16)
                    elif o.signal:
                        ins.then_inc(esem[e], 1)
                if e == "sp":
                    mx = {}
                    for o in final_wait_ops:
                        s_, v_ = tok(o)
                        if v_ > mx.get(id(s_), (None, 0))[1]:
                            mx[id(s_)] = (s_, v_)
                    for s_, v_ in mx.values():
                        if seen.get(id(s_), 0) < v_:
                            seen[id(s_)] = v_
                            handle.wait_ge(s_, v_)

            @block.tensor
            def _(h):
                run("pe", h)

            @block.scalar
            def _(h):
                run("act", h)

            @block.vector
            def _(h):
                run("dve", h)

            @block.gpsimd
            def _(h):
                run("pool", h)

            @block.sync
            def _(h):
                run("sp", h)


def MM(out, lhsT, rhs, start=True, stop=True):
    return lambda e: e.matmul(out, lhsT=lhsT, rhs=rhs, start=start, stop=stop)


def TR(out, in_, ident):
    return lambda e: e.transpose(out=out, in_=in_, identity=ident)


def ACT(out, in_, func, **kw):
    return lambda e: e.activation(out=out, in_=in_, func=func, **kw)


def TT(out, in0, in1, op):
    return lambda e: e.tensor_tensor(out=out, in0=in0, in1=in1, op=op)


def TSC(out, in0, s1, s2, op0, op1=None):
    if op1 is None:
        return lambda e: e.tensor_scalar(out=out, in0=in0, scalar1=s1, scalar2=None, op0=op0)
    return lambda e: e.tensor_scalar(out=out, in0=in0, scalar1=s1, scalar2=s2, op0=op0, op1=op1)


def STT(out, in0, scalar, in1, op0, op1):
    return lambda e: e.scalar_tensor_tensor(out=out, in0=in0, scalar=scalar, in1=in1, op0=op0, op1=op1)


def CP(out, in_):
    return lambda e: e.tensor_copy(out=out, in_=in_)


def MSET(ap, v):
    return lambda e: e.memset(ap, v)


def DMA(out, in_, **kw):
    return lambda e: e.dma_start(out=out, in_=in_, **kw)


def SCAN(out, d0, d1, init, op0, op1):
    return lambda e: e.tensor_tensor_scan(out=out, data0=d0, data1=d1, initial=init, op0=op0, op1=op1)


def build(mb_limit=None, dbg=False, stop=99):
    nc = bass.Bass("TRN2", target_bir_lowering=False)

    def din(name, shape):
        return nc.dram_tensor(name, list(shape), F32, kind="ExternalInput").ap()

    def dout(name, shape):
        return nc.dram_tensor(name, list(shape), F32, kind="ExternalOutput").ap()

    x_d = din("x", [NTOK, D])
    c_d = din("c", [6, D])
    sp_d = din("st_pool", [4, 15, D])
    sC_d = din("st_C", [4, 4, 256, 256])
    sn_d = din("st_n", [4, 4, 256])
    sm_d = din("st_mT", [4, 4])
    wada_d = din("w_ada", [D, 3 * D])
    vec_d = din("vecs", [6, D])
    win_d = din("w_in", [D, DIN])
    bi_d = din("b_i", [4, 1])
    bf_d = din("b_f", [4, 1])
    wpool_d = din("w_pool", [4, 256, 256])
    wout_d = din("w_out", [2 * D, D])
    gfin_d = din("g_final", [1, D])
    ident_d = din("ident", [128, 128])
    mask_d = din("maskT", [128, 128])
    invc_d = din("invcnt", [128, 16])
    y_o = dout("y", [NTOK, D])
    pool_o = dout("pool_o", [6, 15, D])
    C_o = dout("C_o", [6, 4, 256, 256])
    n_o = dout("n_o", [6, 4, 256])
    m_o = dout("m_o", [6, 4])
    dbg_o = {}
    scrP = nc.dram_tensor("scrP", [6, 128, 8 * 512], BF16).ap()
    scrH = nc.dram_tensor("scrH", [5, 128, 8 * 1280], BF16).ap()

    st = contextlib.ExitStack()
    with st:
        st.enter_context(nc.allow_low_precision("bf16 matmul operands, fp32 accumulation"))
        fw = FW(nc)

        def sb(name, shape, dt=F32):
            return st.enter_context(nc.sbuf_tensor("sb_" + name, list(shape), dt))

        def R(name):
            return Res(name)

        identf = sb("identf", [128, 128]); r_identf = R("identf")
        identb = sb("identb", [128, 128], BF16); r_identb = R("identb")
        maskT = sb("maskT", [128, 128]); r_mask = R("mask")
        onesf = sb("onesf", [128, 128]); r_ones = R("ones")
        smask = sb("smask", [128, 4, 128]); r_smask = R("smask")
        mhalf = sb("mhalf", [128, 1]); r_mhalf = R("mhalf")
        vecT = sb("vecT", [128, 8, 6]); r_vecT = R("vecT")
        gfin = sb("gfin", [128, D]); r_gfin = R("gfin")
        invc = sb("invc", [128, 16]); r_invc = R("invc")
        sm4 = sb("sm4", [128, 64]); r_sm4 = R("sm4")
        r_bias = R("bias"); r_carry = R("carry"); r_m0T = R("m0T"); r_Rp = R("Rp"); r_dLr = R("dLr"); r_mend = R("mend")
        Xd = sb("Xd", [128, 4, 8]); r_Xd = R("Xd")
        modT = sb("modT", [128, 24, 6]); r_modT = R("modT")
        Amod = sb("Amod", [128, 8, 6]); r_A = R("A")
        sTbf = sb("sTbf", [128, 8, 6], BF16); r_sT = R("sT")
        NT = 512
        hT = sb("hT", [128, 8, TMB], BF16); r_hT = [R("hT%d" % i) for i in range(8)]
        ycatT = sb("ycatT", [128, 16, TMB], BF16); r_yc = [R("yc%d" % i) for i in range(8)]
        hslot = [sb("hslot%d" % s, [128, 8, 1280], BF16) for s in range(2)]
        r_hs = [[R("hs%d_%d" % (s, b)) for b in range(5)] for s in range(2)]
        pslot = [sb("pslot%d" % s, [128, 8, 512], BF16) for s in range(2)]
        r_ps = [[R("psl%d_%d" % (s, b)) for b in range(2)] for s in range(2)]
        wpool = sb("wpool", [128, 4, 2, 256], BF16); r_wpool = R("wpool")
        wg = sb("wg", [128, 8, 8], BF16); r_wg = R("wg")
        Call = sb("Call", [128, 4, 2, 257]); r_C = [R("C%d" % i) for i in range(4)]
        Cbf = sb("Cbf", [128, 2, 257], BF16); r_Cbf = R("Cbf")
        wLT = sb("wLT", [128, 32]); wLT16 = sb("wLT16", [128, 32]); fl2T = sb("fl2T", [128, 32]); r_tms = R("tms")
        dLbc = sb("dLbc", [128, 4, 8]); r_dLbc = R("dLbc")
        qT = [sb("qT%d" % p, [128, 2, NT], BF16) for p in range(2)]; r_qT = [R("qT%d" % p) for p in range(2)]
        kT = [sb("kT%d" % p, [128, 2, NT], BF16) for p in range(2)]; r_kT = [R("kT%d" % p) for p in range(2)]
        so = [sb("so%d" % p, [128, 2, NT], BF16) for p in range(2)]; r_so = [R("so%d" % p) for p in range(2)]
        gm = [sb("gm%d" % p, [128, 2, NT], BF16) for p in range(2)]; r_gm = [R("gm%d" % p) for p in range(2)]
        xpT = [sb("xpT%d" % p, [128, 2, 15 + NT]) for p in range(2)]; r_xpT = [R("xpT%d" % p) for p in range(2)]
        szp = [sb("szp%d" % p, [128, 2, NT], BF16) for p in range(2)]; r_szp = [R("szp%d" % p) for p in range(2)]
        pooledT = [sb("pooledT%d" % p, [128, 2, NT], BF16) for p in range(2)]; r_pooled = [R("pooled%d" % p) for p in range(2)]
        yv = [sb("yv%d" % p, [128, D]) for p in range(2)]; r_yv = [R("yv%d" % p) for p in range(2)]
        yo = [sb("yo%d" % p, [128, D]) for p in range(2)]; r_yo = [R("yo%d" % p) for p in range(2)]
        tA = yv[0]; r_tA = r_yv[0]
        tB = yv[1]; r_tB = r_yv[1]
        halo = sb("halo", [128, 8, 15]); r_halo = R("halo")
        halos = sb("halos", [128, 8, 4, 15]); r_halos = R("halos")
        pstage = [sb("pstage%d" % p, [128, 256]) for p in range(2)]; r_pstage = [R("pstage%d" % p) for p in range(2)]
        kw = [sb("kw%d" % s, [128, 256], BF16) for s in range(2)]; r_kw = [R("kw%d" % s) for s in range(2)]
        vaug = [sb("vaug%d" % s, [128, 257], BF16) for s in range(2)]; r_va = [R("va%d" % s) for s in range(2)]
        PT = [sb("PT%d" % s, [128, 128], BF16) for s in range(2)]; r_PT = [R("PT%d" % s) for s in range(2)]
        hn = [sb("hn%d" % s, [128, 256], BF16) for s in range(2)]; r_hn = [R("hn%d" % s) for s in range(2)]
        stt = [sb("stt%d" % s, [128, 16]) for s in range(2)]; r_stt = [R("stt%d" % s) for s in range(2)]
        xs = [xpT[p][:].rearrange("p a b -> p (a b)") for p in range(2)]; r_xs = r_xpT
        ss = [sb("ss%d" % s, [128, 4]) for s in range(2)]; r_ss = [R("ss%d" % s) for s in range(2)]
        xsn = [sb("xsn%d" % s, [128, D]) for s in range(2)]; r_xsn = [R("xsn%d" % s) for s in range(2)]
        xhatn = [sb("xhatn%d" % p, [128, D], BF16) for p in range(2)]; r_xhatn = [R("xhatn%d" % p) for p in range(2)]
        ssn = [sb("ssn%d" % s, [128, 4]) for s in range(2)]; r_ssn = [R("ssn%d" % s) for s in range(2)]
        gate_bc = sb("gate_bc", [128, D]); r_gbc = R("gbc")
        dg = [sb("dg%d" % s, [128, 128]) for s in range(2)]; r_dg = [R("dg%d" % s) for s in range(2)]
        banks = [st.enter_context(nc.psum_tensor("ps%d" % i, [128, 512], F32)) for i in range(8)]
        r_bk = [Res("bank%d" % i, excl=True) for i in range(8)]
        big_i = [0]

        def nb():
            i = big_i[0] % 4
            big_i[0] += 1
            return banks[i], r_bk[i]

        outs = []

        fw.op("sp", DMA(identf[:], ident_d), writes=[r_identf], dma_key="c_ident")
        fw.op("sp", DMA(maskT[:], mask_d), writes=[r_mask], dma_key="c_mask")
        fw.op("sp", DMA(invc[:], invc_d), writes=[r_invc], dma_key="c_invc")
        fw.op("sp", DMA(gfin[:], gfin_d.to_broadcast([128, D])), writes=[r_gfin], dma_key="c_gfin")
        fw.op("sp", DMA(sm4[0:4, 0:1], bi_d), writes=[r_bias], dma_key="c_bias")
        fw.op("sp", DMA(sm4[0:4, 1:2], bf_d), writes=[r_bias], dma_key="c_bias")
        fw.op("sp", DMA(sm4[0:4, 8:12], sm_d), writes=[r_m0T], dma_key="c_m0")
        fw.op("sp", DMA(xs[0][0:6, 0:D], c_d), writes=[r_xs[0]], dma_key="xs0")
        fw.op("sp", DMA(xs[1][0:6, 0:D], vec_d), writes=[r_xs[1]], dma_key="xs1")
        fw.op("pool", DMA(wg[:], win_d[:, 7168:7176].rearrange("(kc p) n -> p kc n", p=128)), writes=[r_wg], dma_key="c_wg")
        fw.op("pool", DMA(wpool[:].rearrange("p g c d -> p (g c) d"),
                          wpool_d.rearrange("g (c p) d -> p (g c) d", p=128)), writes=[r_wpool], dma_key="c_wpool")
        fw.op("dve", CP(identb[:], identf[:]), reads=[r_identf], writes=[r_identb])
        fw.op("dve", MSET(onesf[:], 1.0), writes=[r_ones])
        fw.op("dve", MSET(smask[:], 0.0), writes=[r_smask])
        for j in range(4):
            fw.op("dve", MSET(smask[:, j, j * 32:(j + 1) * 32], 1.0), writes=[r_smask])
        fw.op("dve", MSET(mhalf[:], -0.5), writes=[r_mhalf])
        fw.op("dve", TSC(sm4[0:4, 2:3], sm4[0:4, 1:2], -1.0, None, ALU.mult), reads=[r_bias], writes=[r_bias])
        for s in range(2):
            fw.op("dve", MSET(vaug[s][:, 256:257], 1.0), writes=[r_va[s]])
        fw.op("act", ACT(xs[0][0:6, 0:D], xs[0][0:6, 0:D], AF.Silu), reads=[r_xs[0]], writes=[r_xs[0]])
        for kc in range(8):
            fw.op("pe", MM(banks[4][:, kc * 6:kc * 6 + 6], xs[0][0:6, kc * 128:(kc + 1) * 128], identf[0:6, 0:6]),
                  reads=[r_xs[0], r_identf], writes=[r_bk[4]])
        fw.op("dve", CP(sTbf[:].rearrange("p a b -> p (a b)"), banks[4][:, 0:48]), reads=[r_bk[4]], writes=[r_sT])
        for kc in range(8):
            fw.op("pe", MM(banks[5][:, kc * 6:kc * 6 + 6], xs[1][0:6, kc * 128:(kc + 1) * 128], identf[0:6, 0:6]),
                  reads=[r_xs[1], r_identf], writes=[r_bk[5]])
        fw.op("dve", CP(vecT[:].rearrange("p a b -> p (a b)"), banks[5][:, 0:48]), reads=[r_bk[5]], writes=[r_vecT])
        for j in range(3):
            s = j % 2
            for b in range(4):
                fw.op("pool", DMA(hslot[s][:, :, b * 256:(b + 1) * 256],
                                  wada_d[:, j * 1024 + b * 256:j * 1024 + (b + 1) * 256].rearrange("(kc p) n -> p kc n", p=128)),
                      writes=[r_hs[s][b]], dma_key="hs%d_%d" % (s, b))
            bk, rb = nb()
            for fc in range(8):
                for kc in range(8):
                    fw.op("pe", MM(bk[:, fc * 6:fc * 6 + 6], hslot[s][:, kc, fc * 128:(fc + 1) * 128], sTbf[:, kc, :],
                                   start=(kc == 0), stop=(kc == 7)),
                          reads=[r_hs[s][fc // 2], r_sT], writes=[rb])
            fw.op("dve", TT(modT[:, j * 8:(j + 1) * 8, :], bk[:, 0:48].rearrange("p (a b) -> p a b", b=6),
                            vecT[:, :, 3 + j:4 + j].to_broadcast([128, 8, 6]), ALU.add),
                  reads=[rb, r_vecT], writes=[r_modT])
        fw.op("dve", TSC(Amod[:], modT[:, 8:16, :], 1.0, None, ALU.add), reads=[r_modT], writes=[r_A])
        fw.op("dve", TT(Amod[:], Amod[:], vecT[:, :, 0:1].to_broadcast([128, 8, 6]), ALU.mult), reads=[r_A, r_vecT], writes=[r_A])
        for j in range(4):
            s = j % 2
            fw.op("sp", DMA(xs[s][0:15, 0:D], sp_d[j]), writes=[r_xs[s]], dma_key="xs%d" % s)
            for fc in range(8):
                fw.op("pe", MM(banks[6][:, (fc * 4 + j) * 15:(fc * 4 + j) * 15 + 15], xs[s][0:15, fc * 128:(fc + 1) * 128],
                               identf[0:15, 0:15]), reads=[r_xs[s], r_identf], writes=[r_bk[6]])
        fw.op("dve", CP(halos[:].rearrange("p a b c -> p (a b c)"), banks[6][:, 0:480]), reads=[r_bk[6]], writes=[r_halos])

        mbs = []
        for p in range(2):
            for half in range(2):
                mbs.append(dict(tok0=p * TP + half * TMB, ntok=TMB, prompt=True, first=(half == 0), last=(half == 1),
                                segs=[(p, 0, TMB)], L=128))
        mbs.append(dict(tok0=2 * TP, ntok=128, prompt=False, first=True, last=True,
                        segs=[(2 + j, j * 32, 32) for j in range(4)], L=32))
        if mb_limit is not None:
            mbs = [mbs[i] for i in mb_limit]

        PCOLS = [[(0, g * 256), (1, 1024 + g * 256)] for g in range(4)]
        HCOLS = [[(0, 2048 + 256 * h), (1, 5120 + 256 * h), (2, 6144 + 256 * h), (3, 3072 + 256 * h), (4, 4096 + 256 * h)]
                 for h in range(4)]

        r_scrP = [Res("scrP%d" % i) for i in range(6)]
        r_scrH = [Res("scrH%d" % i) for i in range(5)]
        scr_ok = set()
        pflat = [pslot[s_][:].rearrange("p a b -> p (a b)") for s_ in range(2)]
        hflat = [hslot[s_][:].rearrange("p a b -> p (a b)") for s_ in range(2)]

        def load_pool_w(g):
            s = g % 2
            if ("P", g) in scr_ok:
                fw.op("sp", DMA(pflat[s], scrP[g]), reads=[r_scrP[g]], writes=r_ps[s], dma_key="lp%d" % s)
                return
            for (b, c0) in PCOLS[g]:
                fw.op("pool", DMA(pslot[s][:, :, b * 256:(b + 1) * 256],
                                  win_d[:, c0:c0 + 256].rearrange("(kc p) n -> p kc n", p=128)),
                      writes=[r_ps[s][b]], dma_key="psl%d_%d" % (s, b))
            fw.op("sp", DMA(scrP[g], pflat[s]), reads=r_ps[s], writes=[r_scrP[g]], dma_key="sp%d" % g)
            scr_ok.add(("P", g))

        def load_head_w(h):
            s = h % 2
            if ("H", h) in scr_ok:
                fw.op("sp", DMA(hflat[s], scrH[h]), reads=[r_scrH[h]], writes=r_hs[s], dma_key="lh%d" % s)
                return
            for (b, c0) in HCOLS[h]:
                fw.op("pool", DMA(hslot[s][:, :, b * 256:(b + 1) * 256],
                                  win_d[:, c0:c0 + 256].rearrange("(kc p) n -> p kc n", p=128)),
                      writes=[r_hs[s][b]], dma_key="hs%d_%d" % (s, b))
            fw.op("sp", DMA(scrH[h], hflat[s]), reads=r_hs[s], writes=[r_scrH[h]], dma_key="sh%d" % h)
            scr_ok.add(("H", h))

        def load_wout_a():
            for s in range(2):
                if ("WA", s) in scr_ok:
                    fw.op("sp", DMA(pflat[s], scrP[4 + s]), reads=[r_scrP[4 + s]], writes=r_ps[s], dma_key="lp%d" % s)
                    continue
                for b in range(2):
                    fw.op("pool", DMA(pslot[s][:, :, b * 256:(b + 1) * 256],
                                      wout_d[s * 1024:(s + 1) * 1024, b * 256:(b + 1) * 256].rearrange("(kc p) n -> p kc n", p=128)),
                          writes=[r_ps[s][b]], dma_key="psl%d_%d" % (s, b))
                fw.op("sp", DMA(scrP[4 + s], pflat[s]), reads=r_ps[s], writes=[r_scrP[4 + s]], dma_key="sp%d" % (4 + s))
                scr_ok.add(("WA", s))

        def load_wout_b():
            if "WB" in scr_ok:
                fw.op("sp", DMA(hflat[0][:, 0:16 * 512], scrH[4][:, 0:16 * 512]), reads=[r_scrH[4]], writes=r_hs[0], dma_key="lh0")
                return
            wo1 = hflat[0][:, 0:16 * 512].rearrange("p (e c) -> p e c", c=512)
            fw.op("pool", DMA(wo1, wout_d[:, 512:1024].rearrange("(kc p) n -> p kc n", p=128)),
                  writes=r_hs[0], dma_key="hs0_0")
            fw.op("sp", DMA(scrH[4][:, 0:16 * 512], hflat[0][:, 0:16 * 512]), reads=r_hs[0], writes=[r_scrH[4]], dma_key="sh4")
            scr_ok.add("WB")

        def rstd_ops(ssl, r_ssl):
            fw.op("dve", TSC(ssl[:, 1:2], ssl[:, 0:1], 1.0 / D, EPS, ALU.mult, ALU.add), reads=[r_ssl], writes=[r_ssl])
            fw.op("pool", TT(ssl[:, 2:3], ssl[:, 1:2], mhalf[:, 0:1], ALU.pow), reads=[r_ssl, r_mhalf], writes=[r_ssl])

        def htiles(c0, n):
            return [r_hT[i] for i in range(c0 // 128, (c0 + n + 127) // 128)]

        def norm_phase(mb, solo=False):
            nt = mb["ntok"] // 128
            fw.op("sp", DMA(xsn[0][:], x_d[mb["tok0"]:mb["tok0"] + 128, :]), writes=[r_xsn[0]], dma_key="xsn0")
            for i in range(nt):
                s = i % 2
                if i + 1 < nt:
                    s2 = (i + 1) % 2
                    fw.op("sp", DMA(xsn[s2][:], x_d[mb["tok0"] + (i + 1) * 128:mb["tok0"] + (i + 2) * 128, :]),
                          writes=[r_xsn[s2]], dma_key="xsn%d" % s2)
                if solo:
                    fw.op("dve", lambda e, s=s: e.scalar_tensor_tensor(out=xhatn[s][:], in0=xsn[s][:], scalar=1.0, in1=xsn[s][:],
                                                                       op0=ALU.mult, op1=ALU.mult, accum_out=ssn[s][:, 0:1]),
                          reads=[r_xsn[s]], writes=[r_xhatn[s], r_ssn[s]])
                else:
                    fw.op("act", ACT(xhatn[s][:], xsn[s][:], AF.Square, accum_out=ssn[s][:, 0:1]), reads=[r_xsn[s]], writes=[r_xhatn[s], r_ssn[s]])
                fw.op("pool", TSC(ssn[s][:, 1:2], ssn[s][:, 0:1], 1.0 / D, EPS, ALU.mult, ALU.add), reads=[r_ssn[s]], writes=[r_ssn[s]])
                fw.op("pool", TT(ssn[s][:, 2:3], ssn[s][:, 1:2], mhalf[:, 0:1], ALU.pow), reads=[r_ssn[s], r_mhalf], writes=[r_ssn[s]])
                if solo:
                    fw.op("dve", TSC(xhatn[s][:], xsn[s][:], ssn[s][:, 2:3], None, ALU.mult), reads=[r_xsn[s], r_ssn[s]], writes=[r_xhatn[s]])
                else:
                    fw.op("act", ACT(xhatn[s][:], xsn[s][:], AF.Identity, scale=ssn[s][:, 2:3]), reads=[r_xsn[s], r_ssn[s]], writes=[r_xhatn[s]])
                bk, rb = nb()
                psb = bk[:].bitcast(BF16)
                for fc in range(8):
                    fw.op("pe", TR(psb[:, fc * 128:(fc + 1) * 128], xhatn[s][:, fc * 128:(fc + 1) * 128], identb[:]),
                          reads=[r_xhatn[s], r_identb], writes=[rb])
                for fc in range(8):
                    for (seq, c0, T) in mb["segs"]:
                        lo = max(c0, i * 128)
                        hi = min(c0 + T, (i + 1) * 128)
                        if hi <= lo:
                            continue
                        src = psb[:, fc * 128 + lo - i * 128:fc * 128 + hi - i * 128]
                        dst = hT[:, fc, lo:hi]
                        fw.op("act", ACT(dst, src, AF.Identity, scale=Amod[:, fc, seq:seq + 1], bias=modT[:, fc, seq:seq + 1]),
                              reads=[rb, r_A, r_modT], writes=[r_hT[i]])
                yield

        def gate_phase(mb):
            n = mb["ntok"]
            L = mb["L"]
            nch = n // L
            rb_ = yv[0][0:4, 0:n]; r_rb = r_yv[0]
            rsp = yo[0][0:4, 0:n]; r_rsp = r_yo[0]
            rGn = gate_bc[0:4, 0:n]; r_rGn = r_gbc
            rN = xs[0][0:4, 0:n]; r_rN = r_xs[0]
            N = min(512, n)
            for tg in range(n // N):
                cs = slice(tg * N, (tg + 1) * N)
                hr = htiles(tg * N, N)
                bk, rb = nb()
                for kc in range(8):
                    fw.op("pe", MM(bk[0:4, 0:N], wg[:, kc, 0:4], hT[:, kc, cs], start=(kc == 0), stop=(kc == 7)),
                          reads=[r_wg] + hr, writes=[rb])
                fw.op("act", ACT(rb_[:, cs], bk[0:4, 0:N], AF.Identity, bias=sm4[0:4, 0:1]), reads=[rb, r_bias], writes=[r_rb])
                bk, rb = nb()
                for kc in range(8):
                    fw.op("pe", MM(bk[0:4, 0:N], wg[:, kc, 4:8], hT[:, kc, cs], start=(kc == 0), stop=(kc == 7)),
                          reads=[r_wg] + hr, writes=[rb])
                fw.op("act", ACT(rsp[:, cs], bk[0:4, 0:N], AF.Exp, scale=-1.0, bias=sm4[0:4, 2:3]), reads=[rb, r_bias], writes=[r_rsp])
            fw.op("act", ACT(rsp, rsp, AF.Ln, bias=1.0), reads=[r_rsp], writes=[r_rsp])
            for (seq, c0, T) in mb["segs"]:
                init = 0.0 if mb["first"] else sm4[0:4, 3:4]
                fw.op("dve", SCAN(rGn[:, c0:c0 + T], rsp[:, c0:c0 + T], rsp[:, c0:c0 + T], init, ALU.add, ALU.max),
                      reads=[r_rsp, r_carry], writes=[r_rGn])
            fw.op("dve", TT(rb_, rb_, rGn, ALU.add), reads=[r_rb, r_rGn], writes=[r_rb])
            for si, (seq, c0, T) in enumerate(mb["segs"]):
                if mb["prompt"]:
                    init = 0.0 if mb["first"] else sm4[0:4, 4:5]
                else:
                    init = sm4[0:4, 8 + si:9 + si]
                fw.op("dve", SCAN(rN[:, c0:c0 + T], rb_[:, c0:c0 + T], rb_[:, c0:c0 + T], init, ALU.max, ALU.max),
                      reads=[r_rb, r_carry, r_m0T], writes=[r_rN])
            if mb["last"]:
                for (seq, c0, T) in mb["segs"]:
                    fw.op("dve", TT(sm4[0:4, 5:6], rN[:, c0 + T - 1:c0 + T], rGn[:, c0 + T - 1:c0 + T], ALU.subtract),
                          reads=[r_rGn, r_rN], writes=[r_mend])
                    outs.append(fw.op("sp", DMA(m_o[seq:seq + 1, :].rearrange("o h -> h o"), sm4[0:4, 5:6], allow_slow_non_contiguous=True),
                                      reads=[r_mend], dma_key="o_m"))
            v = lambda a: a.rearrange("p (c l) -> p c l", l=L)
            Rv = v(rN)[:, :, L - 1]
            Rp = sm4[0:4, 16:16 + nch]
            dLr = sm4[0:4, 24:24 + nch]
            if mb["prompt"]:
                if mb["first"]:
                    fw.op("dve", MSET(Rp[:, 0:1], 0.0), writes=[r_Rp])
                else:
                    fw.op("dve", CP(Rp[:, 0:1], sm4[0:4, 4:5]), reads=[r_carry], writes=[r_Rp])
                fw.op("dve", CP(Rp[:, 1:nch], Rv[:, 0:nch - 1]), reads=[r_rN], writes=[r_Rp])
            else:
                fw.op("dve", CP(Rp, sm4[0:4, 8:12]), reads=[r_m0T], writes=[r_Rp])
            if not mb["last"]:
                fw.op("dve", CP(sm4[0:4, 3:4], rGn[:, n - 1:n]), reads=[r_rGn], writes=[r_carry])
                fw.op("dve", CP(sm4[0:4, 4:5], rN[:, n - 1:n]), reads=[r_rN, r_Rp], writes=[r_carry])
            Rbc = v(rN)[:, :, L - 1:L].to_broadcast([4, nch, L])
            rwL = rsp
            rfl = rGn
            fw.op("dve", TT(v(rwL), v(rb_), Rbc, ALU.subtract), reads=[r_rb, r_rN], writes=[r_rsp])
            fw.op("dve", TT(v(rfl), v(rGn), Rbc, ALU.subtract), reads=[r_rGn, r_rN, r_carry, r_mend], writes=[r_rGn])
            fw.op("act", ACT(rwL, rwL, AF.Exp), reads=[r_rsp], writes=[r_rsp])
            fw.op("act", ACT(rfl, rfl, AF.Exp, scale=2.0), reads=[r_rGn], writes=[r_rGn])
            fw.op("dve", TT(dLr, Rp, Rv, ALU.subtract), reads=[r_Rp, r_rN], writes=[r_dLr])
            fw.op("act", ACT(dLr, dLr, AF.Exp), reads=[r_dLr], writes=[r_dLr])
            fw.op("dve", TT(Xd[0:4, :, 0:nch], dLr.unsqueeze(1).to_broadcast([4, 4, nch]),
                            identf[0:4, 0:4].unsqueeze(2).to_broadcast([4, 4, nch]), ALU.mult),
                  reads=[r_dLr, r_identf], writes=[r_Xd])
            for hh in range(4):
                fw.op("pe", MM(banks[4][:, 128 + hh * 8:128 + hh * 8 + nch], onesf[0:4, :], Xd[0:4, hh, 0:nch]),
                      reads=[r_ones, r_Xd], writes=[r_bk[4]])
            ntile = n // L
            for ti in range(ntile):
                fw.op("pe", MM(banks[4][0:L, ti * 4:ti * 4 + 4], rwL[:, ti * L:(ti + 1) * L], identf[0:4, 0:4]),
                      reads=[r_rsp, r_identf], writes=[r_bk[4]])
                fw.op("pe", MM(banks[4][0:L, 64 + ti * 4:64 + ti * 4 + 4], rfl[:, ti * L:(ti + 1) * L], identf[0:4, 0:4]),
                      reads=[r_rGn, r_identf], writes=[r_bk[4]])
            fw.op("dve", CP(dLbc[:].rearrange("p a b -> p (a b)"), banks[4][:, 128:160]), reads=[r_bk[4]], writes=[r_dLbc])
            fw.op("dve", CP(wLT[0:L, 0:ntile * 4], banks[4][0:L, 0:ntile * 4]), reads=[r_bk[4]], writes=[r_tms])
            fw.op("dve", TSC(wLT16[0:L, 0:ntile * 4], banks[4][0:L, 0:ntile * 4], 0.0625, None, ALU.mult), reads=[r_bk[4]], writes=[r_tms])
            fw.op("dve", CP(fl2T[0:L, 0:ntile * 4], banks[4][0:L, 64:64 + ntile * 4]), reads=[r_bk[4]], writes=[r_tms])

        def pgeom(mb):
            n = mb["ntok"]
            N = min(NT, n)
            if mb["prompt"]:
                return N, 1, N
            return N, 4, 32

        def pool_inproj(mb, g, tg, ntg):
            par = tg % 2
            s = g % 2
            W = pslot[s]
            N, nseg, T = pgeom(mb)
            E = [xpT[par][:, f, 0:nseg * (15 + T)].rearrange("p (s t) -> p s t", s=nseg) for f in range(2)]
            Eo = [xpT[1 - par][:, f, 0:nseg * (15 + T)].rearrange("p (s t) -> p s t", s=nseg) for f in range(2)]
            cs = slice(tg * N, (tg + 1) * N)
            hr = htiles(tg * N, N)
            for f in range(2):
                if mb["prompt"]:
                    if tg == 0:
                        if mb["first"]:
                            fw.op("dve", MSET(E[f][:, :, 0:15], 0.0), writes=[r_xpT[par]])
                        else:
                            fw.op("dve", CP(E[f][:, 0, 0:15], halo[:, 2 * g + f, :]), reads=[r_halo], writes=[r_xpT[par]])
                    else:
                        fw.op("dve", CP(E[f][:, 0, 0:15], Eo[f][:, 0, T:T + 15]), reads=[r_xpT[1 - par]], writes=[r_xpT[par]])
                else:
                    fw.op("dve", CP(E[f][:, :, 0:15], halos[:, 2 * g + f, :, :]), reads=[r_halos], writes=[r_xpT[par]])
            for f in range(2):
                bk, rb = nb()
                for kc in range(8):
                    fw.op("pe", MM(bk[:, 0:N], W[:, kc, f * 128:(f + 1) * 128], hT[:, kc, cs], start=(kc == 0), stop=(kc == 7)),
                          reads=[r_ps[s][0]] + hr, writes=[rb])
                fw.op("dve", CP(E[f][:, :, 15:15 + T], bk[:, 0:N].rearrange("p (s t) -> p s t", s=nseg)), reads=[rb], writes=[r_xpT[par]])
                yield
            for f in range(2):
                bk, rb = nb()
                for kc in range(8):
                    fw.op("pe", MM(bk[:, 0:N], W[:, kc, 256 + f * 128:256 + (f + 1) * 128], hT[:, kc, cs], start=(kc == 0), stop=(kc == 7)),
                          reads=[r_ps[s][1]] + hr, writes=[rb])
                fw.op("act", ACT(szp[par][:, f, 0:N], bk[:, 0:N], AF.Silu), reads=[rb], writes=[r_szp[par]])
                yield

        pcount = [0]

        def pool_dep(mb, g, tg, ntg):
            par = tg % 2
            s = g % 2
            W = pslot[s]
            N, nseg, T = pgeom(mb)
            w = 2 ** (g + 1)
            E = [xpT[par][:, f, 0:nseg * (15 + T)].rearrange("p (s t) -> p s t", s=nseg) for f in range(2)]
            tAv = tA[:, 0:nseg * (15 + T)].rearrange("p (s t) -> p s t", s=nseg)
            tBv = tB[:, 0:nseg * (15 + T)].rearrange("p (s t) -> p s t", s=nseg)
            cs = slice(tg * N, (tg + 1) * N)
            Ltot = 15 + T
            flush(pend_hn)
            tCv = yo[0][:, 0:nseg * (15 + T)].rearrange("p (s t) -> p s t", s=nseg)
            tDv = yo[1][:, 0:nseg * (15 + T)].rearrange("p (s t) -> p s t", s=nseg)
            for f in range(2):
                cur, rcur = E[f], r_xpT[par]
                tmps = [(tAv, r_tA), (tBv, r_tB)] if f == 0 else [(tCv, r_yo[0]), (tDv, r_yo[1])]
                weng = "pool" if f == 0 else "dve"
                step = 1
                k = 0
                while step < w:
                    lo = 2 * step - 1
                    nxt, rn = tmps[k % 2]
                    fw.op(weng, TT(nxt[:, :, lo:Ltot], cur[:, :, lo:Ltot], cur[:, :, lo - step:Ltot - step], ALU.add),
                          reads=[rcur], writes=[rn])
                    cur, rcur = nxt, rn
                    step *= 2
                    k += 1
                oth, roth = tmps[k % 2]
                pv = pooledT[par][:, f, 0:N].rearrange("p (s t) -> p s t", s=nseg)
                fw.op("dve", STT(pv, cur[:, :, 15:Ltot], 1.0 / w, E[f][:, :, 15:Ltot], ALU.mult, ALU.subtract),
                      reads=[rcur, r_xpT[par]], writes=[r_pooled[par]])
                if mb["prompt"] and mb["first"] and tg == 0:
                    fw.op("dve", TT(oth[:, 0, 0:w - 1], cur[:, 0, 15:15 + w - 1], invc[:, 0:w - 1], ALU.mult),
                          reads=[rcur, r_invc], writes=[roth])
                    fw.op("dve", TT(pooledT[par][:, f, 0:w - 1], oth[:, 0, 0:w - 1], E[f][:, 0, 15:15 + w - 1], ALU.subtract),
                          reads=[roth, r_xpT[par]], writes=[r_pooled[par]])
                if mb["prompt"] and not mb["last"] and tg == ntg - 1:
                    fw.op("dve", CP(halo[:, 2 * g + f, :], E[f][:, 0, T:T + 15]), reads=[r_xpT[par]], writes=[r_halo])
                yield
            for dcl in range(2):
                if dcl == 0:
                    flush(pend_tr)
                bk, rb = nb()
                for ccl in range(2):
                    fw.op("pe", MM(bk[:, 0:N], wpool[:, g, ccl, dcl * 128:(dcl + 1) * 128], pooledT[par][:, ccl, 0:N],
                                   start=(ccl == 0), stop=(ccl == 1)), reads=[r_wpool, r_pooled[par]], writes=[rb])
                fw.op("dve", STT(ycatT[:, 2 * g + dcl, cs], bk[:, 0:N], vecT[:, 2 * g + dcl, 1:2], szp[par][:, dcl, 0:N], ALU.mult, ALU.mult),
                      reads=[rb, r_vecT, r_szp[par]], writes=[r_yc[i] for i in range(tg * N // 128, ((tg + 1) * N + 127) // 128)])
                yield
            if mb["last"] and tg == ntg - 1:
                for si, (seq, c0, Ts) in enumerate(mb["segs"]):
                    bk, rb = nb()
                    lc = slice(c0 + Ts - 15, c0 + Ts)
                    pp = pcount[0] % 2
                    pcount[0] += 1
                    for kc in range(8):
                        fw.op("pe", MM(bk[0:15, 0:256], hT[:, kc, lc], W[:, kc, 0:256], start=(kc == 0), stop=(kc == 7)),
                              reads=[r_ps[s][0]] + htiles(c0 + Ts - 15, 15), writes=[rb])
                    fw.op("act", ACT(pstage[pp][0:15, :], bk[0:15, 0:256], AF.Identity), reads=[rb], writes=[r_pstage[pp]])
                    outs.append(fw.op("sp", DMA(pool_o[seq, :, g * 256:(g + 1) * 256], pstage[pp][0:15, :]),
                                      reads=[r_pstage[pp]], dma_key="o_pool%d" % pp))
                    yield

        def head_inproj(mb, h, tg, ntg):
            par = tg % 2
            s = h % 2
            W = hslot[s]
            n = mb["ntok"]
            N = min(NT, n)
            cs = slice(tg * N, (tg + 1) * N)
            hr = htiles(tg * N, N)
            if tg == 0:
                if mb["prompt"]:
                    if mb["first"]:
                        fw.op("dve", MSET(Call[:, h, :, :], 0.0), writes=[r_C[h]])
                else:
                    for hl in ([0, 1] if h == 0 else ([h + 1] if h + 1 < 4 else [])):
                        for j in range(4):
                            rc = Cres(mb, hl, j)
                            extra = list(r_yc) if hl == 1 else []
                            for dc in range(2):
                                fw.op("sp", DMA(Cst(mb, hl, j, dc)[:, 0:256], sC_d[j, hl, dc * 128:(dc + 1) * 128, :]),
                                      writes=[rc] + extra, dma_key="ldC%d_%d" % (hl % 2, j))
                                fw.op("sp", DMA(Cst(mb, hl, j, dc)[:, 256:257], sn_d[j, hl, dc * 128:(dc + 1) * 128].rearrange("(p o) -> p o", o=1),
                                                allow_slow_non_contiguous=True),
                                      writes=[rc] + extra, dma_key="ldC%d_%d" % (hl % 2, j))
            specs = [(0, None, qT, r_qT), (3, "k", kT, r_kT), (1, "sig", so, r_so), (2, "silu", gm, r_gm)]
            for (blk, kind, dstT, rdst) in specs:
                for f in range(2):
                    bk, rb = nb()
                    for kc in range(8):
                        fw.op("pe", MM(bk[:, 0:N], W[:, kc, blk * 256 + f * 128:blk * 256 + (f + 1) * 128], hT[:, kc, cs],
                                       start=(kc == 0), stop=(kc == 7)), reads=[r_hs[s][blk]] + hr, writes=[rb])
                    dst = dstT[par][:, f, 0:N]
                    if kind is None:
                        fw.op("dve", CP(dst, bk[:, 0:N]), reads=[rb], writes=[rdst[par]])
                    elif kind == "k":
                        fw.op("act", ACT(dst, bk[:, 0:N], AF.Identity, scale=0.0625), reads=[rb], writes=[rdst[par]])
                    elif kind == "sig":
                        fw.op("act", ACT(dst, bk[:, 0:N], AF.Sigmoid), reads=[rb], writes=[rdst[par]])
                    else:
                        fw.op("act", ACT(dst, bk[:, 0:N], AF.Silu), reads=[rb], writes=[rdst[par]])
                    yield
            fw.op("pool", TT(gm[par][:, :, 0:N], gm[par][:, :, 0:N], so[par][:, :, 0:N], ALU.mult), reads=[r_gm[par], r_so[par]], writes=[r_gm[par]])

        ccount = [0]
        r_C2 = [R("C2_%d" % i) for i in range(4)]

        def Cst(mb, h, cidx, dc):
            if mb["prompt"] or h % 2 == 0:
                return Call[:, cidx, dc, :]
            return ycatT[:, cidx * 2 + dc, 128:128 + 514].bitcast(F32)

        def Cres(mb, h, cidx):
            if mb["prompt"] or h % 2 == 0:
                return r_C[cidx]
            return r_C2[cidx]

        pend_tr = []
        pend_hn = []

        def flush(lst):
            while lst:
                lst.pop(0)()


        def head_dep(mb, h, tg, ntg):
            par = tg % 2
            s = h % 2
            W = hslot[s]
            n = mb["ntok"]
            N = min(NT, n)
            L = mb["L"]
            npt = N // L
            ntile = n // L
            cbase = ccount[0]
            ccount[0] += npt
            M = L

            def kv_ops(j):
                ti = tg * npt + j
                c0 = ti * L
                cc = (cbase + j) % 2
                cols = slice(c0, c0 + M)
                sc16 = wLT16[0:M, ti * 4 + h:ti * 4 + h + 1]
                bk, rb = nb()
                psb = bk[:].bitcast(BF16)
                lcj = slice(j * L, (j + 1) * L)
                sc1 = wLT[0:M, ti * 4 + h:ti * 4 + h + 1]
                for kc in range(8):
                    fw.op("pe", MM(bk[0:M, 0:256], hT[:, kc, cols], W[:, kc, 1024:1280], start=(kc == 0), stop=(kc == 7)),
                          reads=[r_hs[s][4]] + htiles(c0, M), writes=[rb])
                for dc in range(2):
                    fw.op("pe", TR(psb[0:M, 512 + dc * 128:512 + (dc + 1) * 128], kT[par][:, dc, lcj], identb[:]),
                          reads=[r_kT[par], r_identb], writes=[rb])
                fw.op("dve", TSC(kw[cc][0:M, :], psb[0:M, 512:768], sc1, None, ALU.mult), reads=[rb, r_tms], writes=[r_kw[cc]])
                fw.op("act", ACT(vaug[cc][0:M, 0:256], bk[0:M, 0:256], AF.Identity), reads=[rb], writes=[r_va[cc]])

            kv_ops(0)
            for j in range(npt):
                ti = tg * npt + j
                c0 = ti * L
                cidx = h if mb["prompt"] else ti
                cc = (cbase + j) % 2
                cols = slice(c0, c0 + M)
                lc = slice(j * L, (j + 1) * L)
                sc = wLT[0:M, ti * 4 + h:ti * 4 + h + 1]
                fl2 = fl2T[0:M, ti * 4 + h:ti * 4 + h + 1]
                dl = dLbc[:, h, ti:ti + 1]
                for dc in range(2):
                    fw.op("pe", MM(banks[4][0:M, 0:M], kT[par][:, dc, lc], qT[par][:, dc, lc], start=(dc == 0), stop=(dc == 1)),
                          reads=[r_kT[par], r_qT[par]], writes=[r_bk[4]])
                fw.op("dve", STT(PT[cc][0:M, 0:M], banks[4][0:M, 0:M], sc, maskT[0:M, 0:M], ALU.mult, ALU.mult),
                      reads=[r_bk[4], r_tms, r_mask], writes=[r_PT[cc]])
                flush(pend_hn)
                rc = Cres(mb, h, cidx)
                for dc in range(2):
                    fw.op("act", ACT(Cbf[:, dc, :], Cst(mb, h, cidx, dc), AF.Identity, scale=dl), reads=[rc, r_dLbc], writes=[r_Cbf])
                if j + 1 < npt:
                    kv_ops(j + 1)
                fw.op("pe", MM(banks[5][0:M, 0:257], PT[cc][0:M, 0:M], vaug[cc][0:M, :], start=True, stop=False),
                      reads=[r_PT[cc], r_va[cc]], writes=[r_bk[5]])
                for dc in range(2):
                    fw.op("pe", MM(banks[5][0:M, 0:257], qT[par][:, dc, lc], Cbf[:, dc, :], start=False, stop=(dc == 1)),
                          reads=[r_qT[par], r_Cbf], writes=[r_bk[5]])
                for dc in range(2):
                    fw.op("pe", MM(banks[6 + dc][:, 0:257], kw[cc][0:M, dc * 128:(dc + 1) * 128], vaug[cc][0:M, :]),
                          reads=[r_kw[cc], r_va[cc]], writes=[r_bk[6 + dc]])
                flush(pend_tr)
                for dc in range(2):
                    fw.op("dve", STT(Cst(mb, h, cidx, dc), Cst(mb, h, cidx, dc), dl, banks[6 + dc][:, 0:257], ALU.mult, ALU.add),
                          reads=[rc, r_dLbc, r_bk[6 + dc]], writes=[rc])
                t = stt[cc]
                rt = r_stt[cc]
                fw.op("dve", lambda e, t=t, M=M: e.bn_stats(out=t[0:M, 0:6], in_=banks[5][0:M, 0:256]), reads=[r_bk[5]], writes=[rt])
                fw.op("dve", lambda e, t=t, M=M: e.bn_aggr(out=t[0:M, 6:8], in_=t[0:M, 0:6]), reads=[rt], writes=[rt])
                fw.op("dve", CP(t[0:M, 8:9], banks[5][0:M, 256:257]), reads=[r_bk[5]], writes=[rt])
                fw.op("dve", TT(t[0:M, 9:10], t[0:M, 8:9], t[0:M, 8:9], ALU.mult), reads=[rt], writes=[rt])
                fw.op("dve", TT(t[0:M, 10:11], t[0:M, 9:10], fl2, ALU.max), reads=[rt, r_tms], writes=[rt])
                fw.op("dve", STT(t[0:M, 11:12], t[0:M, 10:11], EPS, t[0:M, 7:8], ALU.mult, ALU.add), reads=[rt], writes=[rt])
                fw.op("pool", TT(t[0:M, 12:13], t[0:M, 11:12], mhalf[0:M, 0:1], ALU.pow), reads=[rt, r_mhalf], writes=[rt])
                def emit_hn(cc=cc, M=M, t=t, rt=rt):
                    fw.op("dve", TSC(hn[cc][0:M, :], banks[5][0:M, 0:256], t[0:M, 6:7], t[0:M, 12:13], ALU.subtract, ALU.mult),
                          reads=[r_bk[5], rt], writes=[r_hn[cc]])
                pend_hn.append(emit_hn)
                def emit_tr(cc=cc, M=M, cols=cols, lc=lc, c0=c0, h=h, par=par):
                    bk, rb = nb()
                    psb = bk[:].bitcast(BF16)
                    for dcl in range(2):
                        fw.op("pe", TR(psb[:, dcl * 128:dcl * 128 + M], hn[cc][0:M, dcl * 128:(dcl + 1) * 128], identb[0:M, 0:M]),
                              reads=[r_hn[cc], r_identb], writes=[rb])
                    for dcl in range(2):
                        fw.op("dve", STT(ycatT[:, 8 + 2 * h + dcl, cols], psb[:, dcl * 128:dcl * 128 + M], vecT[:, 2 * h + dcl, 2:3],
                                         gm[par][:, dcl, lc], ALU.mult, ALU.mult),
                              reads=[rb, r_vecT, r_gm[par]], writes=[r_yc[i] for i in range(c0 // 128, (c0 + M + 127) // 128)])
                pend_tr.append(emit_tr)
                if mb["last"] and (not mb["prompt"] or ti == ntile - 1):
                    seq = mb["segs"][0][0] if mb["prompt"] else mb["segs"][ti][0]
                    for dc in range(2):
                        outs.append(fw.op("sp", DMA(C_o[seq, h, dc * 128:(dc + 1) * 128, :], Cst(mb, h, cidx, dc)[:, 0:256]),
                                          reads=[rc], dma_key="o_C%d_%d" % (h % 2 if not mb["prompt"] else 0, cidx)))
                        outs.append(fw.op("sp", DMA(n_o[seq, h, dc * 128:(dc + 1) * 128].rearrange("(p o) -> p o", o=1), Cst(mb, h, cidx, dc)[:, 256:257],
                                                    allow_slow_non_contiguous=True),
                                          reads=[rc], dma_key="o_C%d_%d" % (h % 2 if not mb["prompt"] else 0, cidx)))
                yield

        def gatebc_phase(mb, halves=(0, 1)):
            k = 0
            for half in halves:
                bk, rb = nb()
                for fcl in range(4):
                    fc = half * 4 + fcl
                    for si, (seq, c0, T) in enumerate(mb["segs"]):
                        d = k % 2
                        k += 1
                        fw.op("dve", TSC(dg[d][:], identf[:], modT[:, 16 + fc, seq:seq + 1], None, ALU.mult),
                              reads=[r_identf, r_modT], writes=[r_dg[d]])
                        lhs = onesf[:, :] if mb["prompt"] else smask[:, si, :]
                        fw.op("pe", MM(bk[:, fcl * 128:(fcl + 1) * 128], lhs, dg[d][:], start=(si == 0), stop=(si == len(mb["segs"]) - 1)),
                              reads=[r_ones, r_smask, r_dg[d]], writes=[rb])
                fw.op("dve", CP(gate_bc[:, half * 512:(half + 1) * 512], bk[:, 0:512]), reads=[rb], writes=[r_gbc])

        def final_phase(mb):
            n = mb["ntok"]
            nt = n // 128
            wo1 = hslot[0][:].rearrange("p a b -> p (a b)")[:, 0:16 * 512].rearrange("p (e c) -> p e c", c=512)
            prev_tail = [None]
            fw.op("sp", DMA(xs[0][:, 0:D], x_d[mb["tok0"]:mb["tok0"] + 128, :]), writes=[r_xs[0]], dma_key="xs0")
            for i in range(nt):
                s = i % 2
                if i + 1 < nt:
                    s2 = (i + 1) % 2
                    fw.op("sp", DMA(xs[s2][:, 0:D], x_d[mb["tok0"] + (i + 1) * 128:mb["tok0"] + (i + 2) * 128, :]),
                          writes=[r_xs[s2]], dma_key="xs%d" % s2)
                if prev_tail[0] is not None:
                    prev_tail[0]()
                    prev_tail[0] = None
                for nh in range(2):
                    bk, rb = nb()
                    for ec in range(16):
                        if nh == 0:
                            rhs = pslot[ec // 8][:, ec % 8, :]
                            rr = r_ps[ec // 8]
                        else:
                            rhs = wo1[:, ec, :]
                            rr = r_hs[0]
                        fw.op("pe", MM(bk[:, 0:512], ycatT[:, ec, i * 128:(i + 1) * 128], rhs, start=(ec == 0), stop=(ec == 15)),
                              reads=[r_yc[i]] + rr, writes=[rb])
                    fw.op("dve", TT(yv[s][:, nh * 512:(nh + 1) * 512], bk[:, 0:512], gate_bc[:, nh * 512:(nh + 1) * 512], ALU.mult),
                          reads=[rb, r_gbc], writes=[r_yv[s]])
                fw.op("dve", TT(yv[s][:], yv[s][:], xs[s][:, 0:D], ALU.add), reads=[r_yv[s], r_xs[s]], writes=[r_yv[s]])
                fw.op("dve", lambda e, s=s: e.scalar_tensor_tensor(out=yo[s][:], in0=yv[s][:], scalar=1.0, in1=yv[s][:], op0=ALU.mult, op1=ALU.mult,
                                                                   accum_out=ss[s][:, 0:1]),
                      reads=[r_yv[s]], writes=[r_yo[s], r_ss[s]])

                def tail(s=s, i=i):
                    rstd_ops(ss[s], r_ss[s])
                    fw.op("dve", STT(yo[s][:], yv[s][:], ss[s][:, 2:3], gfin[:], ALU.mult, ALU.mult), reads=[r_yv[s], r_ss[s], r_gfin], writes=[r_yo[s]])
                    outs.append(fw.op("sp", DMA(y_o[mb["tok0"] + i * 128:mb["tok0"] + (i + 1) * 128, :], yo[s][:]), reads=[r_yo[s]], dma_key="o_y%d" % s))
                prev_tail[0] = tail
                yield
            if prev_tail[0] is not None:
                prev_tail[0]()
                prev_tail[0] = None

        def run_all(g):
            if g is not None:
                for _ in g:
                    pass

        def interleave(a, b, ra=1, rb_=1):
            gens = [a, b]
            rates = [ra, rb_]
            alive = [a is not None, b is not None]
            while any(alive):
                for gi in range(2):
                    if not alive[gi]:
                        continue
                    for _ in range(rates[gi]):
                        try:
                            next(gens[gi])
                        except StopIteration:
                            alive[gi] = False
                            break

        INP = {"P": pool_inproj, "H": head_inproj}
        DEP = {"P": pool_dep, "H": head_dep}
        prev_final = None
        for mi, mb in enumerate(mbs):
            ntg = max(1, mb["ntok"] // NT)
            load_head_w(1)
            interleave(prev_final, norm_phase(mb, solo=(prev_final is None)))
            prev_final = None
            load_pool_w(0)
            load_pool_w(1)
            load_head_w(0)
            if stop < 1:
                continue
            gate_phase(mb)
            if stop < 2:
                continue
            steps = []
            for i in range(4):
                for tg in range(ntg):
                    steps.append(("P", i, tg))
                for tg in range(ntg):
                    steps.append(("H", i, tg))
            steps = [st_ for st_ in steps if stop >= 3 + (st_[1] * 2 + (1 if st_[0] == "H" else 0))]
            if steps:
                k0, i0, t0_ = steps[0]
                run_all(INP[k0](mb, i0, t0_, ntg))
            for si_, (kind, i, tg) in enumerate(steps):
                nxt = steps[si_ + 1] if si_ + 1 < len(steps) else None
                gen_dep = DEP[kind](mb, i, tg, ntg)
                gen_in = INP[nxt[0]](mb, nxt[1], nxt[2], ntg) if nxt is not None else None
                interleave(gen_dep, gen_in, 1, 2)
                if kind == "H" and i in (0, 1) and tg == ntg - 1:
                    gatebc_phase(mb, halves=(i,))
                if tg == ntg - 1:
                    if kind == "P":
                        if i + 2 < 4:
                            load_pool_w(i + 2)
                        elif i == 3 and stop >= 11:
                            load_wout_a()
                    else:
                        if i + 2 < 4:
                            load_head_w(i + 2)
                        elif i == 2 and stop >= 11:
                            load_wout_b()
            flush(pend_hn)
            flush(pend_tr)
            if stop < 11:
                continue
            prev_final = final_phase(mb)
        run_all(prev_final)

        if dbg:
            def dump(name, ap, shape, res):
                d = dout(name, shape)
                dbg_o[name] = shape
                outs.append(fw.op("pool", DMA(d, ap), reads=res, dma_key="dbg_" + name))
            dump("d_hT", hT[:], [128, 8, TMB], r_hT)
            dump("d_ycatT", ycatT[:], [128, 16, TMB], r_yc)
            dump("d_modT", modT[:], [128, 24, 6], [r_modT])
            dump("d_wLT", wLT[:], [128, 32], [r_tms])
            dump("d_fl2T", fl2T[:], [128, 32], [r_tms])
            dump("d_dLbc", dLbc[:], [128, 4, 8], [r_dLbc])
            dump("d_gbc", gate_bc[:], [128, D], [r_gbc])
            dump("d_pooledT", pooledT[0][:], [128, 2, NT], [r_pooled[0]])
            dump("d_gm", gm[0][:], [128, 2, NT], [r_gm[0]])

        fw.emit(final_wait_ops=outs)
    return nc, dbg_o


_CACHE = {}


def _consts():
    ident = np.eye(128, dtype=np.float32)
    s = np.arange(128)
    maskT = (s[:, None] <= s[None, :]).astype(np.float32)
    invc = np.tile((1.0 / np.arange(1, 17, dtype=np.float32))[None, :], (128, 1)).astype(np.float32)
    return ident, maskT, invc


def make_in_maps(x_prompt, x_sample, c_prompt, c_sample, state_pool, state_C, state_n, state_m,
                 w_ada, b_ada, g_norm, w_in, b_i, b_f, w_pool, pool_scale, g_head, w_out, g_final):
    f = lambda a: np.ascontiguousarray(np.asarray(a, dtype=np.float32))
    ident, maskT, invc = _consts()
    vecs = f(np.stack([np.asarray(g_norm)[0], np.asarray(pool_scale)[0], np.asarray(g_head)[0],
                       np.asarray(b_ada)[0, 0:D], np.asarray(b_ada)[0, D:2 * D], np.asarray(b_ada)[0, 2 * D:3 * D]], axis=0))
    shared = {
        "w_ada": f(np.asarray(w_ada)[0]), "vecs": vecs, "w_in": f(np.asarray(w_in)[0]),
        "b_i": f(np.asarray(b_i)[0].reshape(4, 1)), "b_f": f(np.asarray(b_f)[0].reshape(4, 1)),
        "w_pool": f(np.asarray(w_pool)[0]), "w_out": f(np.asarray(w_out)[0]), "g_final": f(np.asarray(g_final).reshape(1, D)),
        "ident": ident, "maskT": maskT, "invcnt": invc,
    }
    xp = np.asarray(x_prompt); xsm = np.asarray(x_sample)
    in_maps = []
    for c in range(NCORES):
        m = dict(shared)
        m["x"] = f(np.concatenate([xp[2 * c].reshape(TP, D), xp[2 * c + 1].reshape(TP, D), xsm[4 * c:4 * c + 4].reshape(4 * TS, D)], axis=0))
        m["c"] = f(np.concatenate([np.asarray(c_prompt)[2 * c:2 * c + 2], np.asarray(c_sample)[4 * c:4 * c + 4]], axis=0))
        m["st_pool"] = f(np.asarray(state_pool)[0, 4 * c:4 * c + 4])
        m["st_C"] = f(np.asarray(state_C)[0, 4 * c:4 * c + 4])
        m["st_n"] = f(np.asarray(state_n)[0, 4 * c:4 * c + 4])
        m["st_mT"] = f(np.asarray(state_m)[0, 4 * c:4 * c + 4].T)
        in_maps.append(m)
    return in_maps


def kernel(**inputs):
    if "nc" not in _CACHE:
        _CACHE["nc"] = build()[0]
    nc = _CACHE["nc"]
    in_maps = make_in_maps(**inputs)
    res = run_bass_kernel_spmd(nc, in_maps, core_ids=list(range(NCORES)))
    R_ = res.results
    y_prompt = np.zeros((16, TP, D), np.float32)
    y_sample = np.zeros((32, TS, D), np.float32)
    pp = np.zeros((1, 16, 15, D), np.float32); pc = np.zeros((1, 16, 4, 256, 256), np.float32)
    pn = np.zeros((1, 16, 4, 256), np.float32); pm = np.zeros((1, 16, 4), np.float32)
    sp_ = np.zeros((1, 32, 15, D), np.float32); sc = np.zeros((1, 32, 4, 256, 256), np.float32)
    sn = np.zeros((1, 32, 4, 256), np.float32); sm = np.zeros((1, 32, 4), np.float32)
    for c in range(NCORES):
        r = R_[c]
        y = r["y"]
        y_prompt[2 * c] = y[0:TP]
        y_prompt[2 * c + 1] = y[TP:2 * TP]
        y_sample[4 * c:4 * c + 4] = y[2 * TP:].reshape(4, TS, D)
        pp[0, 2 * c:2 * c + 2] = r["pool_o"][0:2]; sp_[0, 4 * c:4 * c + 4] = r["pool_o"][2:6]
        pc[0, 2 * c:2 * c + 2] = r["C_o"][0:2]; sc[0, 4 * c:4 * c + 4] = r["C_o"][2:6]
        pn[0, 2 * c:2 * c + 2] = r["n_o"][0:2]; sn[0, 4 * c:4 * c + 4] = r["n_o"][2:6]
        pm[0, 2 * c:2 * c + 2] = r["m_o"][0:2]; sm[0, 4 * c:4 * c + 4] = r["m_o"][2:6]
    return (y_prompt, y_sample, pp, pc, pn, pm, sp_, sc, sn, sm)
```

```python
import contextlib
import os
import numpy as np
SUB = int(os.environ.get('SUB', '99'))
import concourse.bass as bass
import concourse.mybir as mybir
from concourse.bass_utils import run_bass_kernel_spmd

F32 = mybir.dt.float32
BF16 = mybir.dt.bfloat16
AF = mybir.ActivationFunctionType
ALU = mybir.AluOpType

D = 1024
DIN = 7176
TP = 2048
TS = 32
NTOK = 2 * TP + 4 * TS
TMB = 1024
EPS = 1e-6
NCORES = 8


class Res:
    __slots__ = ("name", "w", "rc", "rd", "excl")

    def __init__(self, name, excl=False):
        self.name = name
        self.excl = excl
        self.w = None
        self.rc = {}
        self.rd = []


class Op:
    __slots__ = ("eng", "fn", "deps", "signal", "count", "dma_sem", "idx")


class FW:
    ENGS = ("pe", "act", "dve", "pool", "sp")

    def __init__(self, nc):
        self.nc = nc
        self.ops = {e: [] for e in self.ENGS}
        self.dma_keys = {}
        self.n = 0

    def op(self, eng, fn, reads=(), writes=(), dma_key=None):
        o = Op()
        o.eng, o.fn, o.signal, o.count, o.dma_sem, o.idx = eng, fn, False, None, None, self.n
        self.n += 1
        deps = []
        writes = list(writes) + [r for r in reads if r.excl]
        reads = [r for r in reads if not r.excl]
        for r in reads:
            if r.w is not None:
                deps.append(r.w)
        for r in writes:
            if r.w is not None:
                pw = r.w
                if not (dma_key is not None and pw.dma_sem is not None and pw.dma_sem[0] == dma_key and pw.eng == eng):
                    deps.append(pw)
            deps.extend(r.rc.values())
            deps.extend(r.rd)
        best = {}
        ded = []
        for d in deps:
            if d.dma_sem is not None:
                ded.append(d)
                continue
            if d.eng == eng and eng in ("pe", "sp"):
                continue
            b = best.get(d.eng)
            if b is None or d.idx > b.idx:
                best[d.eng] = d
        ded.extend(best.values())
        o.deps = ded
        for d in ded:
            d.signal = True
        if dma_key is not None:
            ent = self.dma_keys.setdefault(dma_key, [len(self.dma_keys), 0])
            ent[1] += 16
            o.dma_sem = (dma_key, ent[1])
        for r in reads:
            if dma_key is not None:
                r.rd.append(o)
            else:
                r.rc[eng] = o
        for r in writes:
            r.w = o
            r.rc = {}
            r.rd = []
        self.ops[eng].append(o)
        return o

    def emit(self, final_wait_ops=()):
        nc = self.nc
        for e in self.ENGS:
            c = 0
            for o in self.ops[e]:
                if o.dma_sem is None and o.signal:
                    c += 1
                    o.count = c
        with contextlib.ExitStack() as st:
            esem = {e: st.enter_context(nc.semaphore("s_" + e)) for e in self.ENGS}
            dsem = {k: st.enter_context(nc.semaphore("d_%d" % v[0])) for k, v in self.dma_keys.items()}
            block = st.enter_context(nc.Block())

            def tok(o):
                if o.dma_sem is not None:
                    return dsem[o.dma_sem[0]], o.dma_sem[1]
                return esem[o.eng], o.count

            def run(e, handle):
                seen = {}

                def wait(o):
                    s, v = tok(o)
                    if seen.get(id(s), 0) >= v:
                        return
                    seen[id(s)] = v
                    handle.wait_ge(s, v)

                for o in self.ops[e]:
                    mx = {}
                    for d in o.deps:
                        s_, v_ = tok(d)
                        if v_ > mx.get(id(s_), (None, 0))[1]:
                            mx[id(s_)] = (s_, v_)
                    for s_, v_ in mx.values():
                        if seen.get(id(s_), 0) >= v_:
                            continue
                        seen[id(s_)] = v_
                        handle.wait_ge(s_, v_)
                    ins = o.fn(handle)
                    if o.dma_sem is not None:
                        ins.then_inc(dsem[o.dma_sem[0]], 16)
                    elif o.signal:
                        ins.then_inc(esem[e], 1)
                if e == "sp":
                    mx = {}
                    for o in final_wait_ops:
                        s_, v_ = tok(o)
                        if v_ > mx.get(id(s_), (None, 0))[1]:
                            mx[id(s_)] = (s_, v_)
                    for s_, v_ in mx.values():
                        if seen.get(id(s_), 0) < v_:
                            seen[id(s_)] = v_
                            handle.wait_ge(s_, v_)

            @block.tensor
            def _(h):
                run("pe", h)

            @block.scalar
            def _(h):
                run("act", h)

            @block.vector
            def _(h):
                run("dve", h)

            @block.gpsimd
            def _(h):
                run("pool", h)

            @block.sync
            def _(h):
                run("sp", h)


def MM(out, lhsT, rhs, start=True, stop=True):
    return lambda e: e.matmul(out, lhsT=lhsT, rhs=rhs, start=start, stop=stop)


def TR(out, in_, ident):
    return lambda e: e.transpose(out=out, in_=in_, identity=ident)


def ACT(out, in_, func, **kw):
    return lambda e: e.activation(out=out, in_=in_, func=func, **kw)


def TT(out, in0, in1, op):
    return lambda e: e.tensor_tensor(out=out, in0=in0, in1=in1, op=op)


def TSC(out, in0, s1, s2, op0, op1=None):
    if op1 is None:
        return lambda e: e.tensor_scalar(out=out, in0=in0, scalar1=s1, scalar2=None, op0=op0)
    return lambda e: e.tensor_scalar(out=out, in0=in0, scalar1=s1, scalar2=s2, op0=op0, op1=op1)


def STT(out, in0, scalar, in1, op0, op1):
    return lambda e: e.scalar_tensor_tensor(out=out, in0=in0, scalar=scalar, in1=in1, op0=op0, op1=op1)


def CP(out, in_):
    return lambda e: e.tensor_copy(out=out, in_=in_)


def MSET(ap, v):
    return lambda e: e.memset(ap, v)


def DMA(out, in_, **kw):
    return lambda e: e.dma_start(out=out, in_=in_, **kw)


def SCAN(out, d0, d1, init, op0, op1):
    return lambda e: e.tensor_tensor_scan(out=out, data0=d0, data1=d1, initial=init, op0=op0, op1=op1)


def build(mb_limit=None, dbg=False, stop=99):
    nc = bass.Bass("TRN2", target_bir_lowering=False)

    def din(name, shape):
        return nc.dram_tensor(name, list(shape), F32, kind="ExternalInput").ap()

    def dout(name, shape):
        return nc.dram_tensor(name, list(shape), F32, kind="ExternalOutput").ap()

    x_d = din("x", [NTOK, D])
    c_d = din("c", [6, D])
    sp_d = din("st_pool", [4, 15, D])
    sC_d = din("st_C", [4, 4, 256, 256])
    sn_d = din("st_n", [4, 4, 256])
    sm_d = din("st_mT", [4, 4])
    wada_d = din("w_ada", [D, 3 * D])
    vec_d = din("vecs", [6, D])
    win_d = din("w_in", [D, DIN])
    bi_d = din("b_i", [4, 1])
    bf_d = din("b_f", [4, 1])
    wpool_d = din("w_pool", [4, 256, 256])
    wout_d = din("w_out", [2 * D, D])
    gfin_d = din("g_final", [1, D])
    ident_d = din("ident", [128, 128])
    mask_d = din("maskT", [128, 128])
    invc_d = din("invcnt", [128, 16])
    y_o = dout("y", [NTOK, D])
    pool_o = dout("pool_o", [6, 15, D])
    C_o = dout("C_o", [6, 4, 256, 256])
    n_o = dout("n_o", [6, 4, 256])
    m_o = dout("m_o", [6, 4])
    dbg_o = {}
    scrP = nc.dram_tensor("scrP", [6, 128, 8 * 512], BF16).ap()
    scrH = nc.dram_tensor("scrH", [5, 128, 8 * 1280], BF16).ap()

    st = contextlib.ExitStack()
    with st:
        st.enter_context(nc.allow_low_precision("bf16 matmul operands, fp32 accumulation"))
        fw = FW(nc)

        def sb(name, shape, dt=F32):
            return st.enter_context(nc.sbuf_tensor("sb_" + name, list(shape), dt))

        def R(name):
            return Res(name)

        identf = sb("identf", [128, 128]); r_identf = R("identf")
        identb = sb("identb", [128, 128], BF16); r_identb = R("identb")
        maskT = sb("maskT", [128, 128]); r_mask = R("mask")
        onesf = sb("onesf", [128, 128]); r_ones = R("ones")
        smask = sb("smask", [128, 4, 128]); r_smask = R("smask")
        mhalf = sb("mhalf", [128, 1]); r_mhalf = R("mhalf")
        vecT = sb("vecT", [128, 8, 6]); r_vecT = R("vecT")
        gfin = sb("gfin", [128, D]); r_gfin = R("gfin")
        invc = sb("invc", [128, 16]); r_invc = R("invc")
        sm4 = sb("sm4", [128, 64]); r_sm4 = R("sm4")
        r_bias = R("bias"); r_carry = R("carry"); r_m0T = R("m0T"); r_Rp = R("Rp"); r_dLr = R("dLr"); r_mend = R("mend")
        Xd = sb("Xd", [128, 4, 8]); r_Xd = R("Xd")
        modT = sb("modT", [128, 24, 6]); r_modT = R("modT")
        Amod = sb("Amod", [128, 8, 6]); r_A = R("A")
        sTbf = sb("sTbf", [128, 8, 6], BF16); r_sT = R("sT")
        NT = 512
        hT = sb("hT", [128, 8, TMB], BF16); r_hT = [R("hT%d" % i) for i in range(8)]
        ycatT = sb("ycatT", [128, 16, TMB], BF16); r_yc = [R("yc%d" % i) for i in range(8)]
        hslot = [sb("hslot%d" % s, [128, 8, 1280], BF16) for s in range(2)]
        r_hs = [[R("hs%d_%d" % (s, b)) for b in range(5)] for s in range(2)]
        pslot = [sb("pslot%d" % s, [128, 8, 512], BF16) for s in range(2)]
        r_ps = [[R("psl%d_%d" % (s, b)) for b in range(2)] for s in range(2)]
        wpool = sb("wpool", [128, 4, 2, 256], BF16); r_wpool = R("wpool")
        wg = sb("wg", [128, 8, 8], BF16); r_wg = R("wg")
        Call = sb("Call", [128, 4, 2, 257]); r_C = [R("C%d" % i) for i in range(4)]
        Cbf = sb("Cbf", [128, 2, 257], BF16); r_Cbf = R("Cbf")
        wLT = sb("wLT", [128, 32]); wLT16 = sb("wLT16", [128, 32]); fl2T = sb("fl2T", [128, 32]); r_tms = R("tms")
        dLbc = sb("dLbc", [128, 4, 8]); r_dLbc = R("dLbc")
        qT = [sb("qT%d" % p, [128, 2, NT], BF16) for p in range(2)]; r_qT = [R("qT%d" % p) for p in range(2)]
        kT = [sb("kT%d" % p, [128, 2, NT], BF16) for p in range(2)]; r_kT = [R("kT%d" % p) for p in range(2)]
        so = [sb("so%d" % p, [128, 2, NT], BF16) for p in range(2)]; r_so = [R("so%d" % p) for p in range(2)]
        gm = [sb("gm%d" % p, [128, 2, NT], BF16) for p in range(2)]; r_gm = [R("gm%d" % p) for p in range(2)]
        xpT = [sb("xpT%d" % p, [128, 2, 15 + NT]) for p in range(2)]; r_xpT = [R("xpT%d" % p) for p in range(2)]
        szp = [sb("szp%d" % p, [128, 2, NT], BF16) for p in range(2)]; r_szp = [R("szp%d" % p) for p in range(2)]
        pooledT = [sb("pooledT%d" % p, [128, 2, NT], BF16) for p in range(2)]; r_pooled = [R("pooled%d" % p) for p in range(2)]
        yv = [sb("yv%d" % p, [128, D]) for p in range(2)]; r_yv = [R("yv%d" % p) for p in range(2)]
        yo = [sb("yo%d" % p, [128, D]) for p in range(2)]; r_yo = [R("yo%d" % p) for p in range(2)]
        tA = yv[0]; r_tA = r_yv[0]
        tB = yv[1]; r_tB = r_yv[1]
        halo = sb("halo", [128, 8, 15]); r_halo = R("halo")
        halos = sb("halos", [128, 8, 4, 15]); r_halos = R("halos")
        pstage = [sb("pstage%d" % p, [128, 256]) for p in range(2)]; r_pstage = [R("pstage%d" % p) for p in range(2)]
        kw = [sb("kw%d" % s, [128, 256], BF16) for s in range(2)]; r_kw = [R("kw%d" % s) for s in range(2)]
        vaug = [sb("vaug%d" % s, [128, 257], BF16) for s in range(2)]; r_va = [R("va%d" % s) for s in range(2)]
        PT = [sb("PT%d" % s, [128, 128], BF16) for s in range(2)]; r_PT = [R("PT%d" % s) for s in range(2)]
        hn = [sb("hn%d" % s, [128, 256], BF16) for s in range(2)]; r_hn = [R("hn%d" % s) for s in range(2)]
        stt = [sb("stt%d" % s, [128, 16]) for s in range(2)]; r_stt = [R("stt%d" % s) for s in range(2)]
        xs = [xpT[p][:].rearrange("p a b -> p (a b)") for p in range(2)]; r_xs = r_xpT
        ss = [sb("ss%d" % s, [128, 4]) for s in range(2)]; r_ss = [R("ss%d" % s) for s in range(2)]
        xsn = [sb("xsn%d" % s, [128, D]) for s in range(2)]; r_xsn = [R("xsn%d" % s) for s in range(2)]
        xhatn = [sb("xhatn%d" % p, [128, D], BF16) for p in range(2)]; r_xhatn = [R("xhatn%d" % p) for p in range(2)]
        ssn = [sb("ssn%d" % s, [128, 4]) for s in range(2)]; r_ssn = [R("ssn%d" % s) for s in range(2)]
        gate_bc = sb("gate_bc", [128, D]); r_gbc = R("gbc")
        dg = [sb("dg%d" % s, [128, 128]) for s in range(2)]; r_dg = [R("dg%d" % s) for s in range(2)]
        banks = [st.enter_context(nc.psum_tensor("ps%d" % i, [128, 512], F32)) for i in range(8)]
        r_bk = [Res("bank%d" % i, excl=True) for i in range(8)]
        big_i = [0]

        def nb():
            i = big_i[0] % 4
            big_i[0] += 1
            return banks[i], r_bk[i]

        outs = []

        fw.op("sp", DMA(identf[:], ident_d), writes=[r_identf], dma_key="c_ident")
        fw.op("sp", DMA(maskT[:], mask_d), writes=[r_mask], dma_key="c_mask")
        fw.op("sp", DMA(invc[:], invc_d), writes=[r_invc], dma_key="c_invc")
        fw.op("sp", DMA(gfin[:], gfin_d.to_broadcast([128, D])), writes=[r_gfin], dma_key="c_gfin")
        fw.op("sp", DMA(sm4[0:4, 0:1], bi_d), writes=[r_bias], dma_key="c_bias")
        fw.op("sp", DMA(sm4[0:4, 1:2], bf_d), writes=[r_bias], dma_key="c_bias")
        fw.op("sp", DMA(sm4[0:4, 8:12], sm_d), writes=[r_m0T], dma_key="c_m0")
        fw.op("sp", DMA(xs[0][0:6, 0:D], c_d), writes=[r_xs[0]], dma_key="xs0")
        fw.op("sp", DMA(xs[1][0:6, 0:D], vec_d), writes=[r_xs[1]], dma_key="xs1")
        fw.op("pool", DMA(wg[:], win_d[:, 7168:7176].rearrange("(kc p) n -> p kc n", p=128)), writes=[r_wg], dma_key="c_wg")
        fw.op("pool", DMA(wpool[:].rearrange("p g c d -> p (g c) d"),
                          wpool_d.rearrange("g (c p) d -> p (g c) d", p=128)), writes=[r_wpool], dma_key="c_wpool")
        fw.op("dve", CP(identb[:], identf[:]), reads=[r_identf], writes=[r_identb])
        fw.op("dve", MSET(onesf[:], 1.0), writes=[r_ones])
        fw.op("dve", MSET(smask[:], 0.0), writes=[r_smask])
        for j in range(4):
            fw.op("dve", MSET(smask[:, j, j * 32:(j + 1) * 32], 1.0), writes=[r_smask])
        fw.op("dve", MSET(mhalf[:], -0.5), writes=[r_mhalf])
        fw.op("dve", TSC(sm4[0:4, 2:3], sm4[0:4, 1:2], -1.0, None, ALU.mult), reads=[r_bias], writes=[r_bias])
        for s in range(2):
            fw.op("dve", MSET(vaug[s][:, 256:257], 1.0), writes=[r_va[s]])
        fw.op("act", ACT(xs[0][0:6, 0:D], xs[0][0:6, 0:D], AF.Silu), reads=[r_xs[0]], writes=[r_xs[0]])
        for kc in range(8):
            fw.op("pe", MM(banks[4][:, kc * 6:kc * 6 + 6], xs[0][0:6, kc * 128:(kc + 1) * 128], identf[0:6, 0:6]),
                  reads=[r_xs[0], r_identf], writes=[r_bk[4]])
        fw.op("dve", CP(sTbf[:].rearrange("p a b -> p (a b)"), banks[4][:, 0:48]), reads=[r_bk[4]], writes=[r_sT])
        for kc in range(8):
            fw.op("pe", MM(banks[5][:, kc * 6:kc * 6 + 6], xs[1][0:6, kc * 128:(kc + 1) * 128], identf[0:6, 0:6]),
                  reads=[r_xs[1], r_identf], writes=[r_bk[5]])
        fw.op("dve", CP(vecT[:].rearrange("p a b -> p (a b)"), banks[5][:, 0:48]), reads=[r_bk[5]], writes=[r_vecT])
        for j in range(3):
            s = j % 2
            for b in range(4):
                fw.op("pool", DMA(hslot[s][:, :, b * 256:(b + 1) * 256],
                                  wada_d[:, j * 1024 + b * 256:j * 1024 + (b + 1) * 256].rearrange("(kc p) n -> p kc n", p=128)),
                      writes=[r_hs[s][b]], dma_key="hs%d_%d" % (s, b))
            bk, rb = nb()
            for fc in range(8):
                for kc in range(8):
                    fw.op("pe", MM(bk[:, fc * 6:fc * 6 + 6], hslot[s][:, kc, fc * 128:(fc + 1) * 128], sTbf[:, kc, :],
                                   start=(kc == 0), stop=(kc == 7)),
                          reads=[r_hs[s][fc // 2], r_sT], writes=[rb])
            fw.op("dve", TT(modT[:, j * 8:(j + 1) * 8, :], bk[:, 0:48].rearrange("p (a b) -> p a b", b=6),
                            vecT[:, :, 3 + j:4 + j].to_broadcast([128, 8, 6]), ALU.add),
                  reads=[rb, r_vecT], writes=[r_modT])
        fw.op("dve", TSC(Amod[:], modT[:, 8:16, :], 1.0, None, ALU.add), reads=[r_modT], writes=[r_A])
        fw.op("dve", TT(Amod[:], Amod[:], vecT[:, :, 0:1].to_broadcast([128, 8, 6]), ALU.mult), reads=[r_A, r_vecT], writes=[r_A])
        def halos_phase():
            for j in range(4):
                s = j % 2
                fw.op("sp", DMA(xsn[s][0:15, :], sp_d[j]), writes=[r_xsn[s]], dma_key="xsn%d" % s)
                for fc in range(8):
                    fw.op("pe", MM(banks[6][:, (fc * 4 + j) * 15:(fc * 4 + j) * 15 + 15], xsn[s][0:15, fc * 128:(fc + 1) * 128],
                                   identf[0:15, 0:15]), reads=[r_xsn[s], r_identf], writes=[r_bk[6]])
            fw.op("dve", CP(halos[:].rearrange("p a b c -> p (a b c)"), banks[6][:, 0:480]), reads=[r_bk[6]], writes=[r_halos])

        mbs = []
        for p in range(2):
            for half in range(2):
                mbs.append(dict(tok0=p * TP + half * TMB, ntok=TMB, prompt=True, first=(half == 0), last=(half == 1),
                                segs=[(p, 0, TMB)], L=128))
        mbs.append(dict(tok0=2 * TP, ntok=128, prompt=False, first=True, last=True,
                        segs=[(2 + j, j * 32, 32) for j in range(4)], L=32))
        if mb_limit is not None:
            mbs = [mbs[i] for i in mb_limit]

        PCOLS = [[(0, g * 256), (1, 1024 + g * 256)] for g in range(4)]
        HCOLS = [[(0, 2048 + 256 * h), (1, 5120 + 256 * h), (2, 6144 + 256 * h), (3, 3072 + 256 * h), (4, 4096 + 256 * h)]
                 for h in range(4)]

        r_scrP = [Res("scrP%d" % i) for i in range(6)]
        r_scrH = [Res("scrH%d" % i) for i in range(5)]
        scr_ok = set()
        pflat = [pslot[s_][:].rearrange("p a b -> p (a b)") for s_ in range(2)]
        hflat = [hslot[s_][:].rearrange("p a b -> p (a b)") for s_ in range(2)]

        def load_pool_w(g):
            s = g % 2
            if ("P", g) in scr_ok:
                fw.op("sp", DMA(pflat[s], scrP[g]), reads=[r_scrP[g]], writes=r_ps[s], dma_key="lp%d" % s)
                return
            for (b, c0) in PCOLS[g]:
                fw.op("pool", DMA(pslot[s][:, :, b * 256:(b + 1) * 256],
                                  win_d[:, c0:c0 + 256].rearrange("(kc p) n -> p kc n", p=128)),
                      writes=[r_ps[s][b]], dma_key="psl%d_%d" % (s, b))
            fw.op("sp", DMA(scrP[g], pflat[s]), reads=r_ps[s], writes=[r_scrP[g]], dma_key="sp%d" % g)
            scr_ok.add(("P", g))

        def load_head_w(h):
            s = h % 2
            if ("H", h) in scr_ok:
                fw.op("sp", DMA(hflat[s], scrH[h]), reads=[r_scrH[h]], writes=r_hs[s], dma_key="lh%d" % s)
                return
            for (b, c0) in HCOLS[h]:
                fw.op("pool", DMA(hslot[s][:, :, b * 256:(b + 1) * 256],
                                  win_d[:, c0:c0 + 256].rearrange("(kc p) n -> p kc n", p=128)),
                      writes=[r_hs[s][b]], dma_key="hs%d_%d" % (s, b))
            fw.op("sp", DMA(scrH[h], hflat[s]), reads=r_hs[s], writes=[r_scrH[h]], dma_key="sh%d" % h)
            scr_ok.add(("H", h))

        def load_wout_a():
            for s in range(2):
                if ("WA", s) in scr_ok:
                    fw.op("sp", DMA(pflat[s], scrP[4 + s]), reads=[r_scrP[4 + s]], writes=r_ps[s], dma_key="lp%d" % s)
                    continue
                for b in range(2):
                    fw.op("pool", DMA(pslot[s][:, :, b * 256:(b + 1) * 256],
                                      wout_d[s * 1024:(s + 1) * 1024, b * 256:(b + 1) * 256].rearrange("(kc p) n -> p kc n", p=128)),
                          writes=[r_ps[s][b]], dma_key="psl%d_%d" % (s, b))
                fw.op("sp", DMA(scrP[4 + s], pflat[s]), reads=r_ps[s], writes=[r_scrP[4 + s]], dma_key="sp%d" % (4 + s))
                scr_ok.add(("WA", s))

        def load_wout_b():
            if "WB" in scr_ok:
                fw.op("sp", DMA(hflat[0][:, 0:16 * 512], scrH[4][:, 0:16 * 512]), reads=[r_scrH[4]], writes=r_hs[0], dma_key="lh0")
                return
            wo1 = hflat[0][:, 0:16 * 512].rearrange("p (e c) -> p e c", c=512)
            fw.op("pool", DMA(wo1, wout_d[:, 512:1024].rearrange("(kc p) n -> p kc n", p=128)),
                  writes=r_hs[0], dma_key="hs0_0")
            fw.op("sp", DMA(scrH[4][:, 0:16 * 512], hflat[0][:, 0:16 * 512]), reads=r_hs[0], writes=[r_scrH[4]], dma_key="sh4")
            scr_ok.add("WB")

        def rstd_ops(ssl, r_ssl):
            fw.op("dve", TSC(ssl[:, 1:2], ssl[:, 0:1], 1.0 / D, EPS, ALU.mult, ALU.add), reads=[r_ssl], writes=[r_ssl])
            fw.op("pool", TT(ssl[:, 2:3], ssl[:, 1:2], mhalf[:, 0:1], ALU.pow), reads=[r_ssl, r_mhalf], writes=[r_ssl])

        def htiles(c0, n):
            return [r_hT[i] for i in range(c0 // 128, (c0 + n + 127) // 128)]

        def norm_phase(mb, solo=False):
            nt = mb["ntok"] // 128
            fw.op("sp", DMA(xsn[0][:], x_d[mb["tok0"]:mb["tok0"] + 128, :]), writes=[r_xsn[0]], dma_key="xsn0")
            for i in range(nt):
                s = i % 2
                if i + 1 < nt:
                    s2 = (i + 1) % 2
                    fw.op("sp", DMA(xsn[s2][:], x_d[mb["tok0"] + (i + 1) * 128:mb["tok0"] + (i + 2) * 128, :]),
                          writes=[r_xsn[s2]], dma_key="xsn%d" % s2)
                if solo:
                    fw.op("dve", lambda e, s=s: e.scalar_tensor_tensor(out=xhatn[s][:], in0=xsn[s][:], scalar=1.0, in1=xsn[s][:],
                                                                       op0=ALU.mult, op1=ALU.mult, accum_out=ssn[s][:, 0:1]),
                          reads=[r_xsn[s]], writes=[r_xhatn[s], r_ssn[s]])
                else:
                    fw.op("act", ACT(xhatn[s][:], xsn[s][:], AF.Square, accum_out=ssn[s][:, 0:1]), reads=[r_xsn[s]], writes=[r_xhatn[s], r_ssn[s]])
                fw.op("pool", TSC(ssn[s][:, 1:2], ssn[s][:, 0:1], 1.0 / D, EPS, ALU.mult, ALU.add), reads=[r_ssn[s]], writes=[r_ssn[s]])
                fw.op("pool", TT(ssn[s][:, 2:3], ssn[s][:, 1:2], mhalf[:, 0:1], ALU.pow), reads=[r_ssn[s], r_mhalf], writes=[r_ssn[s]])
                if solo:
                    fw.op("dve", TSC(xhatn[s][:], xsn[s][:], ssn[s][:, 2:3], None, ALU.mult), reads=[r_xsn[s], r_ssn[s]], writes=[r_xhatn[s]])
                else:
                    fw.op("act", ACT(xhatn[s][:], xsn[s][:], AF.Identity, scale=ssn[s][:, 2:3]), reads=[r_xsn[s], r_ssn[s]], writes=[r_xhatn[s]])
                bk, rb = nb()
                psb = bk[:].bitcast(BF16)
                for fc in range(8):
                    fw.op("pe", TR(psb[:, fc * 128:(fc + 1) * 128], xhatn[s][:, fc * 128:(fc + 1) * 128], identb[:]),
                          reads=[r_xhatn[s], r_identb], writes=[rb])
                for fc in range(8):
                    for (seq, c0, T) in mb["segs"]:
                        lo = max(c0, i * 128)
                        hi = min(c0 + T, (i + 1) * 128)
                        if hi <= lo:
                            continue
                        src = psb[:, fc * 128 + lo - i * 128:fc * 128 + hi - i * 128]
                        dst = hT[:, fc, lo:hi]
                        fw.op("act", ACT(dst, src, AF.Identity, scale=Amod[:, fc, seq:seq + 1], bias=modT[:, fc, seq:seq + 1]),
                              reads=[rb, r_A, r_modT], writes=[r_hT[i]])
                yield

        def gate_phase(mb):
            n = mb["ntok"]
            L = mb["L"]
            nch = n // L
            rb_ = yv[0][0:4, 0:n]; r_rb = r_yv[0]
            rsp = yo[0][0:4, 0:n]; r_rsp = r_yo[0]
            rGn = gate_bc[0:4, 0:n]; r_rGn = r_gbc
            rN = xs[0][0:4, 0:n]; r_rN = r_xs[0]
            N = min(512, n)
            for tg in range(n // N):
                cs = slice(tg * N, (tg + 1) * N)
                hr = htiles(tg * N, N)
                bk, rb = nb()
                for kc in range(8):
                    fw.op("pe", MM(bk[0:4, 0:N], wg[:, kc, 0:4], hT[:, kc, cs], start=(kc == 0), stop=(kc == 7)),
                          reads=[r_wg] + hr, writes=[rb])
                fw.op("act", ACT(rb_[:, cs], bk[0:4, 0:N], AF.Identity, bias=sm4[0:4, 0:1]), reads=[rb, r_bias], writes=[r_rb])
                bk, rb = nb()
                for kc in range(8):
                    fw.op("pe", MM(bk[0:4, 0:N], wg[:, kc, 4:8], hT[:, kc, cs], start=(kc == 0), stop=(kc == 7)),
                          reads=[r_wg] + hr, writes=[rb])
                fw.op("act", ACT(rsp[:, cs], bk[0:4, 0:N], AF.Exp, scale=-1.0, bias=sm4[0:4, 2:3]), reads=[rb, r_bias], writes=[r_rsp])
            fw.op("act", ACT(rsp, rsp, AF.Ln, bias=1.0), reads=[r_rsp], writes=[r_rsp])
            for (seq, c0, T) in mb["segs"]:
                init = 0.0 if mb["first"] else sm4[0:4, 3:4]
                fw.op("dve", SCAN(rGn[:, c0:c0 + T], rsp[:, c0:c0 + T], rsp[:, c0:c0 + T], init, ALU.add, ALU.max),
                      reads=[r_rsp, r_carry], writes=[r_rGn])
            fw.op("dve", TT(rb_, rb_, rGn, ALU.add), reads=[r_rb, r_rGn], writes=[r_rb])
            for si, (seq, c0, T) in enumerate(mb["segs"]):
                if mb["prompt"]:
                    init = 0.0 if mb["first"] else sm4[0:4, 4:5]
                else:
                    init = sm4[0:4, 8 + si:9 + si]
                fw.op("dve", SCAN(rN[:, c0:c0 + T], rb_[:, c0:c0 + T], rb_[:, c0:c0 + T], init, ALU.max, ALU.max),
                      reads=[r_rb, r_carry, r_m0T], writes=[r_rN])
            if mb["last"]:
                for (seq, c0, T) in mb["segs"]:
                    fw.op("dve", TT(sm4[0:4, 5:6], rN[:, c0 + T - 1:c0 + T], rGn[:, c0 + T - 1:c0 + T], ALU.subtract),
                          reads=[r_rGn, r_rN], writes=[r_mend])
                    outs.append(fw.op("sp", DMA(m_o[seq:seq + 1, :].rearrange("o h -> h o"), sm4[0:4, 5:6], allow_slow_non_contiguous=True),
                                      reads=[r_mend], dma_key="o_m"))
            v = lambda a: a.rearrange("p (c l) -> p c l", l=L)
            Rv = v(rN)[:, :, L - 1]
            Rp = sm4[0:4, 16:16 + nch]
            dLr = sm4[0:4, 24:24 + nch]
            if mb["prompt"]:
                if mb["first"]:
                    fw.op("dve", MSET(Rp[:, 0:1], 0.0), writes=[r_Rp])
                else:
                    fw.op("dve", CP(Rp[:, 0:1], sm4[0:4, 4:5]), reads=[r_carry], writes=[r_Rp])
                fw.op("dve", CP(Rp[:, 1:nch], Rv[:, 0:nch - 1]), reads=[r_rN], writes=[r_Rp])
            else:
                fw.op("dve", CP(Rp, sm4[0:4, 8:12]), reads=[r_m0T], writes=[r_Rp])
            if not mb["last"]:
                fw.op("dve", CP(sm4[0:4, 3:4], rGn[:, n - 1:n]), reads=[r_rGn], writes=[r_carry])
                fw.op("dve", CP(sm4[0:4, 4:5], rN[:, n - 1:n]), reads=[r_rN, r_Rp], writes=[r_carry])
            Rbc = v(rN)[:, :, L - 1:L].to_broadcast([4, nch, L])
            rwL = rsp
            rfl = rGn
            fw.op("dve", TT(v(rwL), v(rb_), Rbc, ALU.subtract), reads=[r_rb, r_rN], writes=[r_rsp])
            fw.op("dve", TT(v(rfl), v(rGn), Rbc, ALU.subtract), reads=[r_rGn, r_rN, r_carry, r_mend], writes=[r_rGn])
            fw.op("act", ACT(rwL, rwL, AF.Exp), reads=[r_rsp], writes=[r_rsp])
            fw.op("act", ACT(rfl, rfl, AF.Exp, scale=2.0), reads=[r_rGn], writes=[r_rGn])
            fw.op("dve", TT(dLr, Rp, Rv, ALU.subtract), reads=[r_Rp, r_rN], writes=[r_dLr])
            fw.op("act", ACT(dLr, dLr, AF.Exp), reads=[r_dLr], writes=[r_dLr])
            fw.op("dve", TT(Xd[0:4, :, 0:nch], dLr.unsqueeze(1).to_broadcast([4, 4, nch]),
                            identf[0:4, 0:4].unsqueeze(2).to_broadcast([4, 4, nch]), ALU.mult),
                  reads=[r_dLr, r_identf], writes=[r_Xd])
            for hh in range(4):
                fw.op("pe", MM(banks[4][:, 128 + hh * 8:128 + hh * 8 + nch], onesf[0:4, :], Xd[0:4, hh, 0:nch]),
                      reads=[r_ones, r_Xd], writes=[r_bk[4]])
            ntile = n // L
            for ti in range(ntile):
                fw.op("pe", MM(banks[4][0:L, ti * 4:ti * 4 + 4], rwL[:, ti * L:(ti + 1) * L], identf[0:4, 0:4]),
                      reads=[r_rsp, r_identf], writes=[r_bk[4]])
                fw.op("pe", MM(banks[4][0:L, 64 + ti * 4:64 + ti * 4 + 4], rfl[:, ti * L:(ti + 1) * L], identf[0:4, 0:4]),
                      reads=[r_rGn, r_identf], writes=[r_bk[4]])
            fw.op("dve", CP(dLbc[:].rearrange("p a b -> p (a b)"), banks[4][:, 128:160]), reads=[r_bk[4]], writes=[r_dLbc])
            fw.op("dve", CP(wLT[0:L, 0:ntile * 4], banks[4][0:L, 0:ntile * 4]), reads=[r_bk[4]], writes=[r_tms])
            fw.op("dve", TSC(wLT16[0:L, 0:ntile * 4], banks[4][0:L, 0:ntile * 4], 0.0625, None, ALU.mult), reads=[r_bk[4]], writes=[r_tms])
            fw.op("dve", CP(fl2T[0:L, 0:ntile * 4], banks[4][0:L, 64:64 + ntile * 4]), reads=[r_bk[4]], writes=[r_tms])

        def pgeom(mb):
            n = mb["ntok"]
            N = min(NT, n)
            if mb["prompt"]:
                return N, 1, N
            return N, 4, 32

        def pool_inproj(mb, g, tg, ntg):
            par = tg % 2
            s = g % 2
            W = pslot[s]
            N, nseg, T = pgeom(mb)
            E = [xpT[par][:, f, 0:nseg * (15 + T)].rearrange("p (s t) -> p s t", s=nseg) for f in range(2)]
            Eo = [xpT[1 - par][:, f, 0:nseg * (15 + T)].rearrange("p (s t) -> p s t", s=nseg) for f in range(2)]
            cs = slice(tg * N, (tg + 1) * N)
            hr = htiles(tg * N, N)
            for f in range(2):
                if mb["prompt"]:
                    if tg == 0:
                        if mb["first"]:
                            fw.op("dve", MSET(E[f][:, :, 0:15], 0.0), writes=[r_xpT[par]])
                        else:
                            fw.op("dve", CP(E[f][:, 0, 0:15], halo[:, 2 * g + f, :]), reads=[r_halo], writes=[r_xpT[par]])
                    else:
                        fw.op("dve", CP(E[f][:, 0, 0:15], Eo[f][:, 0, T:T + 15]), reads=[r_xpT[1 - par]], writes=[r_xpT[par]])
                else:
                    fw.op("dve", CP(E[f][:, :, 0:15], halos[:, 2 * g + f, :, :]), reads=[r_halos], writes=[r_xpT[par]])
            for f in range(2):
                bk, rb = nb()
                for kc in range(8):
                    fw.op("pe", MM(bk[:, 0:N], W[:, kc, f * 128:(f + 1) * 128], hT[:, kc, cs], start=(kc == 0), stop=(kc == 7)),
                          reads=[r_ps[s][0]] + hr, writes=[rb])
                fw.op("dve", CP(E[f][:, :, 15:15 + T], bk[:, 0:N].rearrange("p (s t) -> p s t", s=nseg)), reads=[rb], writes=[r_xpT[par]])
                yield
            for f in range(2):
                bk, rb = nb()
                for kc in range(8):
                    fw.op("pe", MM(bk[:, 0:N], W[:, kc, 256 + f * 128:256 + (f + 1) * 128], hT[:, kc, cs], start=(kc == 0), stop=(kc == 7)),
                          reads=[r_ps[s][1]] + hr, writes=[rb])
                fw.op("act", ACT(szp[par][:, f, 0:N], bk[:, 0:N], AF.Silu), reads=[rb], writes=[r_szp[par]])
                yield

        pcount = [0]

        def pool_dep(mb, g, tg, ntg):
            par = tg % 2
            s = g % 2
            W = pslot[s]
            N, nseg, T = pgeom(mb)
            w = 2 ** (g + 1)
            E = [xpT[par][:, f, 0:nseg * (15 + T)].rearrange("p (s t) -> p s t", s=nseg) for f in range(2)]
            tAv = tA[:, 0:nseg * (15 + T)].rearrange("p (s t) -> p s t", s=nseg)
            tBv = tB[:, 0:nseg * (15 + T)].rearrange("p (s t) -> p s t", s=nseg)
            cs = slice(tg * N, (tg + 1) * N)
            Ltot = 15 + T
            flush(pend_hn)
            tCv = yo[0][:, 0:nseg * (15 + T)].rearrange("p (s t) -> p s t", s=nseg)
            tDv = yo[1][:, 0:nseg * (15 + T)].rearrange("p (s t) -> p s t", s=nseg)
            for f in range(2):
                cur, rcur = E[f], r_xpT[par]
                tmps = [(tAv, r_tA), (tBv, r_tB)] if f == 0 else [(tCv, r_yo[0]), (tDv, r_yo[1])]
                weng = "pool" if f == 0 else "dve"
                step = 1
                k = 0
                while step < w:
                    lo = 2 * step - 1
                    nxt, rn = tmps[k % 2]
                    fw.op(weng, TT(nxt[:, :, lo:Ltot], cur[:, :, lo:Ltot], cur[:, :, lo - step:Ltot - step], ALU.add),
                          reads=[rcur], writes=[rn])
                    cur, rcur = nxt, rn
                    step *= 2
                    k += 1
                oth, roth = tmps[k % 2]
                pv = pooledT[par][:, f, 0:N].rearrange("p (s t) -> p s t", s=nseg)
                fw.op("dve", STT(pv, cur[:, :, 15:Ltot], 1.0 / w, E[f][:, :, 15:Ltot], ALU.mult, ALU.subtract),
                      reads=[rcur, r_xpT[par]], writes=[r_pooled[par]])
                if mb["prompt"] and mb["first"] and tg == 0:
                    fw.op("dve", TT(oth[:, 0, 0:w - 1], cur[:, 0, 15:15 + w - 1], invc[:, 0:w - 1], ALU.mult),
                          reads=[rcur, r_invc], writes=[roth])
                    fw.op("dve", TT(pooledT[par][:, f, 0:w - 1], oth[:, 0, 0:w - 1], E[f][:, 0, 15:15 + w - 1], ALU.subtract),
                          reads=[roth, r_xpT[par]], writes=[r_pooled[par]])
                if mb["prompt"] and not mb["last"] and tg == ntg - 1:
                    fw.op("dve", CP(halo[:, 2 * g + f, :], E[f][:, 0, T:T + 15]), reads=[r_xpT[par]], writes=[r_halo])
                yield
            for dcl in range(2):
                if dcl == 0:
                    flush(pend_tr)
                bk, rb = nb()
                for ccl in range(2):
                    fw.op("pe", MM(bk[:, 0:N], wpool[:, g, ccl, dcl * 128:(dcl + 1) * 128], pooledT[par][:, ccl, 0:N],
                                   start=(ccl == 0), stop=(ccl == 1)), reads=[r_wpool, r_pooled[par]], writes=[rb])
                fw.op("dve", STT(ycatT[:, 2 * g + dcl, cs], bk[:, 0:N], vecT[:, 2 * g + dcl, 1:2], szp[par][:, dcl, 0:N], ALU.mult, ALU.mult),
                      reads=[rb, r_vecT, r_szp[par]], writes=[r_yc[i] for i in range(tg * N // 128, ((tg + 1) * N + 127) // 128)])
                yield
            if mb["last"] and tg == ntg - 1:
                for si, (seq, c0, Ts) in enumerate(mb["segs"]):
                    bk, rb = nb()
                    lc = slice(c0 + Ts - 15, c0 + Ts)
                    pp = pcount[0] % 2
                    pcount[0] += 1
                    for kc in range(8):
                        fw.op("pe", MM(bk[0:15, 0:256], hT[:, kc, lc], W[:, kc, 0:256], start=(kc == 0), stop=(kc == 7)),
                              reads=[r_ps[s][0]] + htiles(c0 + Ts - 15, 15), writes=[rb])
                    fw.op("act", ACT(pstage[pp][0:15, :], bk[0:15, 0:256], AF.Identity), reads=[rb], writes=[r_pstage[pp]])
                    outs.append(fw.op("sp", DMA(pool_o[seq, :, g * 256:(g + 1) * 256], pstage[pp][0:15, :]),
                                      reads=[r_pstage[pp]], dma_key="o_pool%d" % pp))
                    yield

        def head_inproj(mb, h, tg, ntg):
            par = tg % 2
            s = h % 2
            W = hslot[s]
            n = mb["ntok"]
            N = min(NT, n)
            cs = slice(tg * N, (tg + 1) * N)
            hr = htiles(tg * N, N)
            if tg == 0:
                if mb["prompt"]:
                    if mb["first"]:
                        fw.op("dve", MSET(Call[:, h, :, :], 0.0), writes=[r_C[h]])
                else:
                    for hl in ([0, 1] if h == 0 else ([h + 1] if h + 1 < 4 else [])):
                        for j in range(4):
                            rc = Cres(mb, hl, j)
                            extra = list(r_yc) if hl == 1 else []
                            for dc in range(2):
                                fw.op("sp", DMA(Cst(mb, hl, j, dc)[:, 0:256], sC_d[j, hl, dc * 128:(dc + 1) * 128, :]),
                                      writes=[rc] + extra, dma_key="ldC%d_%d" % (hl % 2, j))
                                fw.op("sp", DMA(Cst(mb, hl, j, dc)[:, 256:257], sn_d[j, hl, dc * 128:(dc + 1) * 128].rearrange("(p o) -> p o", o=1),
                                                allow_slow_non_contiguous=True),
                                      writes=[rc] + extra, dma_key="ldC%d_%d" % (hl % 2, j))
            specs = [(0, None, qT, r_qT), (3, "k", kT, r_kT), (1, "sig", so, r_so), (2, "silu", gm, r_gm)]
            for (blk, kind, dstT, rdst) in specs:
                for f in range(2):
                    bk, rb = nb()
                    for kc in range(8):
                        fw.op("pe", MM(bk[:, 0:N], W[:, kc, blk * 256 + f * 128:blk * 256 + (f + 1) * 128], hT[:, kc, cs],
                                       start=(kc == 0), stop=(kc == 7)), reads=[r_hs[s][blk]] + hr, writes=[rb])
                    dst = dstT[par][:, f, 0:N]
                    if kind is None:
                        fw.op("dve", CP(dst, bk[:, 0:N]), reads=[rb], writes=[rdst[par]])
                    elif kind == "k":
                        fw.op("act", ACT(dst, bk[:, 0:N], AF.Identity, scale=0.0625), reads=[rb], writes=[rdst[par]])
                    elif kind == "sig":
                        fw.op("act", ACT(dst, bk[:, 0:N], AF.Sigmoid), reads=[rb], writes=[rdst[par]])
                    else:
                        fw.op("act", ACT(dst, bk[:, 0:N], AF.Silu), reads=[rb], writes=[rdst[par]])
                    yield
            fw.op("pool", TT(gm[par][:, :, 0:N], gm[par][:, :, 0:N], so[par][:, :, 0:N], ALU.mult), reads=[r_gm[par], r_so[par]], writes=[r_gm[par]])

        ccount = [0]
        r_C2 = [R("C2_%d" % i) for i in range(4)]

        def Cst(mb, h, cidx, dc):
            if mb["prompt"] or h % 2 == 0:
                return Call[:, cidx, dc, :]
            return ycatT[:, cidx * 2 + dc, 128:128 + 514].bitcast(F32)

        def Cres(mb, h, cidx):
            if mb["prompt"] or h % 2 == 0:
                return r_C[cidx]
            return r_C2[cidx]

        pend_tr = []
        pend_hn = []

        def flush(lst):
            while lst:
                lst.pop(0)()


        def head_dep(mb, h, tg, ntg):
            par = tg % 2
            s = h % 2
            W = hslot[s]
            n = mb["ntok"]
            N = min(NT, n)
            L = mb["L"]
            npt = N // L
            ntile = n // L
            cbase = ccount[0]
            ccount[0] += npt
            M = L

            def kv_ops(j):
                ti = tg * npt + j
                c0 = ti * L
                cc = (cbase + j) % 2
                cols = slice(c0, c0 + M)
                sc16 = wLT16[0:M, ti * 4 + h:ti * 4 + h + 1]
                bk, rb = nb()
                psb = bk[:].bitcast(BF16)
                lcj = slice(j * L, (j + 1) * L)
                sc1 = wLT[0:M, ti * 4 + h:ti * 4 + h + 1]
                for kc in range(8):
                    fw.op("pe", MM(bk[0:M, 0:256], hT[:, kc, cols], W[:, kc, 1024:1280], start=(kc == 0), stop=(kc == 7)),
                          reads=[r_hs[s][4]] + htiles(c0, M), writes=[rb])
                for dc in range(2):
                    fw.op("pe", TR(psb[0:M, 512 + dc * 128:512 + (dc + 1) * 128], kT[par][:, dc, lcj], identb[:]),
                          reads=[r_kT[par], r_identb], writes=[rb])
                fw.op("dve", TSC(kw[cc][0:M, :], psb[0:M, 512:768], sc1, None, ALU.mult), reads=[rb, r_tms], writes=[r_kw[cc]])
                fw.op("act", ACT(vaug[cc][0:M, 0:256], bk[0:M, 0:256], AF.Identity), reads=[rb], writes=[r_va[cc]])

            kv_ops(0)
            for j in range(npt):
                ti = tg * npt + j
                c0 = ti * L
                cidx = h if mb["prompt"] else ti
                cc = (cbase + j) % 2
                cols = slice(c0, c0 + M)
                lc = slice(j * L, (j + 1) * L)
                sc = wLT[0:M, ti * 4 + h:ti * 4 + h + 1]
                fl2 = fl2T[0:M, ti * 4 + h:ti * 4 + h + 1]
                dl = dLbc[:, h, ti:ti + 1]
                for dc in range(2):
                    fw.op("pe", MM(banks[4][0:M, 0:M], kT[par][:, dc, lc], qT[par][:, dc, lc], start=(dc == 0), stop=(dc == 1)),
                          reads=[r_kT[par], r_qT[par]], writes=[r_bk[4]])
                fw.op("dve", STT(PT[cc][0:M, 0:M], banks[4][0:M, 0:M], sc, maskT[0:M, 0:M], ALU.mult, ALU.mult),
                      reads=[r_bk[4], r_tms, r_mask], writes=[r_PT[cc]])
                flush(pend_hn)
                rc = Cres(mb, h, cidx)
                for dc in range(2):
                    fw.op("act", ACT(Cbf[:, dc, :], Cst(mb, h, cidx, dc), AF.Identity, scale=dl), reads=[rc, r_dLbc], writes=[r_Cbf])
                if j + 1 < npt:
                    kv_ops(j + 1)
                fw.op("pe", MM(banks[5][0:M, 0:257], PT[cc][0:M, 0:M], vaug[cc][0:M, :], start=True, stop=False),
                      reads=[r_PT[cc], r_va[cc]], writes=[r_bk[5]])
                for dc in range(2):
                    fw.op("pe", MM(banks[5][0:M, 0:257], qT[par][:, dc, lc], Cbf[:, dc, :], start=False, stop=(dc == 1)),
                          reads=[r_qT[par], r_Cbf], writes=[r_bk[5]])
                for dc in range(2):
                    fw.op("pe", MM(banks[6 + dc][:, 0:257], kw[cc][0:M, dc * 128:(dc + 1) * 128], vaug[cc][0:M, :]),
                          reads=[r_kw[cc], r_va[cc]], writes=[r_bk[6 + dc]])
                flush(pend_tr)
                for dc in range(2):
                    fw.op("dve", STT(Cst(mb, h, cidx, dc), Cst(mb, h, cidx, dc), dl, banks[6 + dc][:, 0:257], ALU.mult, ALU.add),
                          reads=[rc, r_dLbc, r_bk[6 + dc]], writes=[rc])
                t = stt[cc]
                rt = r_stt[cc]
                fw.op("dve", lambda e, t=t, M=M: e.bn_stats(out=t[0:M, 0:6], in_=banks[5][0:M, 0:256]), reads=[r_bk[5]], writes=[rt])
                fw.op("dve", lambda e, t=t, M=M: e.bn_aggr(out=t[0:M, 6:8], in_=t[0:M, 0:6]), reads=[rt], writes=[rt])
                fw.op("dve", CP(t[0:M, 8:9], banks[5][0:M, 256:257]), reads=[r_bk[5]], writes=[rt])
                fw.op("dve", TT(t[0:M, 9:10], t[0:M, 8:9], t[0:M, 8:9], ALU.mult), reads=[rt], writes=[rt])
                fw.op("dve", TT(t[0:M, 10:11], t[0:M, 9:10], fl2, ALU.max), reads=[rt, r_tms], writes=[rt])
                fw.op("dve", STT(t[0:M, 11:12], t[0:M, 10:11], EPS, t[0:M, 7:8], ALU.mult, ALU.add), reads=[rt], writes=[rt])
                fw.op("pool", TT(t[0:M, 12:13], t[0:M, 11:12], mhalf[0:M, 0:1], ALU.pow), reads=[rt, r_mhalf], writes=[rt])
                def emit_hn(cc=cc, M=M, t=t, rt=rt):
                    fw.op("dve", TSC(hn[cc][0:M, :], banks[5][0:M, 0:256], t[0:M, 6:7], t[0:M, 12:13], ALU.subtract, ALU.mult),
                          reads=[r_bk[5], rt], writes=[r_hn[cc]])
                pend_hn.append(emit_hn)
                def emit_tr(cc=cc, M=M, cols=cols, lc=lc, c0=c0, h=h, par=par):
                    bk, rb = nb()
                    psb = bk[:].bitcast(BF16)
                    for dcl in range(2):
                        fw.op("pe", TR(psb[:, dcl * 128:dcl * 128 + M], hn[cc][0:M, dcl * 128:(dcl + 1) * 128], identb[0:M, 0:M]),
                              reads=[r_hn[cc], r_identb], writes=[rb])
                    for dcl in range(2):
                        fw.op("dve", STT(ycatT[:, 8 + 2 * h + dcl, cols], psb[:, dcl * 128:dcl * 128 + M], vecT[:, 2 * h + dcl, 2:3],
                                         gm[par][:, dcl, lc], ALU.mult, ALU.mult),
                              reads=[rb, r_vecT, r_gm[par]], writes=[r_yc[i] for i in range(c0 // 128, (c0 + M + 127) // 128)])
                pend_tr.append(emit_tr)
                if mb["last"] and (not mb["prompt"] or ti == ntile - 1):
                    seq = mb["segs"][0][0] if mb["prompt"] else mb["segs"][ti][0]
                    for dc in range(2):
                        outs.append(fw.op("sp", DMA(C_o[seq, h, dc * 128:(dc + 1) * 128, :], Cst(mb, h, cidx, dc)[:, 0:256]),
                                          reads=[rc], dma_key="o_C%d_%d" % (h % 2 if not mb["prompt"] else 0, cidx)))
                        outs.append(fw.op("sp", DMA(n_o[seq, h, dc * 128:(dc + 1) * 128].rearrange("(p o) -> p o", o=1), Cst(mb, h, cidx, dc)[:, 256:257],
                                                    allow_slow_non_contiguous=True),
                                          reads=[rc], dma_key="o_C%d_%d" % (h % 2 if not mb["prompt"] else 0, cidx)))
                yield

        def gatebc_phase(mb, halves=(0, 1)):
            k = 0
            for half in halves:
                bk, rb = nb()
                for fcl in range(4):
                    fc = half * 4 + fcl
                    for si, (seq, c0, T) in enumerate(mb["segs"]):
                        d = k % 2
                        k += 1
                        fw.op("dve", TSC(dg[d][:], identf[:], modT[:, 16 + fc, seq:seq + 1], None, ALU.mult),
                              reads=[r_identf, r_modT], writes=[r_dg[d]])
                        lhs = onesf[:, :] if mb["prompt"] else smask[:, si, :]
                        fw.op("pe", MM(bk[:, fcl * 128:(fcl + 1) * 128], lhs, dg[d][:], start=(si == 0), stop=(si == len(mb["segs"]) - 1)),
                              reads=[r_ones, r_smask, r_dg[d]], writes=[rb])
                fw.op("dve", CP(gate_bc[:, half * 512:(half + 1) * 512], bk[:, 0:512]), reads=[rb], writes=[r_gbc])

        def final_phase(mb):
            n = mb["ntok"]
            nt = n // 128
            wo1 = hslot[0][:].rearrange("p a b -> p (a b)")[:, 0:16 * 512].rearrange("p (e c) -> p e c", c=512)
            prev_tail = [None]
            fw.op("sp", DMA(xs[0][:, 0:D], x_d[mb["tok0"]:mb["tok0"] + 128, :]), writes=[r_xs[0]], dma_key="xs0")
            for i in range(nt):
                s = i % 2
                if i + 1 < nt:
                    s2 = (i + 1) % 2
                    fw.op("sp", DMA(xs[s2][:, 0:D], x_d[mb["tok0"] + (i + 1) * 128:mb["tok0"] + (i + 2) * 128, :]),
                          writes=[r_xs[s2]], dma_key="xs%d" % s2)
                if prev_tail[0] is not None:
                    prev_tail[0]()
                    prev_tail[0] = None
                for nh in range(2):
                    bk, rb = nb()
                    for ec in range(16):
                        if nh == 0:
                            rhs = pslot[ec // 8][:, ec % 8, :]
                            rr = r_ps[ec // 8]
                        else:
                            rhs = wo1[:, ec, :]
                            rr = r_hs[0]
                        fw.op("pe", MM(bk[:, 0:512], ycatT[:, ec, i * 128:(i + 1) * 128], rhs, start=(ec == 0), stop=(ec == 15)),
                              reads=[r_yc[i]] + rr, writes=[rb])
                    fw.op("dve", TT(yv[s][:, nh * 512:(nh + 1) * 512], bk[:, 0:512], gate_bc[:, nh * 512:(nh + 1) * 512], ALU.mult),
                          reads=[rb, r_gbc], writes=[r_yv[s]])
                fw.op("dve", TT(yv[s][:], yv[s][:], xs[s][:, 0:D], ALU.add), reads=[r_yv[s], r_xs[s]], writes=[r_yv[s]])
                fw.op("dve", lambda e, s=s: e.scalar_tensor_tensor(out=yo[s][:], in0=yv[s][:], scalar=1.0, in1=yv[s][:], op0=ALU.mult, op1=ALU.mult,
                                                                   accum_out=ss[s][:, 0:1]),
                      reads=[r_yv[s]], writes=[r_yo[s], r_ss[s]])

                def tail(s=s, i=i):
                    rstd_ops(ss[s], r_ss[s])
                    fw.op("dve", STT(yo[s][:], yv[s][:], ss[s][:, 2:3], gfin[:], ALU.mult, ALU.mult), reads=[r_yv[s], r_ss[s], r_gfin], writes=[r_yo[s]])
                    outs.append(fw.op("sp", DMA(y_o[mb["tok0"] + i * 128:mb["tok0"] + (i + 1) * 128, :], yo[s][:]), reads=[r_yo[s]], dma_key="o_y%d" % s))
                prev_tail[0] = tail
                yield
            if prev_tail[0] is not None:
                prev_tail[0]()
                prev_tail[0] = None

        def run_all(g):
            if g is not None:
                for _ in g:
                    pass

        def interleave(a, b, ra=1, rb_=1):
            gens = [a, b]
            rates = [ra, rb_]
            alive = [a is not None, b is not None]
            while any(alive):
                for gi in range(2):
                    if not alive[gi]:
                        continue
                    for _ in range(rates[gi]):
                        try:
                            next(gens[gi])
                        except StopIteration:
                            alive[gi] = False
                            break

        INP = {"P": pool_inproj, "H": head_inproj}
        DEP = {"P": pool_dep, "H": head_dep}
        prev_final = None
        for mi, mb in enumerate(mbs):
            ntg = max(1, mb["ntok"] // NT)
            load_head_w(1)
            interleave(prev_final, norm_phase(mb, solo=(prev_final is None)))
            prev_final = None
            load_pool_w(0)
            load_pool_w(1)
            load_head_w(0)
            if stop < 1:
                continue
            gate_phase(mb)
            if stop < 2:
                continue
            steps = []
            for i in range(4):
                for tg in range(ntg):
                    steps.append(("P", i, tg))
                for tg in range(ntg):
                    steps.append(("H", i, tg))
            steps = [st_ for st_ in steps if stop >= 3 + (st_[1] * 2 + (1 if st_[0] == "H" else 0))]
            if steps:
                k0, i0, t0_ = steps[0]
                run_all(INP[k0](mb, i0, t0_, ntg))
            for si_, (kind, i, tg) in enumerate(steps):
                nxt = steps[si_ + 1] if si_ + 1 < len(steps) else None
                gen_dep = DEP[kind](mb, i, tg, ntg)
                gen_in = INP[nxt[0]](mb, nxt[1], nxt[2], ntg) if nxt is not None else None
                interleave(gen_dep, gen_in, 1, 2)
                if kind == "H" and i in (0, 1) and tg == ntg - 1:
                    gatebc_phase(mb, halves=(i,))
                if kind == "H" and i == 2 and tg == ntg - 1 and mi == 0:
                    halos_phase()
                if tg == ntg - 1:
                    if kind == "P":
                        if i + 2 < 4:
                            load_pool_w(i + 2)
                        elif i == 3 and stop >= 11:
                            load_wout_a()
                    else:
                        if i + 2 < 4:
                            load_head_w(i + 2)
                        elif i == 2 and stop >= 11:
                            load_wout_b()
            flush(pend_hn)
            flush(pend_tr)
            if stop < 11:
                continue
            prev_final = final_phase(mb)
        run_all(prev_final)

        if dbg:
            def dump(name, ap, shape, res):
                d = dout(name, shape)
                dbg_o[name] = shape
                outs.append(fw.op("pool", DMA(d, ap), reads=res, dma_key="dbg_" + name))
            dump("d_hT", hT[:], [128, 8, TMB], r_hT)
            dump("d_ycatT", ycatT[:], [128, 16, TMB], r_yc)
            dump("d_modT", modT[:], [128, 24, 6], [r_modT])
            dump("d_wLT", wLT[:], [128, 32], [r_tms])
            dump("d_fl2T", fl2T[:], [128, 32], [r_tms])
            dump("d_dLbc", dLbc[:], [128, 4, 8], [r_dLbc])
            dump("d_gbc", gate_bc[:], [128, D], [r_gbc])
            dump("d_pooledT", pooledT[0][:], [128, 2, NT], [r_pooled[0]])
            dump("d_gm", gm[0][:], [128, 2, NT], [r_gm[0]])

        fw.emit(final_wait_ops=outs)
    return nc, dbg_o


_CACHE = {}


def _consts():
    ident = np.eye(128, dtype=np.float32)
    s = np.arange(128)
    maskT = (s[:, None] <= s[None, :]).astype(np.float32)
    invc = np.tile((1.0 / np.arange(1, 17, dtype=np.float32))[None, :], (128, 1)).astype(np.float32)
    return ident, maskT, invc


def make_in_maps(x_prompt, x_sample, c_prompt, c_sample, state_pool, state_C, state_n, state_m,
                 w_ada, b_ada, g_norm, w_in, b_i, b_f, w_pool, pool_scale, g_head, w_out, g_final):
    f = lambda a: np.ascontiguousarray(np.asarray(a, dtype=np.float32))
    ident, maskT, invc = _consts()
    vecs = f(np.stack([np.asarray(g_norm)[0], np.asarray(pool_scale)[0], np.asarray(g_head)[0],
                       np.asarray(b_ada)[0, 0:D], np.asarray(b_ada)[0, D:2 * D], np.asarray(b_ada)[0, 2 * D:3 * D]], axis=0))
    shared = {
        "w_ada": f(np.asarray(w_ada)[0]), "vecs": vecs, "w_in": f(np.asarray(w_in)[0]),
        "b_i": f(np.asarray(b_i)[0].reshape(4, 1)), "b_f": f(np.asarray(b_f)[0].reshape(4, 1)),
        "w_pool": f(np.asarray(w_pool)[0]), "w_out": f(np.asarray(w_out)[0]), "g_final": f(np.asarray(g_final).reshape(1, D)),
        "ident": ident, "maskT": maskT, "invcnt": invc,
    }
    xp = np.asarray(x_prompt); xsm = np.asarray(x_sample)
    in_maps = []
    for c in range(NCORES):
        m = dict(shared)
        m["x"] = f(np.concatenate([xp[2 * c].reshape(TP, D), xp[2 * c + 1].reshape(TP, D), xsm[4 * c:4 * c + 4].reshape(4 * TS, D)], axis=0))
        m["c"] = f(np.concatenate([np.asarray(c_prompt)[2 * c:2 * c + 2], np.asarray(c_sample)[4 * c:4 * c + 4]], axis=0))
        m["st_pool"] = f(np.asarray(state_pool)[0, 4 * c:4 * c + 4])
        m["st_C"] = f(np.asarray(state_C)[0, 4 * c:4 * c + 4])
        m["st_n"] = f(np.asarray(state_n)[0, 4 * c:4 * c + 4])
        m["st_mT"] = f(np.asarray(state_m)[0, 4 * c:4 * c + 4].T)
        in_maps.append(m)
    return in_maps


def kernel(**inputs):
    if "nc" not in _CACHE:
        _CACHE["nc"] = build()[0]
    nc = _CACHE["nc"]
    in_maps = make_in_maps(**inputs)
    res = run_bass_kernel_spmd(nc, in_maps, core_ids=list(range(NCORES)))
    R_ = res.results
    y_prompt = np.zeros((16, TP, D), np.float32)
    y_sample = np.zeros((32, TS, D), np.float32)
    pp = np.zeros((1, 16, 15, D), np.float32); pc = np.zeros((1, 16, 4, 256, 256), np.float32)
    pn = np.zeros((1, 16, 4, 256), np.float32); pm = np.zeros((1, 16, 4), np.float32)
    sp_ = np.zeros((1, 32, 15, D), np.float32); sc = np.zeros((1, 32, 4, 256, 256), np.float32)
    sn = np.zeros((1, 32, 4, 256), np.float32); sm = np.zeros((1, 32, 4), np.float32)
    for c in range(NCORES):
        r = R_[c]
        y = r["y"]
        y_prompt[2 * c] = y[0:TP]
        y_prompt[2 * c + 1] = y[TP:2 * TP]
        y_sample[4 * c:4 * c + 4] = y[2 * TP:].reshape(4, TS, D)
        pp[0, 2 * c:2 * c + 2] = r["pool_o"][0:2]; sp_[0, 4 * c:4 * c + 4] = r["pool_o"][2:6]
        pc[0, 2 * c:2 * c + 2] = r["C_o"][0:2]; sc[0, 4 * c:4 * c + 4] = r["C_o"][2:6]
        pn[0, 2 * c:2 * c + 2] = r["n_o"][0:2]; sn[0, 4 * c:4 * c + 4] = r["n_o"][2:6]
        pm[0, 2 * c:2 * c + 2] = r["m_o"][0:2]; sm[0, 4 * c:4 * c + 4] = r["m_o"][2:6]
    return (y_prompt, y_sample, pp, pc, pn, pm, sp_, sc, sn, sm)
```
